# Optimizing a Trainium2 kernel written in Bass

```python
import jax, jax.numpy as jnp
from jax import lax
import numpy as np

D_MODEL = 1024
BATCH = 8
SEQ = 2048
DEPTH = 2

CTX_LEN = 256
GRID_W = 64
N_Q_HEADS = 8
N_KV_HEADS = 2
HEAD_DIM = 64
Q_BLOCK = 128
ROPE_THETA = 10000.0
D_CONV = 512
CONV_K = 31
D_RNN = 512
RNN_BLOCKS = 8
RNN_CONV_K = 4
LRU_C = 8.0
D_FF = 3072
FFN_CONV_K = 3
N_BRANCH = 3
EPS = 1e-6

D_Q = N_Q_HEADS * HEAD_DIM
D_KV = N_KV_HEADS * HEAD_DIM
Q_GROUP = N_Q_HEADS // N_KV_HEADS
IN_SPLITS = (D_Q, D_KV, D_KV, 2 * D_CONV, D_RNN, D_RNN, N_BRANCH * D_MODEL)
D_IN = 5888

kernel_name = 'hybrid_gqa_conformer_rglru_dit_block'


def rms_norm(x, g):
    xf = x.astype(jnp.float32)
    y = xf * lax.rsqrt(jnp.mean(xf * xf, axis=-1, keepdims=True) + EPS)
    return (y * g.astype(jnp.float32)).astype(x.dtype)


def layer_norm(x, g, b):
    xf = x.astype(jnp.float32)
    mu = jnp.mean(xf, axis=-1, keepdims=True)
    var = jnp.mean(jnp.square(xf - mu), axis=-1, keepdims=True)
    y = (xf - mu) * lax.rsqrt(var + EPS)
    return (y * g.astype(jnp.float32) + b.astype(jnp.float32)).astype(x.dtype)


def modulate(h, shift, scale):
    return h * (1.0 + scale) + shift


def dwconv(x, w, b, pad):
    C = x.shape[-1]
    y = lax.conv_general_dilated(x, w[:, None, :].astype(x.dtype), window_strides=(1,), padding=[pad],
                                 dimension_numbers=('NWC', 'WIO', 'NWC'), feature_group_count=C)
    return y + b.astype(x.dtype)


def split_in(z):
    offs = [int(o) for o in np.cumsum(IN_SPLITS)[:-1]]
    return jnp.split(z, offs, axis=-1)


def axial_rope_tables(n):
    rows = n // GRID_W
    row = jnp.repeat(jnp.arange(rows), GRID_W).astype(jnp.float32)
    col = jnp.tile(jnp.arange(GRID_W), rows).astype(jnp.float32)
    n_freq = HEAD_DIM // 4
    freq = ROPE_THETA ** (-jnp.arange(n_freq, dtype=jnp.float32) / n_freq)
    ang = jnp.concatenate([row[:, None] * freq, col[:, None] * freq], axis=-1)
    return jnp.cos(ang), jnp.sin(ang)


def apply_rope(x, cos, sin):
    x1, x2 = jnp.split(x.astype(jnp.float32), 2, axis=-1)
    c = cos[None, :, None, :]
    s = sin[None, :, None, :]
    return jnp.concatenate([x1 * c - x2 * s, x1 * s + x2 * c], axis=-1).astype(x.dtype)


def heads(z, n_heads, gain):
    B, n = z.shape[:2]
    return rms_norm(z.reshape(B, n, n_heads, HEAD_DIM), gain)


def gqa_attend(q, k, v):
    B, n = q.shape[:2]
    qg = q.reshape(B, n, N_KV_HEADS, Q_GROUP, HEAD_DIM)
    s = jnp.einsum('bqkgd,bmkd->bkgqm', qg, k).astype(jnp.float32) * (HEAD_DIM ** -0.5)
    p = jax.nn.softmax(s, axis=-1).astype(v.dtype)
    o = jnp.einsum('bkgqm,bmkd->bqkgd', p, v)
    return o.reshape(B, n, D_Q)


def blocked_attention(q, k_all, v_all):
    B, N = q.shape[:2]
    nb = N // Q_BLOCK
    qb = q.reshape(B, nb, Q_BLOCK, N_Q_HEADS, HEAD_DIM).swapaxes(0, 1)
    o = lax.map(lambda qi: gqa_attend(qi, k_all, v_all), qb)
    return o.swapaxes(0, 1).reshape(B, N, D_Q)


def conformer_conv(z, p):
    val, gate = jnp.split(z, 2, axis=-1)
    u = val * jax.nn.sigmoid(gate)
    u = dwconv(u, p['conv_dw'], p['conv_dw_b'], (CONV_K // 2, CONV_K // 2))
    u = layer_norm(u, p['conv_ln_g'], p['conv_ln_b'])
    return jax.nn.silu(u) @ p['w_conv_out']


def _lin_combine(left, right):
    a_l, b_l = left
    a_r, b_r = right
    return a_l * a_r, a_r * b_l + b_r


def rglru_direction(xs, h0, conv_w, conv_b, wa, ba, wx, bx, lam):
    B, n = xs.shape[:2]
    xc = dwconv(xs, conv_w, conv_b, (RNN_CONV_K - 1, 0))
    xb = xc.reshape(B, n, RNN_BLOCKS, D_RNN // RNN_BLOCKS)
    r = jax.nn.sigmoid((jnp.einsum('bnhi,hij->bnhj', xb, wa).reshape(B, n, D_RNN) + ba).astype(jnp.float32))
    i = jax.nn.sigmoid((jnp.einsum('bnhi,hij->bnhj', xb, wx).reshape(B, n, D_RNN) + bx).astype(jnp.float32))
    log_a = LRU_C * r * jax.nn.log_sigmoid(lam.astype(jnp.float32))
    a = jnp.exp(log_a)
    b = jnp.sqrt(-jnp.expm1(2.0 * log_a)) * (i * xc.astype(jnp.float32))
    b = b.at[:, 0].add(a[:, 0] * h0)
    _, h = lax.associative_scan(_lin_combine, (a, b), axis=1)
    return h


def rglru_branch(x_lat, x_ctx, gate_lat, gate_ctx, p, need_ctx):
    B = x_lat.shape[0]
    h0 = jnp.zeros((B, D_RNN), jnp.float32)
    def direction(xs, init, d):
        return rglru_direction(xs, init, p['rnn_conv_w'][d], p['rnn_conv_b'][d], p['rnn_wa'][d],
                               p['rnn_ba'][d], p['rnn_wx'][d], p['rnn_bx'][d], p['rnn_lambda'][d])
    hc_f = direction(x_ctx, h0, 0)
    hl_f = direction(x_lat, hc_f[:, -1], 0)
    hc_b = direction(x_ctx[:, ::-1], h0, 1)
    hl_b = direction(x_lat[:, ::-1], hc_b[:, -1], 1)[:, ::-1]
    y_lat = ((hl_f + hl_b).astype(x_lat.dtype) * jax.nn.gelu(gate_lat)) @ p['w_rnn_out']
    if not need_ctx:
        return y_lat, None
    y_ctx = ((hc_f + hc_b[:, ::-1]).astype(x_ctx.dtype) * jax.nn.gelu(gate_ctx)) @ p['w_rnn_out']
    return y_lat, y_ctx


def merge(zg, attn_o, conv_o, rnn_o, w_out):
    g = jax.nn.sigmoid(zg.astype(jnp.float32)).astype(zg.dtype)
    g = g.reshape(zg.shape[:-1] + (N_BRANCH, D_MODEL))
    m = g[..., 0, :] * attn_o + g[..., 1, :] * conv_o + g[..., 2, :] * rnn_o
    return m @ w_out


def token_mixer(h_lat, h_ctx, p, cos, sin, need_ctx):
    B, N, _ = h_lat.shape
    M = h_ctx.shape[1]
    zq, zk, zv, zc, zx, zr, zg = split_in(h_lat @ p['w_in'])
    cq, ck, cv, cc, cx, cr, cg = split_in(h_ctx @ p['w_in'])
    q = apply_rope(heads(zq, N_Q_HEADS, p['q_norm']), cos, sin)
    k = apply_rope(heads(zk, N_KV_HEADS, p['k_norm']), cos, sin)
    v = zv.reshape(B, N, N_KV_HEADS, HEAD_DIM)
    kc = heads(ck, N_KV_HEADS, p['k_norm'])
    vc = cv.reshape(B, M, N_KV_HEADS, HEAD_DIM)
    k_all = jnp.concatenate([kc, k], axis=1)
    v_all = jnp.concatenate([vc, v], axis=1)
    attn_lat = blocked_attention(q, k_all, v_all) @ p['w_attn_out']
    conv_lat = conformer_conv(zc, p)
    rnn_lat, rnn_ctx = rglru_branch(zx, cx, zr, cr, p, need_ctx)
    o_lat = merge(zg, attn_lat, conv_lat, rnn_lat, p['w_out'])
    if not need_ctx:
        return o_lat, None
    qc = heads(cq, N_Q_HEADS, p['q_norm'])
    attn_ctx = gqa_attend(qc, kc, vc) @ p['w_attn_out']
    conv_ctx = conformer_conv(cc, p)
    o_ctx = merge(cg, attn_ctx, conv_ctx, rnn_ctx, p['w_out'])
    return o_lat, o_ctx


def conv_ffn(h, up, dw, dw_b, down):
    u = h @ up
    u = dwconv(u, dw, dw_b, (FFN_CONV_K // 2, FFN_CONV_K // 2))
    g, val = jnp.split(u, 2, axis=-1)
    return (jax.nn.gelu(g) * val) @ down


def setup_inputs(seed: int = 0) -> dict:
    key = jax.random.key(seed)
    keys = jax.random.split(key, 32)
    L = DEPTH
    f32 = jnp.float32
    def nrm(i, shape, scale):
        return jax.random.normal(keys[i], shape, f32) * scale
    def gain(i, shape):
        return 1.0 + nrm(i, shape, 0.05)
    d_blk = D_RNN // RNN_BLOCKS
    u = jax.random.uniform(keys[25], (L, 2, D_RNN), f32, 0.9, 0.999)
    s = u ** (1.0 / LRU_C)
    rnn_lambda = jnp.log(s) - jnp.log1p(-s)
    return {
        'x': nrm(0, (BATCH, SEQ, D_MODEL), 1.0),
        'c': nrm(1, (BATCH, D_MODEL), 1.0),
        'ctx': nrm(2, (BATCH, CTX_LEN, D_MODEL), 1.0),
        'c_ctx': nrm(3, (D_MODEL,), 1.0),
        'w_mod': nrm(4, (L, D_MODEL, 6 * D_MODEL), 0.5 * D_MODEL ** -0.5),
        'b_mod': nrm(5, (L, 6 * D_MODEL), 0.02),
        'norm_pre_mix': gain(6, (L, D_MODEL)),
        'norm_post_mix': gain(7, (L, D_MODEL)),
        'norm_pre_ffn': gain(8, (L, D_MODEL)),
        'norm_post_ffn': gain(9, (L, D_MODEL)),
        'w_in': nrm(10, (L, D_MODEL, D_IN), D_MODEL ** -0.5),
        'q_norm': gain(11, (L, HEAD_DIM)),
        'k_norm': gain(12, (L, HEAD_DIM)),
        'w_attn_out': nrm(13, (L, D_Q, D_MODEL), D_Q ** -0.5),
        'conv_dw': nrm(14, (L, CONV_K, D_CONV), CONV_K ** -0.5),
        'conv_dw_b': nrm(15, (L, D_CONV), 0.01),
        'conv_ln_g': gain(16, (L, D_CONV)),
        'conv_ln_b': nrm(17, (L, D_CONV), 0.01),
        'w_conv_out': nrm(18, (L, D_CONV, D_MODEL), D_CONV ** -0.5),
        'rnn_conv_w': nrm(19, (L, 2, RNN_CONV_K, D_RNN), RNN_CONV_K ** -0.5),
        'rnn_conv_b': nrm(20, (L, 2, D_RNN), 0.01),
        'rnn_wa': nrm(21, (L, 2, RNN_BLOCKS, d_blk, d_blk), d_blk ** -0.5),
        'rnn_ba': nrm(22, (L, 2, D_RNN), 0.01),
        'rnn_wx': nrm(23, (L, 2, RNN_BLOCKS, d_blk, d_blk), d_blk ** -0.5),
        'rnn_bx': nrm(24, (L, 2, D_RNN), 0.01),
        'rnn_lambda': rnn_lambda,
        'w_rnn_out': nrm(26, (L, D_RNN, D_MODEL), D_RNN ** -0.5),
        'w_out': nrm(27, (L, D_MODEL, D_MODEL), D_MODEL ** -0.5),
        'ffn_up': nrm(28, (L, D_MODEL, 2 * D_FF), D_MODEL ** -0.5),
        'ffn_dw': nrm(29, (L, FFN_CONV_K, 2 * D_FF), FFN_CONV_K ** -0.5),
        'ffn_dw_b': nrm(30, (L, 2 * D_FF), 0.01),
        'ffn_down': nrm(31, (L, D_FF, D_MODEL), D_FF ** -0.5),
    }


def reference(x, c, ctx, c_ctx, w_mod, b_mod, norm_pre_mix, norm_post_mix, norm_pre_ffn, norm_post_ffn,
              w_in, q_norm, k_norm, w_attn_out, conv_dw, conv_dw_b, conv_ln_g, conv_ln_b, w_conv_out,
              rnn_conv_w, rnn_conv_b, rnn_wa, rnn_ba, rnn_wx, rnn_bx, rnn_lambda, w_rnn_out, w_out,
              ffn_up, ffn_dw, ffn_dw_b, ffn_down):
    N = x.shape[1]
    cos, sin = axial_rope_tables(N)
    for l in range(DEPTH):
        need_ctx = l < DEPTH - 1
        sh1, sc1, g1, sh2, sc2, g2 = jnp.split((jax.nn.silu(c) @ w_mod[l] + b_mod[l])[:, None, :], 6, axis=-1)
        csh1, csc1, cg1, csh2, csc2, cg2 = jnp.split(jax.nn.silu(c_ctx) @ w_mod[l] + b_mod[l], 6, axis=-1)
        p = dict(w_in=w_in[l], q_norm=q_norm[l], k_norm=k_norm[l], w_attn_out=w_attn_out[l],
                 conv_dw=conv_dw[l], conv_dw_b=conv_dw_b[l], conv_ln_g=conv_ln_g[l], conv_ln_b=conv_ln_b[l],
                 w_conv_out=w_conv_out[l], rnn_conv_w=rnn_conv_w[l], rnn_conv_b=rnn_conv_b[l],
                 rnn_wa=rnn_wa[l], rnn_ba=rnn_ba[l], rnn_wx=rnn_wx[l], rnn_bx=rnn_bx[l],
                 rnn_lambda=rnn_lambda[l], w_rnn_out=w_rnn_out[l], w_out=w_out[l])
        h_lat = modulate(rms_norm(x, norm_pre_mix[l]), sh1, sc1)
        h_ctx = modulate(rms_norm(ctx, norm_pre_mix[l]), csh1, csc1)
        o_lat, o_ctx = token_mixer(h_lat, h_ctx, p, cos, sin, need_ctx)
        x = x + g1 * rms_norm(o_lat, norm_post_mix[l])
        h = modulate(rms_norm(x, norm_pre_ffn[l]), sh2, sc2)
        x = x + g2 * rms_norm(conv_ffn(h, ffn_up[l], ffn_dw[l], ffn_dw_b[l], ffn_down[l]), norm_post_ffn[l])
        if need_ctx:
            ctx = ctx + cg1 * rms_norm(o_ctx, norm_post_mix[l])
            hc = modulate(rms_norm(ctx, norm_pre_ffn[l]), csh2, csc2)
            ctx = ctx + cg2 * rms_norm(conv_ffn(hc, ffn_up[l], ffn_dw[l], ffn_dw_b[l], ffn_down[l]), norm_post_ffn[l])
    return x
```

```python
import contextlib
import numpy as np
import concourse.bass as bass
import concourse.mybir as mybir
from concourse.bass_utils import run_bass_kernel_spmd

F32 = mybir.dt.float32
BF16 = mybir.dt.bfloat16
AF = mybir.ActivationFunctionType
ALU = mybir.AluOpType
ESZ = {F32: 4, BF16: 2}

ENGS = ("sync", "scalar", "vector", "gpsimd", "tensor")

L = 2
D = 1024
NL = 2048
NCX = 256
NT = NL + NCX
EPS = 1e-6
KC = 8


def _esz(ap):
    try:
        return ESZ[ap.dtype]
    except Exception:
        return 4


def _box(ap):
    name = ap.tensor.name
    dims = [(int(s), int(c)) for s, c in ap.ap]
    off = int(ap.offset)
    es = _esz(ap)
    space = str(ap.space).upper()
    if "SB" not in space and "PSUM" not in space:
        lo = off + sum(min(0, s * (c - 1)) for s, c in dims)
        hi = off + sum(max(0, s * (c - 1)) for s, c in dims) + 1
        return (name, 0, 1, lo * es, hi * es)
    pstep, pcnt = dims[0]
    if pstep <= 0:
        p0 = 0
        foff = off
    else:
        p0 = off // pstep
        foff = off - p0 * pstep
    fd = dims[1:]
    lo = foff + sum(min(0, s * (c - 1)) for s, c in fd)
    hi = foff + sum(max(0, s * (c - 1)) for s, c in fd) + 1
    return (name, p0, p0 + pcnt, lo * es, hi * es)


class Op:
    __slots__ = ("eng", "fn", "deps", "tok", "needs_inc", "is_dma", "slot", "val")

    def __init__(self, eng, fn):
        self.eng = eng
        self.fn = fn
        self.deps = set()
        self.tok = None
        self.needs_inc = False
        self.is_dma = False
        self.slot = None
        self.val = 0


class Prog:
    def __init__(self, nc, n_dma_slots=16):
        self.nc = nc
        self.ops = []
        self.recs = {}
        self.n_dma_slots = n_dma_slots
        self.dma_count = {}
        self.slot_last = {}

    def _track(self, op, reads, writes):
        for ap in reads:
            b = _box(ap)
            lst = self.recs.get(b[0], [])
            keep = []
            for r in lst:
                ov = r[0] < b[2] and b[1] < r[1] and r[2] < b[4] and b[3] < r[3]
                if ov and r[5] and r[4] is not op:
                    op.deps.add(r[4])
                if (not r[5]) and (not r[4].is_dma) and (not op.is_dma) and r[4].eng == op.eng \
                        and b[1] <= r[0] and r[1] <= b[2] and b[3] <= r[2] and r[3] <= b[4]:
                    continue
                keep.append(r)
            keep.append([b[1], b[2], b[3], b[4], op, False])
            self.recs[b[0]] = keep
        for ap in writes:
            b = _box(ap)
            lst = self.recs.get(b[0], [])
            keep = []
            for r in lst:
                ov = r[0] < b[2] and b[1] < r[1] and r[2] < b[4] and b[3] < r[3]
                if ov and r[4] is not op:
                    op.deps.add(r[4])
                cov = b[1] <= r[0] and r[1] <= b[2] and b[3] <= r[2] and r[3] <= b[4]
                if cov and r[4] is not op:
                    continue
                keep.append(r)
            keep.append([b[1], b[2], b[3], b[4], op, True])
            self.recs[b[0]] = keep

    def add(self, eng, fn, reads=(), writes=()):
        op = Op(eng, fn)
        self._track(op, reads, writes)
        self.ops.append(op)
        return op

    def dma(self, eng, out, in_, **kw):
        op = Op(eng, None)
        op.is_dma = True
        i = self.dma_count.get(eng, 0)
        self.dma_count[eng] = i + 1
        op.slot = "%s%d" % (eng[0], i % (self.n_dma_slots if eng == "sync" else 4))
        prev = self.slot_last.get(op.slot)
        op.val = (prev.val if prev is not None else 0) + 16
        if prev is not None:
            op.deps.add(prev)
        self.slot_last[op.slot] = op
        op.fn = lambda e: e.dma_start(out=out, in_=in_, **kw)
        self._track(op, [in_], [out])
        self.ops.append(op)
        return op

    def mm(self, out, lhsT, rhs, start=True, stop=True):
        return self.add("tensor", lambda e: e.matmul(out, lhsT, rhs, start=start, stop=stop),
                        [lhsT, rhs] + ([] if start else [out]), [out])

    def act(self, out, in_, func, bias=None, scale=None):
        kw = {}
        rd = [in_]
        if bias is not None:
            kw["bias"] = bias
            if not isinstance(bias, (int, float)):
                rd.append(bias)
        if scale is not None:
            kw["scale"] = scale
            if not isinstance(scale, (int, float)):
                rd.append(scale)
        return self.add("scalar", lambda e: e.activation(out=out, in_=in_, func=func, **kw), rd, [out])

    def tt(self, out, in0, in1, op, eng="vector"):
        return self.add(eng, lambda e: e.tensor_tensor(out=out, in0=in0, in1=in1, op=op), [in0, in1], [out])

    def ts(self, out, in0, s1, s2, op0, op1=None, eng="vector"):
        rd = [in0] + [s for s in (s1, s2) if s is not None and not isinstance(s, (int, float))]
        if op1 is None:
            return self.add(eng, lambda e: e.tensor_scalar(out=out, in0=in0, scalar1=s1, scalar2=None, op0=op0), rd, [out])
        return self.add(eng, lambda e: e.tensor_scalar(out=out, in0=in0, scalar1=s1, scalar2=s2, op0=op0, op1=op1), rd, [out])

    def stt(self, out, in0, scalar, in1, op0, op1):
        rd = [in0, in1] + ([] if isinstance(scalar, (int, float)) else [scalar])
        return self.add("vector", lambda e: e.scalar_tensor_tensor(out=out, in0=in0, scalar=scalar, in1=in1, op0=op0, op1=op1), rd, [out])

    def copy(self, out, in_, eng="vector"):
        return self.add(eng, lambda e: e.tensor_copy(out=out, in_=in_), [in_], [out])

    def memset(self, ap, val, eng="vector"):
        return self.add(eng, lambda e: e.memset(ap, val), [], [ap])

    def recip(self, out, in_):
        return self.add("vector", lambda e: e.reciprocal(out=out, in_=in_), [in_], [out])

    def scan(self, out, d0, d1, init):
        rd = [d0, d1] + ([] if isinstance(init, (int, float)) else [init])
        return self.add("vector", lambda e: e.tensor_tensor_scan(out=out, data0=d0, data1=d1, initial=init,
                                                                 op0=ALU.mult, op1=ALU.add), rd, [out])

    def emit(self, final_waits=()):
        nc = self.nc
        seq = {e: 0 for e in ENGS}
        for op in self.ops:
            for d in op.deps:
                d.needs_inc = True
        for op in final_waits:
            op.needs_inc = True
        for op in self.ops:
            if op.is_dma:
                op.tok = (op.slot, op.val)
            elif op.needs_inc:
                seq[op.eng] += 1
                op.tok = (op.eng, seq[op.eng])
        clock = {e: {} for e in ENGS}
        opclock = {}
        per_eng = {e: [] for e in ENGS}
        slots = set()
        for op in self.ops:
            ck = clock[op.eng]
            wm = {}
            for d in sorted(op.deps, key=lambda o: (o.tok[0], o.tok[1])):
                src, v = d.tok
                if src == "tensor" and op.eng == "tensor":
                    continue
                if ck.get(src, 0) >= v:
                    continue
                wm[src] = max(wm.get(src, 0), v)
                oc = opclock.get(id(d))
                if oc:
                    for k, vv in oc.items():
                        if ck.get(k, 0) < vv:
                            ck[k] = vv
                ck[src] = max(ck.get(src, 0), v)
            if op.tok is not None:
                oc = dict(ck)
                oc[op.tok[0]] = max(oc.get(op.tok[0], 0), op.tok[1])
                opclock[id(op)] = oc
                if op.is_dma:
                    slots.add(op.slot)
            per_eng[op.eng].append((op, list(wm.items())))
        fin = [op.tok for op in final_waits] + [o.tok for o in self.slot_last.values()]
        with contextlib.ExitStack() as st:
            sems = {}
            for e in ENGS:
                sems[e] = st.enter_context(nc.semaphore("s_" + e))
            for s in sorted(slots):
                sems[s] = st.enter_context(nc.semaphore("s_" + s))
            block = st.enter_context(nc.Block())

            def run(engname):
                def body(e):
                    for op, waits in per_eng[engname]:
                        for s, v in waits:
                            e.wait_ge(sems[s], v)
                        ins = op.fn(e)
                        if op.is_dma:
                            ins.then_inc(sems[op.tok[0]], 16)
                        elif op.needs_inc:
                            ins.then_inc(sems[op.eng], 1)
                    if engname == "sync":
                        for s, v in fin:
                            e.wait_ge(sems[s], v)
                return body

            block.sync(run("sync"))
            block.scalar(run("scalar"))
            block.vector(run("vector"))
            block.gpsimd(run("gpsimd"))
            block.tensor(run("tensor"))
        self.stats = {e: len(per_eng[e]) for e in ENGS}


VEC_SPEC = [
    ("b_mod", (L,), 6144), ("norm_pre_mix", (L,), 1024), ("norm_post_mix", (L,), 1024),
    ("norm_pre_ffn", (L,), 1024), ("norm_post_ffn", (L,), 1024),
    ("q_norm", (L,), 128), ("k_norm", (L,), 128),
    ("conv_dw", (L, 31), 512), ("conv_dw_b", (L,), 512), ("conv_ln_g", (L,), 512), ("conv_ln_b", (L,), 512),
    ("rnn_conv_w", (L, 2, 4), 512), ("rnn_conv_b", (L, 2), 512), ("rnn_ba", (L, 2), 512),
    ("rnn_bx", (L, 2), 512), ("rnn_lambda", (L, 2), 512),
    ("ffn_dw", (L, 3), 6144), ("ffn_dw_b", (L,), 6144),
]
VEC_OFF = {}
_o = 0
for _n, _lead, _f in VEC_SPEC:
    VEC_OFF[_n] = (_o, _lead, _f // 128)
    _o += int(np.prod(_lead)) * (_f // 128)
NV = _o


def pack_vecs(inputs):
    out = np.zeros((128, NV), np.float32)
    for n, lead, f in VEC_SPEC:
        a = np.asarray(inputs[n], np.float32)
        if n in ("q_norm", "k_norm"):
            a = np.concatenate([a, a], axis=-1)
        a = a.reshape(int(np.prod(lead)), f // 128, 128)
        base = VEC_OFF[n][0]
        out[:, base:base + a.shape[0] * a.shape[1]] = a.reshape(-1, 128).T
    return out


def build_program(dbg=None):
    nc = bass.Bass("TRN2", target_bir_lowering=False)
    dbg = dbg or {}
    stages = dbg.get("stages")
    nlayers = dbg.get("layers", L)
    st = contextlib.ExitStack()

    def din(name, shape, dt=F32):
        return nc.dram_tensor(name, list(shape), dt, kind="ExternalInput").ap()

    xT_d = din("xT", [128, KC, NT])
    cT_d = din("cT", [128, KC * 2])
    vecs_d = din("vecs", [128, NV])
    rope_d = din("rope", [2, 128, NL])
    rotm_d = din("rotm", [2, 128, 128])
    w_mod_d = din("w_mod", [L, D, 6144])
    w_in_d = din("w_in", [L, D, 5888])
    w_ao_d = din("w_attn_out", [L, 512, D])
    w_co_d = din("w_conv_out", [L, 512, D])
    w_ro_d = din("w_rnn_out", [L, 512, D])
    w_out_d = din("w_out", [L, D, D])
    ffn_up_d = din("ffn_up", [L, D, 6144])
    ffn_down_d = din("ffn_down", [L, 3072, D])
    rnn_wa_d = din("rnn_wa", [L, 2, 8, 64, 64])
    rnn_wx_d = din("rnn_wx", [L, 2, 8, 64, 64])
    yT_d = nc.dram_tensor("yT", [128, KC, NL], F32, kind="ExternalOutput").ap()
    skind = "ExternalOutput" if dbg else "Internal"
    attn_s = nc.dram_tensor("attn_s", [4, 128, NT], BF16, kind=skind).ap()
    conv_s = nc.dram_tensor("conv_s", [4, 128, NT], BF16, kind=skind).ap()
    rnn_s = nc.dram_tensor("rnn_s", [4, 128, NT], BF16, kind=skind).ap()

    def sb(name, shape, dt=F32):
        return st.enter_context(nc.sbuf_tensor(name, list(shape), dt))

    P = Prog(nc)

    xT = sb("xTs", [128, KC, NT])
    hT = sb("hTs", [128, KC, NT], BF16)
    vecs = sb("vecs_s", [128, NV])
    modv = [sb("modv%d" % l, [128, 48, 2]) for l in range(L)]
    A1 = [sb("A1_%d" % l, [128, KC, 2]) for l in range(L)]
    G1 = [sb("G1_%d" % l, [128, KC, 2]) for l in range(L)]
    A2 = [sb("A2_%d" % l, [128, KC, 2]) for l in range(L)]
    G2 = [sb("G2_%d" % l, [128, KC, 2]) for l in range(L)]
    clv = sb("clv", [128, L * 2 * 4])
    ones_bf = sb("ones_bf", [128, 128], BF16)
    ones32 = sb("ones32", [128, 128])
    bo64 = sb("bo64", [128, 128], BF16)
    rotm = sb("rotm_s", [128, 128])
    ident = sb("ident_s", [128, 128])
    ct = sb("ct_s", [128, KC * 2])
    scb = sb("scb", [128, KC * 2], BF16)
    ARENA_W = 23296
    arena = sb("arena", [128, ARENA_W])
    pbig = [st.enter_context(nc.psum_tensor("pbig%d" % i, [128, 1024], F32)) for i in range(4)]
    banks = [pbig[i // 2][:, (i % 2) * 512:(i % 2) * 512 + 512] for i in range(8)]

    class Arena:
        def __init__(self):
            self.top = 0

        def mark(self):
            return self.top

        def release(self, m):
            self.top = m

        def alloc(self, n, dt=F32):
            w = n if dt == F32 else (n + 1) // 2
            w = (w + 7) // 8 * 8
            assert self.top + w <= ARENA_W, ("arena overflow", self.top, w)
            v = arena[:, self.top:self.top + w]
            self.top += w
            if dt != F32:
                v = v.bitcast(dt)[:, 0:n]
            return v

        def rot(self, k, n, dt=F32):
            bufs = [self.alloc(n, dt) for _ in range(k)]
            state = [0]

            def nxt():
                b = bufs[state[0] % k]
                state[0] += 1
                return b
            return nxt

    AR = Arena()

    def V(name, *idx):
        base, lead, nch = VEC_OFF[name]
        *li, c = idx
        flat = 0
        for i, d in zip(li, lead):
            flat = flat * d + i
        col = base + flat * nch + c
        return vecs[:, col:col + 1]

    def wview(wd, l):
        return wd[l].rearrange("(k p) n -> p k n", p=128)

    def wload(dst, src):
        P.dma("gpsimd", dst, src)

    P.dma("sync", vecs[:], vecs_d)
    P.dma("sync", ct[:], cT_d)
    P.dma("sync", rotm[:], rotm_d[0])
    P.dma("sync", ident[:], rotm_d[1])
    P.memset(ones_bf[:], 1.0)
    P.memset(ones32[:], 1.0)
    P.memset(bo64[:], 0.0)
    P.memset(bo64[0:64, 0:64], 1.0)
    P.memset(bo64[64:128, 64:128], 1.0)
    for c in range(KC):
        P.dma("sync", xT[:, c, :], xT_d[:, c, :])
    P.act(scb[:], ct[:], AF.Silu)
    lb = VEC_OFF["rnn_lambda"][0]
    P.act(clv[:], vecs[:, lb:lb + L * 8], AF.Exp, scale=-1.0)
    P.act(clv[:], clv[:], AF.Ln, bias=1.0)
    P.ts(clv[:], clv[:], -8.0, None, ALU.mult)

    m0 = AR.mark()
    wrot = AR.rot(3, KC * 512, BF16)
    scb3 = scb[:].rearrange("p (k s) -> p k s", s=2)
    bi = [0]

    def modblock_load(l, blk):
        wb = wrot().rearrange("p (k n) -> p k n", n=512)
        wload(wb, wview(w_mod_d, l)[:, :, blk * 512:(blk + 1) * 512])
        return wb

    jobs = [(l, blk) for l in range(L) for blk in range(12)]
    pend = [modblock_load(*jobs[0]), modblock_load(*jobs[1])]
    for ji, (l, blk) in enumerate(jobs):
        wb = pend.pop(0)
        for j in range(4):
            ch = blk * 4 + j
            ps = banks[bi[0] % 8][:, 0:2]
            bi[0] += 1
            for k in range(KC):
                P.mm(ps, wb[:, k, j * 128:(j + 1) * 128], scb3[:, k, :], start=(k == 0), stop=(k == KC - 1))
            P.ts(modv[l][:, ch, :], ps, V("b_mod", l, ch), None, ALU.add)
        if ji + 2 < len(jobs):
            pend.append(modblock_load(*jobs[ji + 2]))
    for l in range(L):
        for c in range(KC):
            P.ts(A1[l][:, c, :], modv[l][:, 8 + c, :], 1.0, V("norm_pre_mix", l, c), ALU.add, ALU.mult)
            P.ts(G1[l][:, c, :], modv[l][:, 16 + c, :], V("norm_post_mix", l, c), None, ALU.mult)
            P.ts(A2[l][:, c, :], modv[l][:, 32 + c, :], 1.0, V("norm_pre_ffn", l, c), ALU.add, ALU.mult)
            P.ts(G2[l][:, c, :], modv[l][:, 40 + c, :], V("norm_post_ffn", l, c), None, ALU.mult)
    AR.release(m0)

    LAT_TILES = [(i * 512, 512, 0) for i in range(4)]
    CTX_TILE = (NL, NCX, 1)
    ALL_TILES = LAT_TILES + [CTX_TILE]

    def norm_stage(l, A, shbase, tiles):
        m = AR.mark()
        sqr = AR.rot(3, 512, BF16)
        stdb = AR.rot(2, 512)
        rstb = AR.rot(2, 512)
        tb = AR.rot(3, 512)
        for ti, (g0, n, s) in enumerate(tiles):
            ss = banks[ti % 2][:, 0:n]
            for c in range(KC):
                sq = sqr()[:, 0:n]
                P.act(sq, xT[:, c, g0:g0 + n], AF.Square)
                P.mm(ss, ones_bf[:], sq, start=(c == 0), stop=(c == KC - 1))
            std = stdb()[:, 0:n]
            rstd = rstb()[:, 0:n]
            P.act(std, ss, AF.Sqrt, bias=EPS, scale=1.0 / D)
            P.recip(rstd, std)
            for c in range(KC):
                t = tb()[:, 0:n]
                P.tt(t, xT[:, c, g0:g0 + n], rstd, ALU.mult)
                P.act(hT[:, c, g0:g0 + n], t, AF.Identity, bias=modv[l][:, shbase + c, s:s + 1], scale=A[:, c, s:s + 1])
        AR.release(m)

    def proj(ps, wsb, col0, g0, n):
        for k in range(KC):
            P.mm(ps, wsb[:, k, col0:col0 + 128], hT[:, k, g0:g0 + n], start=(k == 0), stop=(k == KC - 1))

    def rnn_stage(l, need_ctx):
        m = AR.mark()
        LAT0, CTX0, W = 4, 4 + NL + 8, 4 + NL + 8 + NCX + 4

        def col(g0, s):
            return LAT0 + g0 if s == 0 else CTX0 + (g0 - NL)
        wxr = AR.rot(2, KC * 128, BF16)
        wrr = AR.rot(2, KC * 128, BF16)
        wv = wview(w_in_d, l)

        def load_w(j):
            a_ = wxr().rearrange("p (k n) -> p k n", n=128)
            b_ = wrr().rearrange("p (k n) -> p k n", n=128)
            wload(a_, wv[:, :, 1792 + j * 128:1792 + (j + 1) * 128])
            wload(b_, wv[:, :, 2304 + j * 128:2304 + (j + 1) * 128])
            return a_, b_
        xbuf = AR.alloc(W)
        xc = AR.alloc(W)
        rbs = [AR.alloc(W), AR.alloc(W)]
        ibs = [AR.alloc(W), AR.alloc(W)]
        hf = AR.alloc(W)
        gz = AR.alloc(W)
        xcb = AR.alloc(W, BF16)
        yb = AR.alloc(W, BF16)
        wbd = [AR.alloc(128, BF16) for _ in range(4)]
        for bufz in [xbuf, xc, hf, gz] + rbs + ibs:
            P.memset(bufz, 0.0)
        P.memset(xcb, 0.0)
        P.memset(yb, 0.0)
        rtiles = ALL_TILES if need_ctx else LAT_TILES
        o0, o1 = LAT0, CTX0 + NCX
        pendw = [load_w(0), load_w(1)]
        for j in range(4):
            wx, wr = pendw.pop(0)
            for ti, (g0, n, s) in enumerate(ALL_TILES):
                ps = banks[ti % 2][:, 0:n]
                proj(ps, wx, 0, g0, n)
                P.act(xbuf[:, col(g0, s):col(g0, s) + n], ps, AF.Identity)
            for ti, (g0, n, s) in enumerate(rtiles):
                ps = banks[2 + ti % 2][:, 0:n]
                proj(ps, wr, 0, g0, n)
                P.act(gz[:, col(g0, s):col(g0, s) + n], ps, AF.Gelu_apprx_tanh)
            if j + 2 < 4:
                pendw.append(load_w(j + 2))
            for d in range(2):
                rb, ib = rbs[d], ibs[d]
                wa_t = wbd[d * 2]
                wx_t = wbd[d * 2 + 1]
                for wt, src in ((wa_t, rnn_wa_d), (wx_t, rnn_wx_d)):
                    P.memset(wt, 0.0)
                    P.dma("gpsimd", wt[0:64, 0:64], src[l, d, 2 * j])
                    P.dma("gpsimd", wt[64:128, 64:128], src[l, d, 2 * j + 1])
                for tp in range(4):
                    sh = (tp - 3) if d == 0 else (3 - tp)
                    src = xbuf[:, o0 + sh:o1 + sh]
                    wcol = V("rnn_conv_w", l, d, tp, j)
                    if tp == 0:
                        P.act(xc[:, o0:o1], src, AF.Identity, bias=V("rnn_conv_b", l, d, j), scale=wcol)
                    else:
                        P.stt(xc[:, o0:o1], src, wcol, xc[:, o0:o1], ALU.mult, ALU.add)
                P.act(xcb[:, o0:o1], xc[:, o0:o1], AF.Identity)
                for ti, (g0, n, s) in enumerate(ALL_TILES):
                    c0 = col(g0, s)
                    psr = banks[4 + ti % 2][:, 0:n]
                    psi = banks[6 + ti % 2][:, 0:n]
                    P.mm(psr, wa_t, xcb[:, c0:c0 + n])
                    P.mm(psi, wx_t, xcb[:, c0:c0 + n])
                    P.act(rb[:, c0:c0 + n], psr, AF.Sigmoid, bias=V("rnn_ba", l, d, j))
                    P.act(ib[:, c0:c0 + n], psi, AF.Sigmoid, bias=V("rnn_bx", l, d, j))
                ci = (l * 2 + d) * 4 + j
                P.act(rb[:, o0:o1], rb[:, o0:o1], AF.Exp, scale=clv[:, ci:ci + 1])
                P.tt(ib[:, o0:o1], ib[:, o0:o1], xc[:, o0:o1], ALU.mult)
                P.tt(xc[:, o0:o1], rb[:, o0:o1], rb[:, o0:o1], ALU.mult)
                P.act(xc[:, o0:o1], xc[:, o0:o1], AF.Sqrt, bias=1.0, scale=-1.0)
                P.tt(ib[:, o0:o1], ib[:, o0:o1], xc[:, o0:o1], ALU.mult)
            rb, ib = rbs[0], ibs[0]
            P.scan(hf[:, CTX0:CTX0 + NCX], rb[:, CTX0:CTX0 + NCX], ib[:, CTX0:CTX0 + NCX], 0.0)
            P.scan(hf[:, LAT0:LAT0 + NL], rb[:, LAT0:LAT0 + NL], ib[:, LAT0:LAT0 + NL],
                   hf[:, CTX0 + NCX - 1:CTX0 + NCX])
            rb, ib = rbs[1], ibs[1]
            P.scan(xc[:, CTX0:CTX0 + NCX][:, ::-1], rb[:, CTX0:CTX0 + NCX][:, ::-1],
                   ib[:, CTX0:CTX0 + NCX][:, ::-1], 0.0)
            P.scan(xc[:, LAT0:LAT0 + NL][:, ::-1], rb[:, LAT0:LAT0 + NL][:, ::-1],
                   ib[:, LAT0:LAT0 + NL][:, ::-1], xc[:, CTX0:CTX0 + 1])
            segs = [(LAT0, NL, 0)] + ([(CTX0, NCX, NL)] if need_ctx else [])
            for (c0, n, g0) in segs:
                P.tt(hf[:, c0:c0 + n], hf[:, c0:c0 + n], xc[:, c0:c0 + n], ALU.add)
                P.tt(yb[:, c0:c0 + n], hf[:, c0:c0 + n], gz[:, c0:c0 + n], ALU.mult)
                P.dma("sync", rnn_s[j][:, g0:g0 + n], yb[:, c0:c0 + n])
        AR.release(m)

    def conv_stage(l, need_ctx):
        m = AR.mark()
        LAT0, CTX0, W = 15, 15 + NL + 30, 15 + NL + 30 + NCX + 15

        def col(g0, s):
            return LAT0 + g0 if s == 0 else CTX0 + (g0 - NL)
        tiles = ALL_TILES if need_ctx else LAT_TILES
        wval = AR.alloc(KC * 512, BF16).rearrange("p (k n) -> p k n", n=512)
        wgat = AR.alloc(KC * 512, BF16).rearrange("p (k n) -> p k n", n=512)
        wv = wview(w_in_d, l)
        wload(wval, wv[:, :, 768:1280])
        wload(wgat, wv[:, :, 1280:1792])
        ubuf = AR.alloc(W, BF16)
        acc = [AR.alloc(W) for _ in range(4)]
        sigr = AR.rot(2, 512)
        dg = [AR.alloc(31 * 128, BF16).rearrange("p (k n) -> p k n", n=128) for _ in range(1)]
        P.memset(ubuf, 0.0)
        for j in range(4):
            dgj = dg[0]
            for tp in range(31):
                P.ts(dgj[:, tp, :], ident[:], V("conv_dw", l, tp, j), None, ALU.mult)
            for ti, (g0, n, s) in enumerate(tiles):
                pv = banks[ti % 2][:, 0:n]
                pg = banks[2 + ti % 2][:, 0:n]
                proj(pg, wgat, j * 128, g0, n)
                proj(pv, wval, j * 128, g0, n)
                sg = sigr()[:, 0:n]
                P.act(sg, pg, AF.Sigmoid)
                P.tt(ubuf[:, col(g0, s):col(g0, s) + n], pv, sg, ALU.mult)
            for ti, (g0, n, s) in enumerate(tiles):
                c0 = col(g0, s)
                pc = banks[4 + ti % 2][:, 0:n]
                for tp in range(31):
                    P.mm(pc, dgj[:, tp, :], ubuf[:, c0 + tp - 15:c0 + tp - 15 + n], start=(tp == 0), stop=(tp == 30))
                P.act(acc[j][:, c0:c0 + n], pc, AF.Identity, bias=V("conv_dw_b", l, j))
        sqr = AR.rot(2, 512)
        meanb = AR.rot(2, 512)
        varb = AR.rot(2, 512)
        tb = AR.rot(2, 512)
        ob = AR.rot(3, 512, BF16)
        for ti, (g0, n, s) in enumerate(tiles):
            c0 = col(g0, s)
            psum_ = banks[(ti % 2) * 2][:, 0:n]
            psq = banks[(ti % 2) * 2 + 1][:, 0:n]
            for j in range(4):
                P.mm(psum_, ones32[:], acc[j][:, c0:c0 + n], start=(j == 0), stop=(j == 3))
            for j in range(4):
                sq = sqr()[:, 0:n]
                P.act(sq, acc[j][:, c0:c0 + n], AF.Square)
                P.mm(psq, ones32[:], sq, start=(j == 0), stop=(j == 3))
            mean = meanb()[:, 0:n]
            var = varb()[:, 0:n]
            P.act(mean, psum_, AF.Identity, scale=1.0 / 512)
            P.tt(var, mean, mean, ALU.mult)
            P.stt(var, psq, 1.0 / 512, var, ALU.mult, ALU.subtract)
            P.act(var, var, AF.Sqrt, bias=EPS, scale=1.0)
            P.recip(var, var)
            for j in range(4):
                t = tb()[:, 0:n]
                P.tt(t, acc[j][:, c0:c0 + n], mean, ALU.subtract)
                P.tt(t, t, var, ALU.mult)
                o = ob()[:, 0:n]
                P.act(o, t, AF.Silu, bias=V("conv_ln_b", l, j), scale=V("conv_ln_g", l, j))
                P.dma("sync", conv_s[j][:, g0:g0 + n], o)
        AR.release(m)

    def attn_stage(l, need_ctx):
        m = AR.mark()
        qT = AR.alloc(4 * NT, BF16).rearrange("p (j t) -> p j t", t=NT)
        kTp = [AR.alloc(NT, BF16) for _ in range(2)]
        Vs = AR.alloc(18 * 256, BF16).rearrange("p (t h d) -> p t h d", h=2, d=128)
        P.memset(kTp[0][64:128, :], 0.0)
        P.memset(kTp[1][0:64, :], 0.0)
        for kv_ in range(2):
            P.memset(Vs[:, :, kv_, 64:128], 1.0)
        m1 = AR.mark()
        wq = AR.alloc(KC * 512, BF16).rearrange("p (k n) -> p k n", n=512)
        wkv = AR.alloc(KC * 256, BF16).rearrange("p (k n) -> p k n", n=256)
        Ct = AR.alloc(NL)
        St = AR.alloc(NL)
        wv = wview(w_in_d, l)
        for j in range(4):
            for half in range(2):
                hd = half * 4 + j
                wload(wq[:, :, j * 128 + half * 64:j * 128 + half * 64 + 64], wv[:, :, hd * 64:(hd + 1) * 64])
        wload(wkv, wv[:, :, 512:768])
        P.dma("sync", Ct, rope_d[0])
        P.dma("sync", St, rope_d[1])
        sqr = AR.rot(2, 512, BF16)
        stdb = AR.rot(2, 512)
        qnb = AR.rot(2, 512)
        t1b = AR.rot(2, 512)
        t2b = AR.rot(2, 512)
        qtiles = ALL_TILES if need_ctx else LAT_TILES
        jobs = [(j, t, "q") for j in range(4) for t in qtiles] + [(0, t, "k") for t in ALL_TILES]
        for ji, (j, (g0, n, s), kind) in enumerate(jobs):
            ps = banks[ji % 2][:, 0:n]
            if kind == "q":
                proj(ps, wq, j * 128, g0, n)
                gain = V("q_norm", l, 0)
                dst = qT[:, j, g0:g0 + n]
            else:
                proj(ps, wkv, 0, g0, n)
                gain = V("k_norm", l, 0)
                dst = None
            sq = sqr()[:, 0:n]
            P.act(sq, ps, AF.Square)
            ps2 = banks[2 + ji % 2][:, 0:n]
            P.mm(ps2, bo64[:], sq)
            std = stdb()[:, 0:n]
            P.act(std, ps2, AF.Sqrt, bias=EPS, scale=1.0 / 64)
            P.recip(std, std)
            qn = qnb()[:, 0:n]
            P.stt(qn, ps, gain, std, ALU.mult, ALU.mult)
            if s == 0:
                ps3 = banks[4 + ji % 2][:, 0:n]
                P.mm(ps3, rotm[:], qn)
                t1 = t1b()[:, 0:n]
                t2 = t2b()[:, 0:n]
                P.tt(t1, qn, Ct[:, g0:g0 + n], ALU.mult)
                P.tt(t2, ps3, St[:, g0:g0 + n], ALU.mult)
                if dst is not None:
                    P.tt(dst, t1, t2, ALU.add)
                else:
                    P.tt(kTp[0][0:64, g0:g0 + n], t1[0:64, :], t2[0:64, :], ALU.add)
                    P.tt(kTp[1][64:128, g0:g0 + n], t1[64:128, :], t2[64:128, :], ALU.add)
            else:
                if dst is not None:
                    P.act(dst, qn, AF.Identity)
                else:
                    P.act(kTp[0][0:64, g0:g0 + n], qn[0:64, :], AF.Identity)
                    P.act(kTp[1][64:128, g0:g0 + n], qn[64:128, :], AF.Identity)
        for tt_ in range(18):
            ps = banks[6 + tt_ % 2][:, 0:128]
            for k in range(KC):
                P.mm(ps, hT[:, k, tt_ * 128:(tt_ + 1) * 128], wkv[:, k, 128:256], start=(k == 0), stop=(k == KC - 1))
            P.act(Vs[:, tt_, :, 0:64], ps.rearrange("p (a b) -> p a b", b=64), AF.Identity)
        AR.release(m1)
        aT = AR.alloc(4 * NT, BF16).rearrange("p (j t) -> p j t", t=NT)
        Er = AR.rot(3, 1024, BF16)
        rdb = AR.rot(2, 512)
        units = []
        for h in range(8):
            qsets = [(g0, n, list(range(18))) for (g0, n, s) in LAT_TILES]
            if need_ctx:
                qsets.append((NL, NCX, [16, 17]))
            for qi, (g0, n, kts) in enumerate(qsets):
                prs = [kts[i:i + 2] for i in range(0, len(kts), 2)]
                for pi, pr in enumerate(prs):
                    units.append((h, g0, n, pr, pi == 0, pi == len(prs) - 1))
        state = {"pso_i": -1}

        def emit_S(ui):
            h, g0, n, pr, first, last = units[ui]
            j, half = h % 4, h // 4
            pss = pbig[ui % 2].rearrange("p (a b) -> p a b", b=512)
            for a_, kt in enumerate(pr):
                P.mm(pss[:, a_, 0:n], kTp[half][:, kt * 128:(kt + 1) * 128], qT[:, j, g0:g0 + n])
            E = Er().rearrange("p (a b) -> p a b", b=512)
            P.act(E[:, 0:len(pr), 0:n], pss[:, 0:len(pr), 0:n], AF.Exp, scale=0.125)
            return E

        def emit_PV(ui, E):
            h, g0, n, pr, first, last = units[ui]
            j, half = h % 4, h // 4
            po = half * 64
            if first:
                state["pso_i"] += 1
            pso = banks[4 + state["pso_i"] % 3][:, 0:n]
            for a_, kt in enumerate(pr):
                P.mm(pso, Vs[:, kt, half, :], E[:, a_, 0:n], start=(first and a_ == 0), stop=(last and a_ == len(pr) - 1))
            if last:
                rd = rdb()
                P.recip(rd[0:64, 0:n], pso[64:128, :])
                P.tt(aT[po:po + 64, j, g0:g0 + n], pso[0:64, :], rd[0:64, 0:n], ALU.mult)
        Eq = [emit_S(0)]
        for ui in range(len(units)):
            if ui + 1 < len(units):
                Eq.append(emit_S(ui + 1))
            emit_PV(ui, Eq.pop(0))
        nn = NT if need_ctx else NL
        for j in range(4):
            P.dma("sync", attn_s[j][:, 0:nn], aT[:, j, 0:nn])
        AR.release(m)

    def merge_stage(l, need_ctx):
        m = AR.mark()
        tiles = ALL_TILES if need_ctx else LAT_TILES
        mT = AR.alloc(KC * NT, BF16).rearrange("p (k t) -> p k t", t=NT)
        m1 = AR.mark()
        wbr = [[AR.alloc(4 * 128, BF16).rearrange("p (k n) -> p k n", n=128) for _ in range(3)] for _ in range(2)]
        wgt = [[AR.alloc(KC * 128, BF16).rearrange("p (k n) -> p k n", n=128) for _ in range(3)] for _ in range(2)]
        brr = [AR.rot(2, 4 * 512, BF16) for _ in range(3)]
        gsb = [AR.alloc(512) for _ in range(3)]
        mt = AR.rot(2, 512)
        wv = wview(w_in_d, l)
        wao = w_ao_d[l].rearrange("(h j d) n -> h d j n", h=2, j=4)
        wco = wview(w_co_d, l)
        wro = wview(w_ro_d, l)

        def load_c(c):
            wb, wg = wbr[c % 2], wgt[c % 2]
            for half in range(2):
                wload(wb[0][half * 64:(half + 1) * 64, :, :], wao[half][:, :, c * 128:(c + 1) * 128])
            wload(wb[1], wco[:, :, c * 128:(c + 1) * 128])
            wload(wb[2], wro[:, :, c * 128:(c + 1) * 128])
            for br in range(3):
                c0 = 2816 + br * 1024 + c * 128
                wload(wg[br], wv[:, :, c0:c0 + 128])
        if not dbg.get("skip_merge"):
            load_c(0)
            load_c(1)
        scr = (attn_s, conv_s, rnn_s)
        for c in (range(KC) if not dbg.get("skip_merge") else []):
            wb, wg = wbr[c % 2], wgt[c % 2]
            for ti, (g0, n, s) in enumerate(tiles):
                brt = []
                for br in range(3):
                    bt = brr[br]().rearrange("p (j t) -> p j t", t=512)
                    P.dma("sync", bt[:, :, 0:n], scr[br].rearrange("j p t -> p j t")[:, :, g0:g0 + n])
                    brt.append(bt)
                pb = [banks[br][:, 0:n] for br in range(3)]
                pg = [banks[3 + br][:, 0:n] for br in range(3)]
                for br in range(3):
                    for k in range(KC):
                        P.mm(pg[br], wg[br][:, k, :], hT[:, k, g0:g0 + n], start=(k == 0), stop=(k == KC - 1))
                    for k in range(4):
                        P.mm(pb[br], wb[br][:, k, :], brt[br][:, k, 0:n], start=(k == 0), stop=(k == 3))
                for br in range(3):
                    P.act(gsb[br][:, 0:n], pg[br], AF.Sigmoid)
                ma = mt()[:, 0:n]
                mb = mt()[:, 0:n]
                P.tt(ma, pb[0], gsb[0][:, 0:n], ALU.mult)
                P.tt(mb, pb[1], gsb[1][:, 0:n], ALU.mult)
                P.tt(ma, ma, mb, ALU.add)
                P.tt(mb, pb[2], gsb[2][:, 0:n], ALU.mult)
                P.tt(mT[:, c, g0:g0 + n], ma, mb, ALU.add)
            if c + 2 < KC:
                load_c(c + 2)
        AR.release(m1)
        wo = AR.alloc(KC * D, BF16).rearrange("p (k n) -> p k n", n=D)
        wov = wview(w_out_d, l)
        wload(wo[:, :, 0:512], wov[:, :, 0:512])
        wload(wo[:, :, 512:1024], wov[:, :, 512:1024])
        sqr = AR.rot(3, 512, BF16)
        stdb = AR.rot(2, 512)
        tb = AR.rot(2, 512)
        ob = AR.alloc(KC * 512).rearrange("p (k t) -> p k t", t=512)
        t512 = LAT_TILES + ([CTX_TILE] if need_ctx else [])
        for ti, (g0, n, s) in enumerate(t512 if not (dbg.get("skip_out") or dbg.get("o_nomm")) else []):
            pss = banks[6 + ti % 2][:, 0:n]
            for c in range(KC):
                pso = banks[c % 6][:, 0:n]
                for k in range(KC):
                    P.mm(pso, wo[:, k, c * 128:(c + 1) * 128], mT[:, k, g0:g0 + n], start=(k == 0), stop=(k == KC - 1))
                if dbg.get("o_mmonly"):
                    continue
                sq = sqr()[:, 0:n]
                P.act(sq, pso, AF.Square)
                P.act(ob[:, c, 0:n], pso, AF.Identity)
                if not dbg.get("o_noss"):
                    P.mm(pss, ones_bf[:], sq, start=(c == 0), stop=(c == KC - 1))
            if dbg.get("o_noss"):
                continue
            std = stdb()[:, 0:n]
            P.act(std, pss, AF.Sqrt, bias=EPS, scale=1.0 / D)
            P.recip(std, std)
            if dbg.get("o_noupd"):
                continue
            for c in range(KC):
                t = tb()[:, 0:n]
                P.tt(t, ob[:, c, 0:n], std, ALU.mult)
                P.stt(xT[:, c, g0:g0 + n], t, G1[l][:, c, s:s + 1], xT[:, c, g0:g0 + n], ALU.mult, ALU.add)
        AR.release(m)

    def ffn_stage(l, need_ctx):
        m = AR.mark()
        ftiles = [[(0, 768, 0)], [(768, 1536, 0)],
                  [(1536, 2048, 0)] + ([(NL, NT, 1)] if need_ctx else [])]
        aT = AR.alloc(24 * 768, BF16).rearrange("p (k t) -> p k t", t=768)
        wgr = AR.rot(3, KC * 128, BF16)
        wvr = AR.rot(3, KC * 128, BF16)
        wdr = AR.rot(2, 24 * 128, BF16)
        UW = 776
        m2 = AR.mark()
        ug = AR.rot(2, UW)
        uv = AR.rot(2, UW)
        ag = AR.rot(2, 768)
        av = AR.rot(2, 768)
        AR.release(m2)
        obf = AR.alloc(KC * 768, BF16).rearrange("p (k t) -> p k t", t=768)
        sqr = AR.rot(2, 512, BF16)
        stdb = AR.alloc(768)
        tb = AR.rot(2, 768)
        AR.top = max(AR.top, m2 + 2 * (2 * UW + 2 * 768))
        upv = wview(ffn_up_d, l)
        dnv = wview(ffn_down_d, l)
        for segs in ftiles:
            lay = []
            ucol = 0
            bcol = 0
            for (g0, g1, s) in segs:
                slo, shi = (0, NL) if s == 0 else (NL, NT)
                hl = 1 if g0 > slo else 0
                hr = 1 if g1 < shi else 0
                lay.append((g0, g1, s, hl, hr, ucol, bcol))
                ucol += (g1 - g0) + 2
                bcol += g1 - g0
            ntok = bcol

            def load_pair(p):
                a = wgr().rearrange("p (k n) -> p k n", n=128)
                b = wvr().rearrange("p (k n) -> p k n", n=128)
                wload(a, upv[:, :, p * 128:(p + 1) * 128])
                wload(b, upv[:, :, 3072 + p * 128:3072 + (p + 1) * 128])
                return a, b
            pend = [load_pair(0), load_pair(1), load_pair(2)]
            bk = [0]
            tail = None
            for p in range(24):
                wg_, wv_ = pend.pop(0)
                ugb, uvb, agb, avb = ug(), uv(), ag(), av()
                for (wsb, ub, ab, fch) in ((wg_, ugb, agb, p), (wv_, uvb, avb, 24 + p)):
                    for (g0, g1, s, hl, hr, uc, bc) in lay:
                        n = g1 - g0
                        if not hl:
                            P.memset(ub[:, uc:uc + 1], 0.0)
                        if not hr:
                            P.memset(ub[:, uc + n + 1:uc + n + 2], 0.0)
                        r0, r1 = g0 - hl, g1 + hr
                        dc = uc + 1 - hl
                        while r0 < r1:
                            pn = min(512, r1 - r0)
                            ps = banks[bk[0] % 6][:, 0:pn]
                            bk[0] += 1
                            for k in range(KC):
                                P.mm(ps, wsb[:, k, :], hT[:, k, r0:r0 + pn], start=(k == 0), stop=(k == KC - 1))
                            P.act(ub[:, dc:dc + pn], ps, AF.Copy)
                            r0 += pn
                            dc += pn
                        for tp in range(3):
                            src = ub[:, uc + tp:uc + tp + n]
                            wcol = V("ffn_dw", l, tp, fch)
                            if tp == 0:
                                P.act(ab[:, bc:bc + n], src, AF.Copy, scale=wcol)
                            else:
                                P.stt(ab[:, bc:bc + n], src, wcol, ab[:, bc:bc + n], ALU.mult, ALU.add)
                if tail is not None:
                    tail()

                def tail(agb=agb, avb=avb, p=p):
                    P.act(agb[:, 0:ntok], agb[:, 0:ntok], AF.Gelu_apprx_tanh, bias=V("ffn_dw_b", l, p))
                    P.stt(aT[:, p, 0:ntok], avb[:, 0:ntok], V("ffn_dw_b", l, 24 + p), agb[:, 0:ntok], ALU.add, ALU.mult)
                if p + 3 < 24:
                    pend.append(load_pair(p + 3))
            tail()
            pieces = []
            r0 = 0
            while r0 < ntok:
                pn = min(512, ntok - r0)
                pieces.append((r0, pn))
                r0 += pn

            def load_d(c):
                wd = wdr().rearrange("p (k n) -> p k n", n=128)
                wload(wd[:, 0:12, :], dnv[:, 0:12, c * 128:(c + 1) * 128])
                wload(wd[:, 12:24, :], dnv[:, 12:24, c * 128:(c + 1) * 128])
                return wd
            pendd = [load_d(0), load_d(1)]
            for c in range(KC):
                wd = pendd.pop(0)
                for pi, (r0, pn) in enumerate(pieces):
                    ps = banks[(c * 2 + pi) % 6][:, 0:pn]
                    for k in range(24):
                        P.mm(ps, wd[:, k, :], aT[:, k, r0:r0 + pn], start=(k == 0), stop=(k == 23))
                    P.act(obf[:, c, r0:r0 + pn], ps, AF.Copy)
                    sq = sqr()[:, 0:pn]
                    P.act(sq, ps, AF.Square)
                    P.mm(banks[6 + pi][:, 0:pn], ones_bf[:], sq, start=(c == 0), stop=(c == KC - 1))
                if c + 2 < KC:
                    pendd.append(load_d(c + 2))
            for pi, (r0, pn) in enumerate(pieces):
                P.act(stdb[:, r0:r0 + pn], banks[6 + pi][:, 0:pn], AF.Sqrt, bias=EPS, scale=1.0 / D)
            P.recip(stdb[:, 0:ntok], stdb[:, 0:ntok])
            for c in range(KC):
                t = tb()
                P.tt(t[:, 0:ntok], obf[:, c, 0:ntok], stdb[:, 0:ntok], ALU.mult)
                for (g0, g1, s, hl, hr, uc, bc) in lay:
                    n = g1 - g0
                    P.stt(xT[:, c, g0:g1], t[:, bc:bc + n], G2[l][:, c, s:s + 1], xT[:, c, g0:g1], ALU.mult, ALU.add)
        AR.release(m)

    finals = []

    def on(name):
        return stages is None or name in stages
    for l in range(nlayers):
        need_ctx = l < L - 1
        if on("N1"):
            norm_stage(l, A1[l], 0, ALL_TILES)
        if l == 0 and dbg.get("dump_h1"):
            hd = nc.dram_tensor("hT_dump", [128, KC, NT], BF16, kind="ExternalOutput").ap()
            for c in range(KC):
                finals.append(P.dma("sync", hd[:, c, :], hT[:, c, :]))
            md = nc.dram_tensor("modv_dump", [128, 96], F32, kind="ExternalOutput").ap()
            finals.append(P.dma("sync", md, modv[0][:].rearrange("p a b -> p (a b)")))
        if on("R"):
            rnn_stage(l, need_ctx)
        if on("C"):
            conv_stage(l, need_ctx)
        if on("Q"):
            attn_stage(l, need_ctx)
        if on("M"):
            merge_stage(l, need_ctx)
        if l == 0 and dbg.get("dump_x1"):
            xd = nc.dram_tensor("x1_dump", [128, KC, NT], F32, kind="ExternalOutput").ap()
            for c in range(KC):
                finals.append(P.dma("sync", xd[:, c, :], xT[:, c, :]))
        if on("N2"):
            norm_stage(l, A2[l], 24, ALL_TILES if need_ctx else LAT_TILES)
        if on("F"):
            ffn_stage(l, need_ctx)

    for c in range(KC):
        finals.append(P.dma("sync", yT_d[:, c, :], xT[:, c, 0:NL]))
    P.emit(final_waits=finals)
    st.close()
    return nc, P


def rope_tables():
    rows = NL // 64
    row = np.repeat(np.arange(rows), 64).astype(np.float32)
    colv = np.tile(np.arange(64), rows).astype(np.float32)
    n_freq = 16
    freq = (np.float32(10000.0) ** (-np.arange(n_freq, dtype=np.float32) / np.float32(n_freq))).astype(np.float32)
    ang = np.concatenate([row[:, None] * freq, colv[:, None] * freq], axis=-1).astype(np.float32)
    cos = np.cos(ang).astype(np.float32).T
    sin = np.sin(ang).astype(np.float32).T
    C = np.concatenate([cos, cos, cos, cos], axis=0)
    S = np.concatenate([sin, sin, sin, sin], axis=0)
    rot = np.zeros((128, 128), np.float32)
    for hb in (0, 64):
        for mI in range(32):
            rot[hb + mI + 32, hb + mI] = -1.0
            rot[hb + mI, hb + mI + 32] = 1.0
    return np.ascontiguousarray(np.stack([C, S], 0)), np.ascontiguousarray(np.stack([rot, np.eye(128, dtype=np.float32)], 0))


_CACHE = {}


def kernel(**inputs):
    if "nc" not in _CACHE:
        _CACHE["nc"] = build_program()
    nc, _ = _CACHE["nc"]
    f = lambda k: np.ascontiguousarray(np.asarray(inputs[k], np.float32))
    x, ctx, c, c_ctx = f("x"), f("ctx"), f("c"), f("c_ctx")
    vecs = pack_vecs(inputs)
    rope, rot = rope_tables()
    shared = {"vecs": vecs, "rope": rope, "rotm": rot}
    for k in ("w_mod", "w_in", "w_attn_out", "w_conv_out", "w_rnn_out", "w_out", "ffn_up", "ffn_down", "rnn_wa", "rnn_wx"):
        shared[k] = f(k)
    in_maps = []
    B = x.shape[0]
    for b in range(B):
        xa = np.concatenate([x[b], ctx[b]], axis=0)
        xTb = np.ascontiguousarray(xa.T.reshape(KC, 128, NT).transpose(1, 0, 2))
        cc = np.stack([c[b], c_ctx], axis=-1)
        cTb = np.ascontiguousarray(cc.reshape(KC, 128, 2).transpose(1, 0, 2).reshape(128, KC * 2))
        d = dict(shared)
        d["xT"] = xTb
        d["cT"] = cTb
        in_maps.append(d)
    res = run_bass_kernel_spmd(nc, in_maps, core_ids=list(range(B)))
    out = np.empty((B, NL, D), np.float32)
    for b in range(B):
        yT = np.asarray(res.results[b]["yT"])
        out[b] = yT.transpose(2, 1, 0).reshape(NL, D)
    return out
```

```python
import contextlib
import numpy as np
import concourse.bass as bass
import concourse.mybir as mybir
from concourse.bass_utils import run_bass_kernel_spmd

F32 = mybir.dt.float32
BF16 = mybir.dt.bfloat16
AF = mybir.ActivationFunctionType
ALU = mybir.AluOpType
ESZ = {F32: 4, BF16: 2}

ENGS = ("sync", "scalar", "vector", "gpsimd", "tensor")

L = 2
D = 1024
NL = 2048
NCX = 256
NT = NL + NCX
EPS = 1e-6
KC = 8


def _esz(ap):
    try:
        return ESZ[ap.dtype]
    except Exception:
        return 4


def _box(ap):
    name = ap.tensor.name
    dims = [(int(s), int(c)) for s, c in ap.ap]
    off = int(ap.offset)
    es = _esz(ap)
    space = str(ap.space).upper()
    if "SB" not in space and "PSUM" not in space:
        lo = off + sum(min(0, s * (c - 1)) for s, c in dims)
        hi = off + sum(max(0, s * (c - 1)) for s, c in dims) + 1
        return (name, 0, 1, lo * es, hi * es)
    pstep, pcnt = dims[0]
    if pstep <= 0:
        p0 = 0
        foff = off
    else:
        p0 = off // pstep
        foff = off - p0 * pstep
    fd = dims[1:]
    lo = foff + sum(min(0, s * (c - 1)) for s, c in fd)
    hi = foff + sum(max(0, s * (c - 1)) for s, c in fd) + 1
    return (name, p0, p0 + pcnt, lo * es, hi * es)


class Op:
    __slots__ = ("eng", "fn", "deps", "tok", "needs_inc", "is_dma", "slot", "val")

    def __init__(self, eng, fn):
        self.eng = eng
        self.fn = fn
        self.deps = set()
        self.tok = None
        self.needs_inc = False
        self.is_dma = False
        self.slot = None
        self.val = 0


class Prog:
    def __init__(self, nc, n_dma_slots=16):
        self.nc = nc
        self.ops = []
        self.recs = {}
        self.n_dma_slots = n_dma_slots
        self.dma_count = {}
        self.slot_last = {}

    def _track(self, op, reads, writes):
        for ap in reads:
            b = _box(ap)
            lst = self.recs.get(b[0], [])
            keep = []
            for r in lst:
                ov = r[0] < b[2] and b[1] < r[1] and r[2] < b[4] and b[3] < r[3]
                if ov and r[5] and r[4] is not op:
                    op.deps.add(r[4])
                if (not r[5]) and (not r[4].is_dma) and (not op.is_dma) and r[4].eng == op.eng \
                        and b[1] <= r[0] and r[1] <= b[2] and b[3] <= r[2] and r[3] <= b[4]:
                    continue
                keep.append(r)
            keep.append([b[1], b[2], b[3], b[4], op, False])
            self.recs[b[0]] = keep
        for ap in writes:
            b = _box(ap)
            lst = self.recs.get(b[0], [])
            keep = []
            for r in lst:
                ov = r[0] < b[2] and b[1] < r[1] and r[2] < b[4] and b[3] < r[3]
                if ov and r[4] is not op:
                    op.deps.add(r[4])
                cov = b[1] <= r[0] and r[1] <= b[2] and b[3] <= r[2] and r[3] <= b[4]
                if cov and r[4] is not op:
                    continue
                keep.append(r)
            keep.append([b[1], b[2], b[3], b[4], op, True])
            self.recs[b[0]] = keep

    def add(self, eng, fn, reads=(), writes=()):
        op = Op(eng, fn)
        self._track(op, reads, writes)
        self.ops.append(op)
        return op

    def dma(self, eng, out, in_, **kw):
        op = Op(eng, None)
        op.is_dma = True
        i = self.dma_count.get(eng, 0)
        self.dma_count[eng] = i + 1
        op.slot = "%s%d" % (eng[0], i % (self.n_dma_slots if eng == "sync" else 4))
        prev = self.slot_last.get(op.slot)
        op.val = (prev.val if prev is not None else 0) + 16
        if prev is not None:
            op.deps.add(prev)
        self.slot_last[op.slot] = op
        op.fn = lambda e: e.dma_start(out=out, in_=in_, **kw)
        self._track(op, [in_], [out])
        self.ops.append(op)
        return op

    def mm(self, out, lhsT, rhs, start=True, stop=True):
        return self.add("tensor", lambda e: e.matmul(out, lhsT, rhs, start=start, stop=stop),
                        [lhsT, rhs] + ([] if start else [out]), [out])

    def act(self, out, in_, func, bias=None, scale=None):
        kw = {}
        rd = [in_]
        if bias is not None:
            kw["bias"] = bias
            if not isinstance(bias, (int, float)):
                rd.append(bias)
        if scale is not None:
            kw["scale"] = scale
            if not isinstance(scale, (int, float)):
                rd.append(scale)
        return self.add("scalar", lambda e: e.activation(out=out, in_=in_, func=func, **kw), rd, [out])

    def tt(self, out, in0, in1, op, eng="vector"):
        return self.add(eng, lambda e: e.tensor_tensor(out=out, in0=in0, in1=in1, op=op), [in0, in1], [out])

    def ts(self, out, in0, s1, s2, op0, op1=None, eng="vector"):
        rd = [in0] + [s for s in (s1, s2) if s is not None and not isinstance(s, (int, float))]
        if op1 is None:
            return self.add(eng, lambda e: e.tensor_scalar(out=out, in0=in0, scalar1=s1, scalar2=None, op0=op0), rd, [out])
        return self.add(eng, lambda e: e.tensor_scalar(out=out, in0=in0, scalar1=s1, scalar2=s2, op0=op0, op1=op1), rd, [out])

    def stt(self, out, in0, scalar, in1, op0, op1):
        rd = [in0, in1] + ([] if isinstance(scalar, (int, float)) else [scalar])
        return self.add("vector", lambda e: e.scalar_tensor_tensor(out=out, in0=in0, scalar=scalar, in1=in1, op0=op0, op1=op1), rd, [out])

    def copy(self, out, in_, eng="vector"):
        return self.add(eng, lambda e: e.tensor_copy(out=out, in_=in_), [in_], [out])

    def memset(self, ap, val, eng="vector"):
        return self.add(eng, lambda e: e.memset(ap, val), [], [ap])

    def recip(self, out, in_):
        return self.add("vector", lambda e: e.reciprocal(out=out, in_=in_), [in_], [out])

    def scan(self, out, d0, d1, init):
        rd = [d0, d1] + ([] if isinstance(init, (int, float)) else [init])
        return self.add("vector", lambda e: e.tensor_tensor_scan(out=out, data0=d0, data1=d1, initial=init,
                                                                 op0=ALU.mult, op1=ALU.add), rd, [out])

    def emit(self, final_waits=()):
        nc = self.nc
        seq = {e: 0 for e in ENGS}
        for op in self.ops:
            for d in op.deps:
                d.needs_inc = True
        for op in final_waits:
            op.needs_inc = True
        for op in self.ops:
            if op.is_dma:
                op.tok = (op.slot, op.val)
            elif op.needs_inc:
                seq[op.eng] += 1
                op.tok = (op.eng, seq[op.eng])
        clock = {e: {} for e in ENGS}
        opclock = {}
        per_eng = {e: [] for e in ENGS}
        slots = set()
        for op in self.ops:
            ck = clock[op.eng]
            wm = {}
            for d in sorted(op.deps, key=lambda o: (o.tok[0], o.tok[1])):
                src, v = d.tok
                if src == "tensor" and op.eng == "tensor":
                    continue
                if ck.get(src, 0) >= v:
                    continue
                wm[src] = max(wm.get(src, 0), v)
                oc = opclock.get(id(d))
                if oc:
                    for k, vv in oc.items():
                        if ck.get(k, 0) < vv:
                            ck[k] = vv
                ck[src] = max(ck.get(src, 0), v)
            if op.tok is not None:
                oc = dict(ck)
                oc[op.tok[0]] = max(oc.get(op.tok[0], 0), op.tok[1])
                opclock[id(op)] = oc
                if op.is_dma:
                    slots.add(op.slot)
            per_eng[op.eng].append((op, list(wm.items())))
        fin = [op.tok for op in final_waits] + [o.tok for o in self.slot_last.values()]
        with contextlib.ExitStack() as st:
            sems = {}
            for e in ENGS:
                sems[e] = st.enter_context(nc.semaphore("s_" + e))
            for s in sorted(slots):
                sems[s] = st.enter_context(nc.semaphore("s_" + s))
            block = st.enter_context(nc.Block())

            def run(engname):
                def body(e):
                    for op, waits in per_eng[engname]:
                        for s, v in waits:
                            e.wait_ge(sems[s], v)
                        ins = op.fn(e)
                        if op.is_dma:
                            ins.then_inc(sems[op.tok[0]], 16)
                        elif op.needs_inc:
                            ins.then_inc(sems[op.eng], 1)
                    if engname == "sync":
                        for s, v in fin:
                            e.wait_ge(sems[s], v)
                return body

            block.sync(run("sync"))
            block.scalar(run("scalar"))
            block.vector(run("vector"))
            block.gpsimd(run("gpsimd"))
            block.tensor(run("tensor"))
        self.stats = {e: len(per_eng[e]) for e in ENGS}


VEC_SPEC = [
    ("b_mod", (L,), 6144), ("norm_pre_mix", (L,), 1024), ("norm_post_mix", (L,), 1024),
    ("norm_pre_ffn", (L,), 1024), ("norm_post_ffn", (L,), 1024),
    ("q_norm", (L,), 128), ("k_norm", (L,), 128),
    ("conv_dw", (L, 31), 512), ("conv_dw_b", (L,), 512), ("conv_ln_g", (L,), 512), ("conv_ln_b", (L,), 512),
    ("rnn_conv_w", (L, 2, 4), 512), ("rnn_conv_b", (L, 2), 512), ("rnn_ba", (L, 2), 512),
    ("rnn_bx", (L, 2), 512), ("rnn_lambda", (L, 2), 512),
    ("ffn_dw", (L, 3), 6144), ("ffn_dw_b", (L,), 6144),
]
VEC_OFF = {}
_o = 0
for _n, _lead, _f in VEC_SPEC:
    VEC_OFF[_n] = (_o, _lead, _f // 128)
    _o += int(np.prod(_lead)) * (_f // 128)
NV = _o


def pack_vecs(inputs):
    out = np.zeros((128, NV), np.float32)
    for n, lead, f in VEC_SPEC:
        a = np.asarray(inputs[n], np.float32)
        if n in ("q_norm", "k_norm"):
            a = np.concatenate([a, a], axis=-1)
        a = a.reshape(int(np.prod(lead)), f // 128, 128)
        base = VEC_OFF[n][0]
        out[:, base:base + a.shape[0] * a.shape[1]] = a.reshape(-1, 128).T
    return out


def build_program(dbg=None):
    nc = bass.Bass("TRN2", target_bir_lowering=False)
    dbg = dbg or {}
    stages = dbg.get("stages")
    nlayers = dbg.get("layers", L)
    st = contextlib.ExitStack()

    def din(name, shape, dt=F32):
        return nc.dram_tensor(name, list(shape), dt, kind="ExternalInput").ap()

    xT_d = din("xT", [128, KC, NT])
    cT_d = din("cT", [128, KC * 2])
    vecs_d = din("vecs", [128, NV])
    rope_d = din("rope", [2, 128, NL])
    rotm_d = din("rotm", [2, 128, 128])
    w_mod_d = din("w_mod", [L, D, 6144])
    w_in_d = din("w_in", [L, D, 5888])
    w_ao_d = din("w_attn_out", [L, 512, D])
    w_co_d = din("w_conv_out", [L, 512, D])
    w_ro_d = din("w_rnn_out", [L, 512, D])
    w_out_d = din("w_out", [L, D, D])
    ffn_up_d = din("ffn_up", [L, D, 6144])
    ffn_down_d = din("ffn_down", [L, 3072, D])
    rnn_wa_d = din("rnn_wa", [L, 2, 8, 64, 64])
    rnn_wx_d = din("rnn_wx", [L, 2, 8, 64, 64])
    yT_d = nc.dram_tensor("yT", [128, KC, NL], F32, kind="ExternalOutput").ap()
    skind = "ExternalOutput" if dbg else "Internal"
    attn_s = nc.dram_tensor("attn_s", [4, 128, NT], BF16, kind=skind).ap()
    conv_s = nc.dram_tensor("conv_s", [4, 128, NT], BF16, kind=skind).ap()
    rnn_s = nc.dram_tensor("rnn_s", [4, 128, NT], BF16, kind=skind).ap()

    def sb(name, shape, dt=F32):
        return st.enter_context(nc.sbuf_tensor(name, list(shape), dt))

    P = Prog(nc)

    xT = sb("xTs", [128, KC, NT])
    hT = sb("hTs", [128, KC, NT], BF16)
    vecs = sb("vecs_s", [128, NV])
    modv = [sb("modv%d" % l, [128, 48, 2]) for l in range(L)]
    A1 = [sb("A1_%d" % l, [128, KC, 2]) for l in range(L)]
    G1 = [sb("G1_%d" % l, [128, KC, 2]) for l in range(L)]
    A2 = [sb("A2_%d" % l, [128, KC, 2]) for l in range(L)]
    G2 = [sb("G2_%d" % l, [128, KC, 2]) for l in range(L)]
    clv = sb("clv", [128, L * 2 * 4])
    ones_bf = sb("ones_bf", [128, 128], BF16)
    ones32 = sb("ones32", [128, 128])
    bo64 = sb("bo64", [128, 128], BF16)
    rotm = sb("rotm_s", [128, 128])
    ident = sb("ident_s", [128, 128])
    ct = sb("ct_s", [128, KC * 2])
    scb = sb("scb", [128, KC * 2], BF16)
    ARENA_W = 23296
    arena = sb("arena", [128, ARENA_W])
    pbig = [st.enter_context(nc.psum_tensor("pbig%d" % i, [128, 1024], F32)) for i in range(4)]
    banks = [pbig[i // 2][:, (i % 2) * 512:(i % 2) * 512 + 512] for i in range(8)]

    class Arena:
        def __init__(self):
            self.top = 0

        def mark(self):
            return self.top

        def release(self, m):
            self.top = m

        def alloc(self, n, dt=F32):
            w = n if dt == F32 else (n + 1) // 2
            w = (w + 7) // 8 * 8
            assert self.top + w <= ARENA_W, ("arena overflow", self.top, w)
            v = arena[:, self.top:self.top + w]
            self.top += w
            if dt != F32:
                v = v.bitcast(dt)[:, 0:n]
            return v

        def rot(self, k, n, dt=F32):
            bufs = [self.alloc(n, dt) for _ in range(k)]
            state = [0]

            def nxt():
                b = bufs[state[0] % k]
                state[0] += 1
                return b
            return nxt

    AR = Arena()

    def V(name, *idx):
        base, lead, nch = VEC_OFF[name]
        *li, c = idx
        flat = 0
        for i, d in zip(li, lead):
            flat = flat * d + i
        col = base + flat * nch + c
        return vecs[:, col:col + 1]

    def wview(wd, l):
        return wd[l].rearrange("(k p) n -> p k n", p=128)

    def wload(dst, src):
        P.dma("gpsimd", dst, src)

    P.dma("sync", vecs[:], vecs_d)
    P.dma("sync", ct[:], cT_d)
    P.dma("sync", rotm[:], rotm_d[0])
    P.dma("sync", ident[:], rotm_d[1])
    P.memset(ones_bf[:], 1.0)
    P.memset(ones32[:], 1.0)
    P.memset(bo64[:], 0.0)
    P.memset(bo64[0:64, 0:64], 1.0)
    P.memset(bo64[64:128, 64:128], 1.0)
    for c in range(KC):
        P.dma("sync", xT[:, c, :], xT_d[:, c, :])
    P.act(scb[:], ct[:], AF.Silu)
    lb = VEC_OFF["rnn_lambda"][0]
    P.act(clv[:], vecs[:, lb:lb + L * 8], AF.Exp, scale=-1.0)
    P.act(clv[:], clv[:], AF.Ln, bias=1.0)
    P.ts(clv[:], clv[:], -8.0, None, ALU.mult)

    m0 = AR.mark()
    wrot = AR.rot(3, KC * 512, BF16)
    scb3 = scb[:].rearrange("p (k s) -> p k s", s=2)
    bi = [0]

    def modblock_load(l, blk):
        wb = wrot().rearrange("p (k n) -> p k n", n=512)
        wload(wb, wview(w_mod_d, l)[:, :, blk * 512:(blk + 1) * 512])
        return wb

    jobs = [(l, blk) for l in range(L) for blk in range(12)]
    pend = [modblock_load(*jobs[0]), modblock_load(*jobs[1])]
    for ji, (l, blk) in enumerate(jobs):
        wb = pend.pop(0)
        for j in range(4):
            ch = blk * 4 + j
            ps = banks[bi[0] % 8][:, 0:2]
            bi[0] += 1
            for k in range(KC):
                P.mm(ps, wb[:, k, j * 128:(j + 1) * 128], scb3[:, k, :], start=(k == 0), stop=(k == KC - 1))
            P.ts(modv[l][:, ch, :], ps, V("b_mod", l, ch), None, ALU.add)
        if ji + 2 < len(jobs):
            pend.append(modblock_load(*jobs[ji + 2]))
    for l in range(L):
        for c in range(KC):
            P.ts(A1[l][:, c, :], modv[l][:, 8 + c, :], 1.0, V("norm_pre_mix", l, c), ALU.add, ALU.mult)
            P.ts(G1[l][:, c, :], modv[l][:, 16 + c, :], V("norm_post_mix", l, c), None, ALU.mult)
            P.ts(A2[l][:, c, :], modv[l][:, 32 + c, :], 1.0, V("norm_pre_ffn", l, c), ALU.add, ALU.mult)
            P.ts(G2[l][:, c, :], modv[l][:, 40 + c, :], V("norm_post_ffn", l, c), None, ALU.mult)
    AR.release(m0)

    LAT_TILES = [(i * 512, 512, 0) for i in range(4)]
    CTX_TILE = (NL, NCX, 1)
    ALL_TILES = LAT_TILES + [CTX_TILE]

    def norm_stage(l, A, shbase, tiles):
        m = AR.mark()
        sqr = AR.rot(3, 512, BF16)
        stdb = AR.rot(2, 512)
        rstb = AR.rot(2, 512)
        tb = AR.rot(3, 512)
        for ti, (g0, n, s) in enumerate(tiles):
            ss = banks[ti % 2][:, 0:n]
            for c in range(KC):
                sq = sqr()[:, 0:n]
                P.act(sq, xT[:, c, g0:g0 + n], AF.Square)
                P.mm(ss, ones_bf[:], sq, start=(c == 0), stop=(c == KC - 1))
            std = stdb()[:, 0:n]
            rstd = rstb()[:, 0:n]
            P.act(std, ss, AF.Sqrt, bias=EPS, scale=1.0 / D)
            P.recip(rstd, std)
            for c in range(KC):
                t = tb()[:, 0:n]
                P.tt(t, xT[:, c, g0:g0 + n], rstd, ALU.mult)
                P.act(hT[:, c, g0:g0 + n], t, AF.Identity, bias=modv[l][:, shbase + c, s:s + 1], scale=A[:, c, s:s + 1])
        AR.release(m)

    def proj(ps, wsb, col0, g0, n):
        for k in range(KC):
            P.mm(ps, wsb[:, k, col0:col0 + 128], hT[:, k, g0:g0 + n], start=(k == 0), stop=(k == KC - 1))

    def rnn_stage(l, need_ctx):
        m = AR.mark()
        LAT0, CTX0, W = 4, 4 + NL + 8, 4 + NL + 8 + NCX + 4

        def col(g0, s):
            return LAT0 + g0 if s == 0 else CTX0 + (g0 - NL)
        wxr = AR.rot(2, KC * 128, BF16)
        wrr = AR.rot(2, KC * 128, BF16)
        wv = wview(w_in_d, l)

        def load_w(j):
            a_ = wxr().rearrange("p (k n) -> p k n", n=128)
            b_ = wrr().rearrange("p (k n) -> p k n", n=128)
            wload(a_, wv[:, :, 1792 + j * 128:1792 + (j + 1) * 128])
            wload(b_, wv[:, :, 2304 + j * 128:2304 + (j + 1) * 128])
            return a_, b_
        xbuf = AR.alloc(W)
        xc = AR.alloc(W)
        rbs = [AR.alloc(W), AR.alloc(W)]
        ibs = [AR.alloc(W), AR.alloc(W)]
        hf = AR.alloc(W)
        gz = AR.alloc(W)
        xcb = AR.alloc(W, BF16)
        yb = AR.alloc(W, BF16)
        wbd = [AR.alloc(128, BF16) for _ in range(4)]
        for bufz in [xbuf, xc, hf, gz] + rbs + ibs:
            P.memset(bufz, 0.0)
        P.memset(xcb, 0.0)
        P.memset(yb, 0.0)
        rtiles = ALL_TILES if need_ctx else LAT_TILES
        o0, o1 = LAT0, CTX0 + NCX
        pendw = [load_w(0), load_w(1)]
        for j in range(4):
            wx, wr = pendw.pop(0)
            for ti, (g0, n, s) in enumerate(ALL_TILES):
                ps = banks[ti % 2][:, 0:n]
                proj(ps, wx, 0, g0, n)
                P.act(xbuf[:, col(g0, s):col(g0, s) + n], ps, AF.Identity)
            for ti, (g0, n, s) in enumerate(rtiles):
                ps = banks[2 + ti % 2][:, 0:n]
                proj(ps, wr, 0, g0, n)
                P.act(gz[:, col(g0, s):col(g0, s) + n], ps, AF.Gelu_apprx_tanh)
            if j + 2 < 4:
                pendw.append(load_w(j + 2))
            for d in range(2):
                rb, ib = rbs[d], ibs[d]
                wa_t = wbd[d * 2]
                wx_t = wbd[d * 2 + 1]
                for wt, src in ((wa_t, rnn_wa_d), (wx_t, rnn_wx_d)):
                    P.memset(wt, 0.0)
                    P.dma("gpsimd", wt[0:64, 0:64], src[l, d, 2 * j])
                    P.dma("gpsimd", wt[64:128, 64:128], src[l, d, 2 * j + 1])
                for tp in range(4):
                    sh = (tp - 3) if d == 0 else (3 - tp)
                    src = xbuf[:, o0 + sh:o1 + sh]
                    wcol = V("rnn_conv_w", l, d, tp, j)
                    if tp == 0:
                        P.act(xc[:, o0:o1], src, AF.Identity, bias=V("rnn_conv_b", l, d, j), scale=wcol)
                    else:
                        P.stt(xc[:, o0:o1], src, wcol, xc[:, o0:o1], ALU.mult, ALU.add)
                P.act(xcb[:, o0:o1], xc[:, o0:o1], AF.Identity)
                for ti, (g0, n, s) in enumerate(ALL_TILES):
                    c0 = col(g0, s)
                    psr = banks[4 + ti % 2][:, 0:n]
                    psi = banks[6 + ti % 2][:, 0:n]
                    P.mm(psr, wa_t, xcb[:, c0:c0 + n])
                    P.mm(psi, wx_t, xcb[:, c0:c0 + n])
                    P.act(rb[:, c0:c0 + n], psr, AF.Sigmoid, bias=V("rnn_ba", l, d, j))
                    P.act(ib[:, c0:c0 + n], psi, AF.Sigmoid, bias=V("rnn_bx", l, d, j))
                ci = (l * 2 + d) * 4 + j
                P.act(rb[:, o0:o1], rb[:, o0:o1], AF.Exp, scale=clv[:, ci:ci + 1])
                P.tt(ib[:, o0:o1], ib[:, o0:o1], xc[:, o0:o1], ALU.mult)
                P.tt(xc[:, o0:o1], rb[:, o0:o1], rb[:, o0:o1], ALU.mult)
                P.act(xc[:, o0:o1], xc[:, o0:o1], AF.Sqrt, bias=1.0, scale=-1.0)
                P.tt(ib[:, o0:o1], ib[:, o0:o1], xc[:, o0:o1], ALU.mult)
            rb, ib = rbs[0], ibs[0]
            P.scan(hf[:, CTX0:CTX0 + NCX], rb[:, CTX0:CTX0 + NCX], ib[:, CTX0:CTX0 + NCX], 0.0)
            P.scan(hf[:, LAT0:LAT0 + NL], rb[:, LAT0:LAT0 + NL], ib[:, LAT0:LAT0 + NL],
                   hf[:, CTX0 + NCX - 1:CTX0 + NCX])
            rb, ib = rbs[1], ibs[1]
            P.scan(xc[:, CTX0:CTX0 + NCX][:, ::-1], rb[:, CTX0:CTX0 + NCX][:, ::-1],
                   ib[:, CTX0:CTX0 + NCX][:, ::-1], 0.0)
            P.scan(xc[:, LAT0:LAT0 + NL][:, ::-1], rb[:, LAT0:LAT0 + NL][:, ::-1],
                   ib[:, LAT0:LAT0 + NL][:, ::-1], xc[:, CTX0:CTX0 + 1])
            segs = [(LAT0, NL, 0)] + ([(CTX0, NCX, NL)] if need_ctx else [])
            for (c0, n, g0) in segs:
                P.tt(hf[:, c0:c0 + n], hf[:, c0:c0 + n], xc[:, c0:c0 + n], ALU.add)
                P.tt(yb[:, c0:c0 + n], hf[:, c0:c0 + n], gz[:, c0:c0 + n], ALU.mult)
                P.dma("sync", rnn_s[j][:, g0:g0 + n], yb[:, c0:c0 + n])
        AR.release(m)

    def conv_stage(l, need_ctx):
        m = AR.mark()
        LAT0, CTX0, W = 15, 15 + NL + 30, 15 + NL + 30 + NCX + 15

        def col(g0, s):
            return LAT0 + g0 if s == 0 else CTX0 + (g0 - NL)
        tiles = ALL_TILES if need_ctx else LAT_TILES
        wval = AR.alloc(KC * 512, BF16).rearrange("p (k n) -> p k n", n=512)
        wgat = AR.alloc(KC * 512, BF16).rearrange("p (k n) -> p k n", n=512)
        wv = wview(w_in_d, l)
        wload(wval, wv[:, :, 768:1280])
        wload(wgat, wv[:, :, 1280:1792])
        ubuf = AR.alloc(W, BF16)
        acc = [AR.alloc(W) for _ in range(4)]
        sigr = AR.rot(2, 512)
        dg = [AR.alloc(31 * 128, BF16).rearrange("p (k n) -> p k n", n=128) for _ in range(1)]
        P.memset(ubuf, 0.0)
        for j in range(4):
            dgj = dg[0]
            for tp in range(31):
                P.ts(dgj[:, tp, :], ident[:], V("conv_dw", l, tp, j), None, ALU.mult)
            for ti, (g0, n, s) in enumerate(tiles):
                pv = banks[ti % 2][:, 0:n]
                pg = banks[2 + ti % 2][:, 0:n]
                proj(pg, wgat, j * 128, g0, n)
                proj(pv, wval, j * 128, g0, n)
                sg = sigr()[:, 0:n]
                P.act(sg, pg, AF.Sigmoid)
                P.tt(ubuf[:, col(g0, s):col(g0, s) + n], pv, sg, ALU.mult)
            for ti, (g0, n, s) in enumerate(tiles):
                c0 = col(g0, s)
                pc = banks[4 + ti % 2][:, 0:n]
                for tp in range(31):
                    P.mm(pc, dgj[:, tp, :], ubuf[:, c0 + tp - 15:c0 + tp - 15 + n], start=(tp == 0), stop=(tp == 30))
                P.act(acc[j][:, c0:c0 + n], pc, AF.Identity, bias=V("conv_dw_b", l, j))
        sqr = AR.rot(2, 512)
        meanb = AR.rot(2, 512)
        varb = AR.rot(2, 512)
        tb = AR.rot(2, 512)
        ob = AR.rot(3, 512, BF16)
        for ti, (g0, n, s) in enumerate(tiles):
            c0 = col(g0, s)
            psum_ = banks[(ti % 2) * 2][:, 0:n]
            psq = banks[(ti % 2) * 2 + 1][:, 0:n]
            for j in range(4):
                P.mm(psum_, ones32[:], acc[j][:, c0:c0 + n], start=(j == 0), stop=(j == 3))
            for j in range(4):
                sq = sqr()[:, 0:n]
                P.act(sq, acc[j][:, c0:c0 + n], AF.Square)
                P.mm(psq, ones32[:], sq, start=(j == 0), stop=(j == 3))
            mean = meanb()[:, 0:n]
            var = varb()[:, 0:n]
            P.act(mean, psum_, AF.Identity, scale=1.0 / 512)
            P.tt(var, mean, mean, ALU.mult)
            P.stt(var, psq, 1.0 / 512, var, ALU.mult, ALU.subtract)
            P.act(var, var, AF.Sqrt, bias=EPS, scale=1.0)
            P.recip(var, var)
            for j in range(4):
                t = tb()[:, 0:n]
                P.tt(t, acc[j][:, c0:c0 + n], mean, ALU.subtract)
                P.tt(t, t, var, ALU.mult)
                o = ob()[:, 0:n]
                P.act(o, t, AF.Silu, bias=V("conv_ln_b", l, j), scale=V("conv_ln_g", l, j))
                P.dma("sync", conv_s[j][:, g0:g0 + n], o)
        AR.release(m)

    def attn_stage(l, need_ctx):
        m = AR.mark()
        qT = AR.alloc(4 * NT, BF16).rearrange("p (j t) -> p j t", t=NT)
        kTp = [AR.alloc(NT, BF16) for _ in range(2)]
        Vs = AR.alloc(18 * 256, BF16).rearrange("p (t h d) -> p t h d", h=2, d=128)
        P.memset(kTp[0][64:128, :], 0.0)
        P.memset(kTp[1][0:64, :], 0.0)
        for kv_ in range(2):
            P.memset(Vs[:, :, kv_, 64:128], 1.0)
        m1 = AR.mark()
        wq = AR.alloc(KC * 512, BF16).rearrange("p (k n) -> p k n", n=512)
        wkv = AR.alloc(KC * 256, BF16).rearrange("p (k n) -> p k n", n=256)
        Ct = AR.alloc(NL)
        St = AR.alloc(NL)
        wv = wview(w_in_d, l)
        for j in range(4):
            for half in range(2):
                hd = half * 4 + j
                wload(wq[:, :, j * 128 + half * 64:j * 128 + half * 64 + 64], wv[:, :, hd * 64:(hd + 1) * 64])
        wload(wkv, wv[:, :, 512:768])
        P.dma("sync", Ct, rope_d[0])
        P.dma("sync", St, rope_d[1])
        sqr = AR.rot(2, 512, BF16)
        stdb = AR.rot(2, 512)
        qnb = AR.rot(2, 512)
        t1b = AR.rot(2, 512)
        t2b = AR.rot(2, 512)
        qtiles = ALL_TILES if need_ctx else LAT_TILES
        jobs = [(j, t, "q") for j in range(4) for t in qtiles] + [(0, t, "k") for t in ALL_TILES]
        for ji, (j, (g0, n, s), kind) in enumerate(jobs):
            ps = banks[ji % 2][:, 0:n]
            if kind == "q":
                proj(ps, wq, j * 128, g0, n)
                gain = V("q_norm", l, 0)
                dst = qT[:, j, g0:g0 + n]
            else:
                proj(ps, wkv, 0, g0, n)
                gain = V("k_norm", l, 0)
                dst = None
            sq = sqr()[:, 0:n]
            P.act(sq, ps, AF.Square)
            ps2 = banks[2 + ji % 2][:, 0:n]
            P.mm(ps2, bo64[:], sq)
            std = stdb()[:, 0:n]
            P.act(std, ps2, AF.Sqrt, bias=EPS, scale=1.0 / 64)
            P.recip(std, std)
            qn = qnb()[:, 0:n]
            P.stt(qn, ps, gain, std, ALU.mult, ALU.mult)
            if s == 0:
                ps3 = banks[4 + ji % 2][:, 0:n]
                P.mm(ps3, rotm[:], qn)
                t1 = t1b()[:, 0:n]
                t2 = t2b()[:, 0:n]
                P.tt(t1, qn, Ct[:, g0:g0 + n], ALU.mult)
                P.tt(t2, ps3, St[:, g0:g0 + n], ALU.mult)
                if dst is not None:
                    P.tt(dst, t1, t2, ALU.add)
                else:
                    P.tt(kTp[0][0:64, g0:g0 + n], t1[0:64, :], t2[0:64, :], ALU.add)
                    P.tt(kTp[1][64:128, g0:g0 + n], t1[64:128, :], t2[64:128, :], ALU.add)
            else:
                if dst is not None:
                    P.act(dst, qn, AF.Identity)
                else:
                    P.act(kTp[0][0:64, g0:g0 + n], qn[0:64, :], AF.Identity)
                    P.act(kTp[1][64:128, g0:g0 + n], qn[64:128, :], AF.Identity)
        for tt_ in range(18):
            ps = banks[6 + tt_ % 2][:, 0:128]
            for k in range(KC):
                P.mm(ps, hT[:, k, tt_ * 128:(tt_ + 1) * 128], wkv[:, k, 128:256], start=(k == 0), stop=(k == KC - 1))
            P.act(Vs[:, tt_, :, 0:64], ps.rearrange("p (a b) -> p a b", b=64), AF.Identity)
        AR.release(m1)
        aT = AR.alloc(4 * NT, BF16).rearrange("p (j t) -> p j t", t=NT)
        Er = AR.rot(3, 1024, BF16)
        rdb = AR.rot(2, 512)
        units = []
        for h in range(8):
            qsets = [(g0, n, list(range(18))) for (g0, n, s) in LAT_TILES]
            if need_ctx:
                qsets.append((NL, NCX, [16, 17]))
            for qi, (g0, n, kts) in enumerate(qsets):
                prs = [kts[i:i + 2] for i in range(0, len(kts), 2)]
                for pi, pr in enumerate(prs):
                    units.append((h, g0, n, pr, pi == 0, pi == len(prs) - 1))
        state = {"pso_i": -1}

        def emit_S(ui):
            h, g0, n, pr, first, last = units[ui]
            j, half = h % 4, h // 4
            pss = pbig[ui % 2].rearrange("p (a b) -> p a b", b=512)
            for a_, kt in enumerate(pr):
                P.mm(pss[:, a_, 0:n], kTp[half][:, kt * 128:(kt + 1) * 128], qT[:, j, g0:g0 + n])
            E = Er().rearrange("p (a b) -> p a b", b=512)
            P.act(E[:, 0:len(pr), 0:n], pss[:, 0:len(pr), 0:n], AF.Exp, scale=0.125)
            return E

        def emit_PV(ui, E):
            h, g0, n, pr, first, last = units[ui]
            j, half = h % 4, h // 4
            po = half * 64
            if first:
                state["pso_i"] += 1
            pso = banks[4 + state["pso_i"] % 3][:, 0:n]
            for a_, kt in enumerate(pr):
                P.mm(pso, Vs[:, kt, half, :], E[:, a_, 0:n], start=(first and a_ == 0), stop=(last and a_ == len(pr) - 1))
            if last:
                rd = rdb()
                P.recip(rd[0:64, 0:n], pso[64:128, :])
                P.tt(aT[po:po + 64, j, g0:g0 + n], pso[0:64, :], rd[0:64, 0:n], ALU.mult)
        Eq = [emit_S(0)]
        for ui in range(len(units)):
            if ui + 1 < len(units):
                Eq.append(emit_S(ui + 1))
            emit_PV(ui, Eq.pop(0))
        nn = NT if need_ctx else NL
        for j in range(4):
            P.dma("sync", attn_s[j][:, 0:nn], aT[:, j, 0:nn])
        AR.release(m)

    def merge_stage(l, need_ctx):
        m = AR.mark()
        tiles = ALL_TILES if need_ctx else LAT_TILES
        mT = AR.alloc(KC * NT, BF16).rearrange("p (k t) -> p k t", t=NT)
        m1 = AR.mark()
        wbr = [[AR.alloc(4 * 128, BF16).rearrange("p (k n) -> p k n", n=128) for _ in range(3)] for _ in range(2)]
        wgt = [[AR.alloc(KC * 128, BF16).rearrange("p (k n) -> p k n", n=128) for _ in range(3)] for _ in range(2)]
        brr = [AR.rot(2, 4 * 512, BF16) for _ in range(3)]
        gsb = [AR.alloc(512) for _ in range(3)]
        mt = AR.rot(2, 512)
        wv = wview(w_in_d, l)
        wao = w_ao_d[l].rearrange("(h j d) n -> h d j n", h=2, j=4)
        wco = wview(w_co_d, l)
        wro = wview(w_ro_d, l)

        def load_c(c):
            wb, wg = wbr[c % 2], wgt[c % 2]
            for half in range(2):
                wload(wb[0][half * 64:(half + 1) * 64, :, :], wao[half][:, :, c * 128:(c + 1) * 128])
            wload(wb[1], wco[:, :, c * 128:(c + 1) * 128])
            wload(wb[2], wro[:, :, c * 128:(c + 1) * 128])
            for br in range(3):
                c0 = 2816 + br * 1024 + c * 128
                wload(wg[br], wv[:, :, c0:c0 + 128])
        if not dbg.get("skip_merge"):
            load_c(0)
            load_c(1)
        scr = (attn_s, conv_s, rnn_s)
        for c in (range(KC) if not dbg.get("skip_merge") else []):
            wb, wg = wbr[c % 2], wgt[c % 2]
            for ti, (g0, n, s) in enumerate(tiles):
                brt = []
                for br in range(3):
                    bt = brr[br]().rearrange("p (j t) -> p j t", t=512)
                    P.dma("sync", bt[:, :, 0:n], scr[br].rearrange("j p t -> p j t")[:, :, g0:g0 + n])
                    brt.append(bt)
                pb = [banks[br][:, 0:n] for br in range(3)]
                pg = [banks[3 + br][:, 0:n] for br in range(3)]
                for br in range(3):
                    for k in range(KC):
                        P.mm(pg[br], wg[br][:, k, :], hT[:, k, g0:g0 + n], start=(k == 0), stop=(k == KC - 1))
                    for k in range(4):
                        P.mm(pb[br], wb[br][:, k, :], brt[br][:, k, 0:n], start=(k == 0), stop=(k == 3))
                for br in range(3):
                    P.act(gsb[br][:, 0:n], pg[br], AF.Sigmoid)
                ma = mt()[:, 0:n]
                mb = mt()[:, 0:n]
                P.tt(ma, pb[0], gsb[0][:, 0:n], ALU.mult)
                P.tt(mb, pb[1], gsb[1][:, 0:n], ALU.mult)
                P.tt(ma, ma, mb, ALU.add)
                P.tt(mb, pb[2], gsb[2][:, 0:n], ALU.mult)
                P.tt(mT[:, c, g0:g0 + n], ma, mb, ALU.add)
            if c + 2 < KC:
                load_c(c + 2)
        AR.release(m1)
        wo = AR.alloc(KC * D, BF16).rearrange("p (k n) -> p k n", n=D)
        wov = wview(w_out_d, l)
        wload(wo[:, :, 0:512], wov[:, :, 0:512])
        wload(wo[:, :, 512:1024], wov[:, :, 512:1024])
        sqr = AR.rot(3, 512, BF16)
        stdb = AR.rot(2, 512)
        tb = AR.rot(2, 512)
        ob = AR.alloc(KC * 512).rearrange("p (k t) -> p k t", t=512)
        t512 = LAT_TILES + ([CTX_TILE] if need_ctx else [])
        for ti, (g0, n, s) in enumerate(t512 if not (dbg.get("skip_out") or dbg.get("o_nomm")) else []):
            pss = banks[6 + ti % 2][:, 0:n]
            for c in range(KC):
                pso = banks[c % 6][:, 0:n]
                for k in range(KC):
                    P.mm(pso, wo[:, k, c * 128:(c + 1) * 128], mT[:, k, g0:g0 + n], start=(k == 0), stop=(k == KC - 1))
                if dbg.get("o_mmonly"):
                    continue
                sq = sqr()[:, 0:n]
                P.act(sq, pso, AF.Square)
                P.act(ob[:, c, 0:n], pso, AF.Identity)
                if not dbg.get("o_noss"):
                    P.mm(pss, ones_bf[:], sq, start=(c == 0), stop=(c == KC - 1))
            if dbg.get("o_noss"):
                continue
            std = stdb()[:, 0:n]
            P.act(std, pss, AF.Sqrt, bias=EPS, scale=1.0 / D)
            P.recip(std, std)
            if dbg.get("o_noupd"):
                continue
            for c in range(KC):
                t = tb()[:, 0:n]
                P.tt(t, ob[:, c, 0:n], std, ALU.mult)
                P.stt(xT[:, c, g0:g0 + n], t, G1[l][:, c, s:s + 1], xT[:, c, g0:g0 + n], ALU.mult, ALU.add)
        AR.release(m)

    def ffn_stage(l, need_ctx):
        m = AR.mark()
        ftiles = [[(0, 768, 0)], [(768, 1536, 0)],
                  [(1536, 2048, 0)] + ([(NL, NT, 1)] if need_ctx else [])]
        aT = AR.alloc(24 * 768, BF16).rearrange("p (k t) -> p k t", t=768)
        wgr = AR.rot(3, KC * 128, BF16)
        wvr = AR.rot(3, KC * 128, BF16)
        wdr = AR.rot(2, 24 * 128, BF16)
        UW = 776
        m2 = AR.mark()
        ug = AR.rot(2, UW)
        uv = AR.rot(2, UW)
        ag = AR.rot(2, 768)
        av = AR.rot(2, 768)
        AR.release(m2)
        obf = AR.alloc(KC * 768, BF16).rearrange("p (k t) -> p k t", t=768)
        sqr = AR.rot(3, 512, BF16)
        stdb = AR.alloc(768)
        tb = AR.rot(2, 768)
        AR.top = max(AR.top, m2 + 2 * (2 * UW + 2 * 768))
        upv = wview(ffn_up_d, l)
        dnv = wview(ffn_down_d, l)
        for segs in ftiles:
            lay = []
            ucol = 0
            bcol = 0
            for (g0, g1, s) in segs:
                slo, shi = (0, NL) if s == 0 else (NL, NT)
                hl = 1 if g0 > slo else 0
                hr = 1 if g1 < shi else 0
                lay.append((g0, g1, s, hl, hr, ucol, bcol))
                ucol += (g1 - g0) + 2
                bcol += g1 - g0
            ntok = bcol

            def load_pair(p):
                a = wgr().rearrange("p (k n) -> p k n", n=128)
                b = wvr().rearrange("p (k n) -> p k n", n=128)
                wload(a, upv[:, :, p * 128:(p + 1) * 128])
                wload(b, upv[:, :, 3072 + p * 128:3072 + (p + 1) * 128])
                return a, b
            pend = [load_pair(0), load_pair(1), load_pair(2)]
            bk = [0]
            tail = None
            for p in range(24):
                wg_, wv_ = pend.pop(0)
                ugb, uvb, agb, avb = ug(), uv(), ag(), av()
                for (wsb, ub, ab, fch) in ((wg_, ugb, agb, p), (wv_, uvb, avb, 24 + p)):
                    for (g0, g1, s, hl, hr, uc, bc) in lay:
                        n = g1 - g0
                        if not hl:
                            P.memset(ub[:, uc:uc + 1], 0.0)
                        if not hr:
                            P.memset(ub[:, uc + n + 1:uc + n + 2], 0.0)
                        r0, r1 = g0 - hl, g1 + hr
                        dc = uc + 1 - hl
                        while r0 < r1:
                            pn = min(512, r1 - r0)
                            ps = banks[bk[0] % 6][:, 0:pn]
                            bk[0] += 1
                            for k in range(KC):
                                P.mm(ps, wsb[:, k, :], hT[:, k, r0:r0 + pn], start=(k == 0), stop=(k == KC - 1))
                            P.act(ub[:, dc:dc + pn], ps, AF.Copy)
                            r0 += pn
                            dc += pn
                        for tp in range(3):
                            src = ub[:, uc + tp:uc + tp + n]
                            wcol = V("ffn_dw", l, tp, fch)
                            if tp == 0:
                                P.act(ab[:, bc:bc + n], src, AF.Copy, scale=wcol)
                            else:
                                P.stt(ab[:, bc:bc + n], src, wcol, ab[:, bc:bc + n], ALU.mult, ALU.add)
                if tail is not None:
                    tail()

                def tail(agb=agb, avb=avb, p=p):
                    P.act(agb[:, 0:ntok], agb[:, 0:ntok], AF.Gelu_apprx_tanh, bias=V("ffn_dw_b", l, p))
                    P.stt(aT[:, p, 0:ntok], avb[:, 0:ntok], V("ffn_dw_b", l, 24 + p), agb[:, 0:ntok], ALU.add, ALU.mult)
                if p + 3 < 24:
                    pend.append(load_pair(p + 3))
            tail()
            pieces = []
            r0 = 0
            while r0 < ntok:
                pn = min(512, ntok - r0)
                pieces.append((r0, pn))
                r0 += pn

            def load_d(c):
                wd = wdr().rearrange("p (k n) -> p k n", n=128)
                wload(wd[:, 0:12, :], dnv[:, 0:12, c * 128:(c + 1) * 128])
                wload(wd[:, 12:24, :], dnv[:, 12:24, c * 128:(c + 1) * 128])
                return wd
            pendd = [load_d(0), load_d(1)]
            ssq = []
            for c in range(KC):
                wd = pendd.pop(0)
                for pi, (r0, pn) in enumerate(pieces):
                    ps = banks[(c * 2 + pi) % 6][:, 0:pn]
                    for k in range(24):
                        P.mm(ps, wd[:, k, :], aT[:, k, r0:r0 + pn], start=(k == 0), stop=(k == 23))
                    P.act(obf[:, c, r0:r0 + pn], ps, AF.Copy)
                    sq = sqr()[:, 0:pn]
                    P.act(sq, ps, AF.Square)
                    if ssq:
                        ssq.pop(0)()
                    ssq.append(lambda sq=sq, pi=pi, pn=pn, c=c: P.mm(banks[6 + pi][:, 0:pn], ones_bf[:], sq,
                                                                   start=(c == 0), stop=(c == KC - 1)))
                if c + 2 < KC:
                    pendd.append(load_d(c + 2))
            while ssq:
                ssq.pop(0)()
            for pi, (r0, pn) in enumerate(pieces):
                P.act(stdb[:, r0:r0 + pn], banks[6 + pi][:, 0:pn], AF.Sqrt, bias=EPS, scale=1.0 / D)
            P.recip(stdb[:, 0:ntok], stdb[:, 0:ntok])
            for c in range(KC):
                t = tb()
                P.tt(t[:, 0:ntok], obf[:, c, 0:ntok], stdb[:, 0:ntok], ALU.mult)
                for (g0, g1, s, hl, hr, uc, bc) in lay:
                    n = g1 - g0
                    P.stt(xT[:, c, g0:g1], t[:, bc:bc + n], G2[l][:, c, s:s + 1], xT[:, c, g0:g1], ALU.mult, ALU.add)
        AR.release(m)

    finals = []

    def on(name):
        return stages is None or name in stages
    for l in range(nlayers):
        need_ctx = l < L - 1
        if on("N1"):
            norm_stage(l, A1[l], 0, ALL_TILES)
        if l == 0 and dbg.get("dump_h1"):
            hd = nc.dram_tensor("hT_dump", [128, KC, NT], BF16, kind="ExternalOutput").ap()
            for c in range(KC):
                finals.append(P.dma("sync", hd[:, c, :], hT[:, c, :]))
            md = nc.dram_tensor("modv_dump", [128, 96], F32, kind="ExternalOutput").ap()
            finals.append(P.dma("sync", md, modv[0][:].rearrange("p a b -> p (a b)")))
        if on("R"):
            rnn_stage(l, need_ctx)
        if on("C"):
            conv_stage(l, need_ctx)
        if on("Q"):
            attn_stage(l, need_ctx)
        if on("M"):
            merge_stage(l, need_ctx)
        if l == 0 and dbg.get("dump_x1"):
            xd = nc.dram_tensor("x1_dump", [128, KC, NT], F32, kind="ExternalOutput").ap()
            for c in range(KC):
                finals.append(P.dma("sync", xd[:, c, :], xT[:, c, :]))
        if on("N2"):
            norm_stage(l, A2[l], 24, ALL_TILES if need_ctx else LAT_TILES)
        if on("F"):
            ffn_stage(l, need_ctx)

    for c in range(KC):
        finals.append(P.dma("sync", yT_d[:, c, :], xT[:, c, 0:NL]))
    P.emit(final_waits=finals)
    st.close()
    return nc, P


def rope_tables():
    rows = NL // 64
    row = np.repeat(np.arange(rows), 64).astype(np.float32)
    colv = np.tile(np.arange(64), rows).astype(np.float32)
    n_freq = 16
    freq = (np.float32(10000.0) ** (-np.arange(n_freq, dtype=np.float32) / np.float32(n_freq))).astype(np.float32)
    ang = np.concatenate([row[:, None] * freq, colv[:, None] * freq], axis=-1).astype(np.float32)
    cos = np.cos(ang).astype(np.float32).T
    sin = np.sin(ang).astype(np.float32).T
    C = np.concatenate([cos, cos, cos, cos], axis=0)
    S = np.concatenate([sin, sin, sin, sin], axis=0)
    rot = np.zeros((128, 128), np.float32)
    for hb in (0, 64):
        for mI in range(32):
            rot[hb + mI + 32, hb + mI] = -1.0
            rot[hb + mI, hb + mI + 32] = 1.0
    return np.ascontiguousarray(np.stack([C, S], 0)), np.ascontiguousarray(np.stack([rot, np.eye(128, dtype=np.float32)], 0))


_CACHE = {}


def kernel(**inputs):
    if "nc" not in _CACHE:
        _CACHE["nc"] = build_program()
    nc, _ = _CACHE["nc"]
    f = lambda k: np.ascontiguousarray(np.asarray(inputs[k], np.float32))
    x, ctx, c, c_ctx = f("x"), f("ctx"), f("c"), f("c_ctx")
    vecs = pack_vecs(inputs)
    rope, rot = rope_tables()
    shared = {"vecs": vecs, "rope": rope, "rotm": rot}
    for k in ("w_mod", "w_in", "w_attn_out", "w_conv_out", "w_rnn_out", "w_out", "ffn_up", "ffn_down", "rnn_wa", "rnn_wx"):
        shared[k] = f(k)
    in_maps = []
    B = x.shape[0]
    for b in range(B):
        xa = np.concatenate([x[b], ctx[b]], axis=0)
        xTb = np.ascontiguousarray(xa.T.reshape(KC, 128, NT).transpose(1, 0, 2))
        cc = np.stack([c[b], c_ctx], axis=-1)
        cTb = np.ascontiguousarray(cc.reshape(KC, 128, 2).transpose(1, 0, 2).reshape(128, KC * 2))
        d = dict(shared)
        d["xT"] = xTb
        d["cT"] = cTb
        in_maps.append(d)
    res = run_bass_kernel_spmd(nc, in_maps, core_ids=list(range(B)))
    out = np.empty((B, NL, D), np.float32)
    for b in range(B):
        yT = np.asarray(res.results[b]["yT"])
        out[b] = yT.transpose(2, 1, 0).reshape(NL, D)
    return out
```

```python
import contextlib
import numpy as np
import concourse.bass as bass
import concourse.mybir as mybir
from concourse.bass_utils import run_bass_kernel_spmd

F32 = mybir.dt.float32
BF16 = mybir.dt.bfloat16
AF = mybir.ActivationFunctionType
ALU = mybir.AluOpType
ESZ = {F32: 4, BF16: 2}

ENGS = ("sync", "scalar", "vector", "gpsimd", "tensor")

L = 2
D = 1024
NL = 2048
NCX = 256
NT = NL + NCX
EPS = 1e-6
KC = 8


def _esz(ap):
    try:
        return ESZ[ap.dtype]
    except Exception:
        return 4


def _box(ap):
    name = ap.tensor.name
    dims = [(int(s), int(c)) for s, c in ap.ap]
    off = int(ap.offset)
    es = _esz(ap)
    space = str(ap.space).upper()
    if "SB" not in space and "PSUM" not in space:
        lo = off + sum(min(0, s * (c - 1)) for s, c in dims)
        hi = off + sum(max(0, s * (c - 1)) for s, c in dims) + 1
        return (name, 0, 1, lo * es, hi * es)
    pstep, pcnt = dims[0]
    if pstep <= 0:
        p0 = 0
        foff = off
    else:
        p0 = off // pstep
        foff = off - p0 * pstep
    fd = dims[1:]
    lo = foff + sum(min(0, s * (c - 1)) for s, c in fd)
    hi = foff + sum(max(0, s * (c - 1)) for s, c in fd) + 1
    return (name, p0, p0 + pcnt, lo * es, hi * es)


class Op:
    __slots__ = ("eng", "fn", "deps", "tok", "needs_inc", "is_dma", "slot", "val")

    def __init__(self, eng, fn):
        self.eng = eng
        self.fn = fn
        self.deps = set()
        self.tok = None
        self.needs_inc = False
        self.is_dma = False
        self.slot = None
        self.val = 0


class Prog:
    def __init__(self, nc, n_dma_slots=16):
        self.nc = nc
        self.ops = []
        self.recs = {}
        self.n_dma_slots = n_dma_slots
        self.dma_count = {}
        self.slot_last = {}

    def _track(self, op, reads, writes):
        for ap in reads:
            b = _box(ap)
            lst = self.recs.get(b[0], [])
            keep = []
            for r in lst:
                ov = r[0] < b[2] and b[1] < r[1] and r[2] < b[4] and b[3] < r[3]
                if ov and r[5] and r[4] is not op:
                    op.deps.add(r[4])
                if (not r[5]) and (not r[4].is_dma) and (not op.is_dma) and r[4].eng == op.eng \
                        and b[1] <= r[0] and r[1] <= b[2] and b[3] <= r[2] and r[3] <= b[4]:
                    continue
                keep.append(r)
            keep.append([b[1], b[2], b[3], b[4], op, False])
            self.recs[b[0]] = keep
        for ap in writes:
            b = _box(ap)
            lst = self.recs.get(b[0], [])
            keep = []
            for r in lst:
                ov = r[0] < b[2] and b[1] < r[1] and r[2] < b[4] and b[3] < r[3]
                if ov and r[4] is not op:
                    op.deps.add(r[4])
                cov = b[1] <= r[0] and r[1] <= b[2] and b[3] <= r[2] and r[3] <= b[4]
                if cov and r[4] is not op:
                    continue
                keep.append(r)
            keep.append([b[1], b[2], b[3], b[4], op, True])
            self.recs[b[0]] = keep

    def add(self, eng, fn, reads=(), writes=()):
        op = Op(eng, fn)
        self._track(op, reads, writes)
        self.ops.append(op)
        return op

    def dma(self, eng, out, in_, **kw):
        op = Op(eng, None)
        op.is_dma = True
        i = self.dma_count.get(eng, 0)
        self.dma_count[eng] = i + 1
        op.slot = "%s%d" % (eng[0], i % (self.n_dma_slots if eng == "sync" else 4))
        prev = self.slot_last.get(op.slot)
        op.val = (prev.val if prev is not None else 0) + 16
        if prev is not None:
            op.deps.add(prev)
        self.slot_last[op.slot] = op
        op.fn = lambda e: e.dma_start(out=out, in_=in_, **kw)
        self._track(op, [in_], [out])
        self.ops.append(op)
        return op

    def mm(self, out, lhsT, rhs, start=True, stop=True):
        return self.add("tensor", lambda e: e.matmul(out, lhsT, rhs, start=start, stop=stop),
                        [lhsT, rhs] + ([] if start else [out]), [out])

    def act(self, out, in_, func, bias=None, scale=None):
        kw = {}
        rd = [in_]
        if bias is not None:
            kw["bias"] = bias
            if not isinstance(bias, (int, float)):
                rd.append(bias)
        if scale is not None:
            kw["scale"] = scale
            if not isinstance(scale, (int, float)):
                rd.append(scale)
        return self.add("scalar", lambda e: e.activation(out=out, in_=in_, func=func, **kw), rd, [out])

    def tt(self, out, in0, in1, op, eng="vector"):
        return self.add(eng, lambda e: e.tensor_tensor(out=out, in0=in0, in1=in1, op=op), [in0, in1], [out])

    def ts(self, out, in0, s1, s2, op0, op1=None, eng="vector"):
        rd = [in0] + [s for s in (s1, s2) if s is not None and not isinstance(s, (int, float))]
        if op1 is None:
            return self.add(eng, lambda e: e.tensor_scalar(out=out, in0=in0, scalar1=s1, scalar2=None, op0=op0), rd, [out])
        return self.add(eng, lambda e: e.tensor_scalar(out=out, in0=in0, scalar1=s1, scalar2=s2, op0=op0, op1=op1), rd, [out])

    def stt(self, out, in0, scalar, in1, op0, op1):
        rd = [in0, in1] + ([] if isinstance(scalar, (int, float)) else [scalar])
        return self.add("vector", lambda e: e.scalar_tensor_tensor(out=out, in0=in0, scalar=scalar, in1=in1, op0=op0, op1=op1), rd, [out])

    def copy(self, out, in_, eng="vector"):
        return self.add(eng, lambda e: e.tensor_copy(out=out, in_=in_), [in_], [out])

    def memset(self, ap, val, eng="vector"):
        return self.add(eng, lambda e: e.memset(ap, val), [], [ap])

    def recip(self, out, in_):
        return self.add("vector", lambda e: e.reciprocal(out=out, in_=in_), [in_], [out])

    def scan(self, out, d0, d1, init):
        rd = [d0, d1] + ([] if isinstance(init, (int, float)) else [init])
        return self.add("vector", lambda e: e.tensor_tensor_scan(out=out, data0=d0, data1=d1, initial=init,
                                                                 op0=ALU.mult, op1=ALU.add), rd, [out])

    def emit(self, final_waits=()):
        nc = self.nc
        seq = {e: 0 for e in ENGS}
        for op in self.ops:
            for d in op.deps:
                d.needs_inc = True
        for op in final_waits:
            op.needs_inc = True
        for op in self.ops:
            if op.is_dma:
                op.tok = (op.slot, op.val)
            elif op.needs_inc:
                seq[op.eng] += 1
                op.tok = (op.eng, seq[op.eng])
        clock = {e: {} for e in ENGS}
        opclock = {}
        per_eng = {e: [] for e in ENGS}
        slots = set()
        for op in self.ops:
            ck = clock[op.eng]
            wm = {}
            for d in sorted(op.deps, key=lambda o: (o.tok[0], o.tok[1])):
                src, v = d.tok
                if src == "tensor" and op.eng == "tensor":
                    continue
                if ck.get(src, 0) >= v:
                    continue
                wm[src] = max(wm.get(src, 0), v)
                oc = opclock.get(id(d))
                if oc:
                    for k, vv in oc.items():
                        if ck.get(k, 0) < vv:
                            ck[k] = vv
                ck[src] = max(ck.get(src, 0), v)
            if op.tok is not None:
                oc = dict(ck)
                oc[op.tok[0]] = max(oc.get(op.tok[0], 0), op.tok[1])
                opclock[id(op)] = oc
                if op.is_dma:
                    slots.add(op.slot)
            per_eng[op.eng].append((op, list(wm.items())))
        fin = [op.tok for op in final_waits] + [o.tok for o in self.slot_last.values()]
        with contextlib.ExitStack() as st:
            sems = {}
            for e in ENGS:
                sems[e] = st.enter_context(nc.semaphore("s_" + e))
            for s in sorted(slots):
                sems[s] = st.enter_context(nc.semaphore("s_" + s))
            block = st.enter_context(nc.Block())

            def run(engname):
                def body(e):
                    for op, waits in per_eng[engname]:
                        for s, v in waits:
                            e.wait_ge(sems[s], v)
                        ins = op.fn(e)
                        if op.is_dma:
                            ins.then_inc(sems[op.tok[0]], 16)
                        elif op.needs_inc:
                            ins.then_inc(sems[op.eng], 1)
                    if engname == "sync":
                        for s, v in fin:
                            e.wait_ge(sems[s], v)
                return body

            block.sync(run("sync"))
            block.scalar(run("scalar"))
            block.vector(run("vector"))
            block.gpsimd(run("gpsimd"))
            block.tensor(run("tensor"))
        self.stats = {e: len(per_eng[e]) for e in ENGS}


VEC_SPEC = [
    ("b_mod", (L,), 6144), ("norm_pre_mix", (L,), 1024), ("norm_post_mix", (L,), 1024),
    ("norm_pre_ffn", (L,), 1024), ("norm_post_ffn", (L,), 1024),
    ("q_norm", (L,), 128), ("k_norm", (L,), 128),
    ("conv_dw", (L, 31), 512), ("conv_dw_b", (L,), 512), ("conv_ln_g", (L,), 512), ("conv_ln_b", (L,), 512),
    ("rnn_conv_w", (L, 2, 4), 512), ("rnn_conv_b", (L, 2), 512), ("rnn_ba", (L, 2), 512),
    ("rnn_bx", (L, 2), 512), ("rnn_lambda", (L, 2), 512),
    ("ffn_dw", (L, 3), 6144), ("ffn_dw_b", (L,), 6144),
]
VEC_OFF = {}
_o = 0
for _n, _lead, _f in VEC_SPEC:
    VEC_OFF[_n] = (_o, _lead, _f // 128)
    _o += int(np.prod(_lead)) * (_f // 128)
NV = _o


def pack_vecs(inputs):
    out = np.zeros((128, NV), np.float32)
    for n, lead, f in VEC_SPEC:
        a = np.asarray(inputs[n], np.float32)
        if n in ("q_norm", "k_norm"):
            a = np.concatenate([a, a], axis=-1)
        a = a.reshape(int(np.prod(lead)), f // 128, 128)
        base = VEC_OFF[n][0]
        out[:, base:base + a.shape[0] * a.shape[1]] = a.reshape(-1, 128).T
    return out


def build_program(dbg=None):
    nc = bass.Bass("TRN2", target_bir_lowering=False)
    dbg = dbg or {}
    stages = dbg.get("stages")
    nlayers = dbg.get("layers", L)
    st = contextlib.ExitStack()

    def din(name, shape, dt=F32):
        return nc.dram_tensor(name, list(shape), dt, kind="ExternalInput").ap()

    xT_d = din("xT", [128, KC, NT])
    cT_d = din("cT", [128, KC * 2])
    vecs_d = din("vecs", [128, NV])
    rope_d = din("rope", [2, 128, NL])
    rotm_d = din("rotm", [2, 128, 128])
    w_mod_d = din("w_mod", [L, D, 6144])
    w_in_d = din("w_in", [L, D, 5888])
    w_ao_d = din("w_attn_out", [L, 512, D])
    w_co_d = din("w_conv_out", [L, 512, D])
    w_ro_d = din("w_rnn_out", [L, 512, D])
    w_out_d = din("w_out", [L, D, D])
    ffn_up_d = din("ffn_up", [L, D, 6144])
    ffn_down_d = din("ffn_down", [L, 3072, D])
    rnn_wa_d = din("rnn_wa", [L, 2, 8, 64, 64])
    rnn_wx_d = din("rnn_wx", [L, 2, 8, 64, 64])
    yT_d = nc.dram_tensor("yT", [128, KC, NL], F32, kind="ExternalOutput").ap()
    skind = "ExternalOutput" if dbg else "Internal"
    attn_s = nc.dram_tensor("attn_s", [4, 128, NT], BF16, kind=skind).ap()
    conv_s = nc.dram_tensor("conv_s", [4, 128, NT], BF16, kind=skind).ap()
    rnn_s = nc.dram_tensor("rnn_s", [4, 128, NT], BF16, kind=skind).ap()

    def sb(name, shape, dt=F32):
        return st.enter_context(nc.sbuf_tensor(name, list(shape), dt))

    P = Prog(nc)

    xT = sb("xTs", [128, KC, NT])
    hT = sb("hTs", [128, KC, NT], BF16)
    vecs = sb("vecs_s", [128, NV])
    modv = [sb("modv%d" % l, [128, 48, 2]) for l in range(L)]
    A1 = [sb("A1_%d" % l, [128, KC, 2]) for l in range(L)]
    G1 = [sb("G1_%d" % l, [128, KC, 2]) for l in range(L)]
    A2 = [sb("A2_%d" % l, [128, KC, 2]) for l in range(L)]
    G2 = [sb("G2_%d" % l, [128, KC, 2]) for l in range(L)]
    clv = sb("clv", [128, L * 2 * 4])
    ones_bf = sb("ones_bf", [128, 128], BF16)
    ones32 = sb("ones32", [128, 128])
    bo64 = sb("bo64", [128, 128], BF16)
    rotm = sb("rotm_s", [128, 128])
    ident = sb("ident_s", [128, 128])
    ct = sb("ct_s", [128, KC * 2])
    scb = sb("scb", [128, KC * 2], BF16)
    ARENA_W = 23296
    arena = sb("arena", [128, ARENA_W])
    pbig = [st.enter_context(nc.psum_tensor("pbig%d" % i, [128, 1024], F32)) for i in range(4)]
    banks = [pbig[i // 2][:, (i % 2) * 512:(i % 2) * 512 + 512] for i in range(8)]

    class Arena:
        def __init__(self):
            self.top = 0

        def mark(self):
            return self.top

        def release(self, m):
            self.top = m

        def alloc(self, n, dt=F32):
            w = n if dt == F32 else (n + 1) // 2
            w = (w + 7) // 8 * 8
            assert self.top + w <= ARENA_W, ("arena overflow", self.top, w)
            v = arena[:, self.top:self.top + w]
            self.top += w
            if dt != F32:
                v = v.bitcast(dt)[:, 0:n]
            return v

        def rot(self, k, n, dt=F32):
            bufs = [self.alloc(n, dt) for _ in range(k)]
            state = [0]

            def nxt():
                b = bufs[state[0] % k]
                state[0] += 1
                return b
            return nxt

    AR = Arena()

    def V(name, *idx):
        base, lead, nch = VEC_OFF[name]
        *li, c = idx
        flat = 0
        for i, d in zip(li, lead):
            flat = flat * d + i
        col = base + flat * nch + c
        return vecs[:, col:col + 1]

    def wview(wd, l):
        return wd[l].rearrange("(k p) n -> p k n", p=128)

    def wload(dst, src):
        P.dma("gpsimd", dst, src)

    P.dma("sync", vecs[:], vecs_d)
    P.dma("sync", ct[:], cT_d)
    P.dma("sync", rotm[:], rotm_d[0])
    P.dma("sync", ident[:], rotm_d[1])
    P.memset(ones_bf[:], 1.0)
    P.memset(ones32[:], 1.0)
    P.memset(bo64[:], 0.0)
    P.memset(bo64[0:64, 0:64], 1.0)
    P.memset(bo64[64:128, 64:128], 1.0)
    for c in range(KC):
        P.dma("sync", xT[:, c, :], xT_d[:, c, :])
    P.act(scb[:], ct[:], AF.Silu)
    lb = VEC_OFF["rnn_lambda"][0]
    P.act(clv[:], vecs[:, lb:lb + L * 8], AF.Exp, scale=-1.0)
    P.act(clv[:], clv[:], AF.Ln, bias=1.0)
    P.ts(clv[:], clv[:], -8.0, None, ALU.mult)

    m0 = AR.mark()
    wrot = AR.rot(3, KC * 512, BF16)
    scb3 = scb[:].rearrange("p (k s) -> p k s", s=2)
    bi = [0]

    def modblock_load(l, blk):
        wb = wrot().rearrange("p (k n) -> p k n", n=512)
        wload(wb, wview(w_mod_d, l)[:, :, blk * 512:(blk + 1) * 512])
        return wb

    jobs = [(l, blk) for l in range(L) for blk in range(12)]
    pend = [modblock_load(*jobs[0]), modblock_load(*jobs[1])]
    for ji, (l, blk) in enumerate(jobs):
        wb = pend.pop(0)
        for j in range(4):
            ch = blk * 4 + j
            ps = banks[bi[0] % 8][:, 0:2]
            bi[0] += 1
            for k in range(KC):
                P.mm(ps, wb[:, k, j * 128:(j + 1) * 128], scb3[:, k, :], start=(k == 0), stop=(k == KC - 1))
            P.ts(modv[l][:, ch, :], ps, V("b_mod", l, ch), None, ALU.add)
        if ji + 2 < len(jobs):
            pend.append(modblock_load(*jobs[ji + 2]))
    for l in range(L):
        for c in range(KC):
            P.ts(A1[l][:, c, :], modv[l][:, 8 + c, :], 1.0, V("norm_pre_mix", l, c), ALU.add, ALU.mult)
            P.ts(G1[l][:, c, :], modv[l][:, 16 + c, :], V("norm_post_mix", l, c), None, ALU.mult)
            P.ts(A2[l][:, c, :], modv[l][:, 32 + c, :], 1.0, V("norm_pre_ffn", l, c), ALU.add, ALU.mult)
            P.ts(G2[l][:, c, :], modv[l][:, 40 + c, :], V("norm_post_ffn", l, c), None, ALU.mult)
    AR.release(m0)

    LAT_TILES = [(i * 512, 512, 0) for i in range(4)]
    CTX_TILE = (NL, NCX, 1)
    ALL_TILES = LAT_TILES + [CTX_TILE]

    def norm_stage(l, A, shbase, tiles):
        m = AR.mark()
        sqr = AR.rot(3, 512, BF16)
        stdb = AR.rot(2, 512)
        rstb = AR.rot(2, 512)
        tb = AR.rot(3, 512)
        for ti, (g0, n, s) in enumerate(tiles):
            ss = banks[ti % 2][:, 0:n]
            for c in range(KC):
                sq = sqr()[:, 0:n]
                P.act(sq, xT[:, c, g0:g0 + n], AF.Square)
                P.mm(ss, ones_bf[:], sq, start=(c == 0), stop=(c == KC - 1))
            std = stdb()[:, 0:n]
            rstd = rstb()[:, 0:n]
            P.act(std, ss, AF.Sqrt, bias=EPS, scale=1.0 / D)
            P.recip(rstd, std)
            for c in range(KC):
                t = tb()[:, 0:n]
                P.tt(t, xT[:, c, g0:g0 + n], rstd, ALU.mult)
                P.act(hT[:, c, g0:g0 + n], t, AF.Identity, bias=modv[l][:, shbase + c, s:s + 1], scale=A[:, c, s:s + 1])
        AR.release(m)

    def proj(ps, wsb, col0, g0, n):
        for k in range(KC):
            P.mm(ps, wsb[:, k, col0:col0 + 128], hT[:, k, g0:g0 + n], start=(k == 0), stop=(k == KC - 1))

    def rnn_stage(l, need_ctx):
        m = AR.mark()
        LAT0, CTX0, W = 4, 4 + NL + 8, 4 + NL + 8 + NCX + 4

        def col(g0, s):
            return LAT0 + g0 if s == 0 else CTX0 + (g0 - NL)
        wxr = AR.rot(2, KC * 128, BF16)
        wrr = AR.rot(2, KC * 128, BF16)
        wv = wview(w_in_d, l)

        def load_w(j):
            a_ = wxr().rearrange("p (k n) -> p k n", n=128)
            b_ = wrr().rearrange("p (k n) -> p k n", n=128)
            wload(a_, wv[:, :, 1792 + j * 128:1792 + (j + 1) * 128])
            wload(b_, wv[:, :, 2304 + j * 128:2304 + (j + 1) * 128])
            return a_, b_
        xbuf = AR.alloc(W)
        xc = AR.alloc(W)
        rbs = [AR.alloc(W), AR.alloc(W)]
        ibs = [AR.alloc(W), AR.alloc(W)]
        hf = AR.alloc(W)
        gz = AR.alloc(W)
        xcb = AR.alloc(W, BF16)
        yb = AR.alloc(W, BF16)
        wbd = [AR.alloc(128, BF16) for _ in range(4)]
        for bufz in [xbuf, xc, hf, gz] + rbs + ibs:
            P.memset(bufz, 0.0)
        P.memset(xcb, 0.0)
        P.memset(yb, 0.0)
        rtiles = ALL_TILES if need_ctx else LAT_TILES
        o0, o1 = LAT0, CTX0 + NCX
        pendw = [load_w(0), load_w(1)]
        for j in range(4):
            wx, wr = pendw.pop(0)
            for ti, (g0, n, s) in enumerate(ALL_TILES):
                ps = banks[ti % 2][:, 0:n]
                proj(ps, wx, 0, g0, n)
                P.act(xbuf[:, col(g0, s):col(g0, s) + n], ps, AF.Identity)
            for ti, (g0, n, s) in enumerate(rtiles):
                ps = banks[2 + ti % 2][:, 0:n]
                proj(ps, wr, 0, g0, n)
                P.act(gz[:, col(g0, s):col(g0, s) + n], ps, AF.Gelu_apprx_tanh)
            if j + 2 < 4:
                pendw.append(load_w(j + 2))
            for d in range(2):
                rb, ib = rbs[d], ibs[d]
                wa_t = wbd[d * 2]
                wx_t = wbd[d * 2 + 1]
                for wt, src in ((wa_t, rnn_wa_d), (wx_t, rnn_wx_d)):
                    P.memset(wt, 0.0)
                    P.dma("gpsimd", wt[0:64, 0:64], src[l, d, 2 * j])
                    P.dma("gpsimd", wt[64:128, 64:128], src[l, d, 2 * j + 1])
                for tp in range(4):
                    sh = (tp - 3) if d == 0 else (3 - tp)
                    src = xbuf[:, o0 + sh:o1 + sh]
                    wcol = V("rnn_conv_w", l, d, tp, j)
                    if tp == 0:
                        P.act(xc[:, o0:o1], src, AF.Identity, bias=V("rnn_conv_b", l, d, j), scale=wcol)
                    else:
                        P.stt(xc[:, o0:o1], src, wcol, xc[:, o0:o1], ALU.mult, ALU.add)
                P.act(xcb[:, o0:o1], xc[:, o0:o1], AF.Identity)
                for ti, (g0, n, s) in enumerate(ALL_TILES):
                    c0 = col(g0, s)
                    psr = banks[4 + ti % 2][:, 0:n]
                    psi = banks[6 + ti % 2][:, 0:n]
                    P.mm(psr, wa_t, xcb[:, c0:c0 + n])
                    P.mm(psi, wx_t, xcb[:, c0:c0 + n])
                    P.act(rb[:, c0:c0 + n], psr, AF.Sigmoid, bias=V("rnn_ba", l, d, j))
                    P.act(ib[:, c0:c0 + n], psi, AF.Sigmoid, bias=V("rnn_bx", l, d, j))
                ci = (l * 2 + d) * 4 + j
                P.act(rb[:, o0:o1], rb[:, o0:o1], AF.Exp, scale=clv[:, ci:ci + 1])
                P.tt(ib[:, o0:o1], ib[:, o0:o1], xc[:, o0:o1], ALU.mult)
                P.tt(xc[:, o0:o1], rb[:, o0:o1], rb[:, o0:o1], ALU.mult)
                P.act(xc[:, o0:o1], xc[:, o0:o1], AF.Sqrt, bias=1.0, scale=-1.0)
                P.tt(ib[:, o0:o1], ib[:, o0:o1], xc[:, o0:o1], ALU.mult)
            rb, ib = rbs[0], ibs[0]
            P.scan(hf[:, CTX0:CTX0 + NCX], rb[:, CTX0:CTX0 + NCX], ib[:, CTX0:CTX0 + NCX], 0.0)
            P.scan(hf[:, LAT0:LAT0 + NL], rb[:, LAT0:LAT0 + NL], ib[:, LAT0:LAT0 + NL],
                   hf[:, CTX0 + NCX - 1:CTX0 + NCX])
            rb, ib = rbs[1], ibs[1]
            P.scan(xc[:, CTX0:CTX0 + NCX][:, ::-1], rb[:, CTX0:CTX0 + NCX][:, ::-1],
                   ib[:, CTX0:CTX0 + NCX][:, ::-1], 0.0)
            P.scan(xc[:, LAT0:LAT0 + NL][:, ::-1], rb[:, LAT0:LAT0 + NL][:, ::-1],
                   ib[:, LAT0:LAT0 + NL][:, ::-1], xc[:, CTX0:CTX0 + 1])
            segs = [(LAT0, NL, 0)] + ([(CTX0, NCX, NL)] if need_ctx else [])
            for (c0, n, g0) in segs:
                P.tt(hf[:, c0:c0 + n], hf[:, c0:c0 + n], xc[:, c0:c0 + n], ALU.add)
                P.tt(yb[:, c0:c0 + n], hf[:, c0:c0 + n], gz[:, c0:c0 + n], ALU.mult)
                P.dma("sync", rnn_s[j][:, g0:g0 + n], yb[:, c0:c0 + n])
        AR.release(m)

    def conv_stage(l, need_ctx):
        m = AR.mark()
        LAT0, CTX0, W = 15, 15 + NL + 30, 15 + NL + 30 + NCX + 15

        def col(g0, s):
            return LAT0 + g0 if s == 0 else CTX0 + (g0 - NL)
        tiles = ALL_TILES if need_ctx else LAT_TILES
        wval = AR.alloc(KC * 512, BF16).rearrange("p (k n) -> p k n", n=512)
        wgat = AR.alloc(KC * 512, BF16).rearrange("p (k n) -> p k n", n=512)
        wv = wview(w_in_d, l)
        wload(wval, wv[:, :, 768:1280])
        wload(wgat, wv[:, :, 1280:1792])
        ubuf = AR.alloc(W, BF16)
        acc = [AR.alloc(W) for _ in range(4)]
        sigr = AR.rot(2, 512)
        dg = [AR.alloc(31 * 128, BF16).rearrange("p (k n) -> p k n", n=128) for _ in range(1)]
        P.memset(ubuf, 0.0)
        for j in range(4):
            dgj = dg[0]
            for tp in range(31):
                P.ts(dgj[:, tp, :], ident[:], V("conv_dw", l, tp, j), None, ALU.mult)
            for ti, (g0, n, s) in enumerate(tiles):
                pv = banks[ti % 2][:, 0:n]
                pg = banks[2 + ti % 2][:, 0:n]
                proj(pg, wgat, j * 128, g0, n)
                proj(pv, wval, j * 128, g0, n)
                sg = sigr()[:, 0:n]
                P.act(sg, pg, AF.Sigmoid)
                P.tt(ubuf[:, col(g0, s):col(g0, s) + n], pv, sg, ALU.mult)
            for ti, (g0, n, s) in enumerate(tiles):
                c0 = col(g0, s)
                pc = banks[4 + ti % 2][:, 0:n]
                for tp in range(31):
                    P.mm(pc, dgj[:, tp, :], ubuf[:, c0 + tp - 15:c0 + tp - 15 + n], start=(tp == 0), stop=(tp == 30))
                P.act(acc[j][:, c0:c0 + n], pc, AF.Identity, bias=V("conv_dw_b", l, j))
        sqr = AR.rot(2, 512)
        meanb = AR.rot(2, 512)
        varb = AR.rot(2, 512)
        tb = AR.rot(2, 512)
        ob = AR.rot(3, 512, BF16)
        for ti, (g0, n, s) in enumerate(tiles):
            c0 = col(g0, s)
            psum_ = banks[(ti % 2) * 2][:, 0:n]
            psq = banks[(ti % 2) * 2 + 1][:, 0:n]
            for j in range(4):
                P.mm(psum_, ones32[:], acc[j][:, c0:c0 + n], start=(j == 0), stop=(j == 3))
            for j in range(4):
                sq = sqr()[:, 0:n]
                P.act(sq, acc[j][:, c0:c0 + n], AF.Square)
                P.mm(psq, ones32[:], sq, start=(j == 0), stop=(j == 3))
            mean = meanb()[:, 0:n]
            var = varb()[:, 0:n]
            P.act(mean, psum_, AF.Identity, scale=1.0 / 512)
            P.tt(var, mean, mean, ALU.mult)
            P.stt(var, psq, 1.0 / 512, var, ALU.mult, ALU.subtract)
            P.act(var, var, AF.Sqrt, bias=EPS, scale=1.0)
            P.recip(var, var)
            for j in range(4):
                t = tb()[:, 0:n]
                P.tt(t, acc[j][:, c0:c0 + n], mean, ALU.subtract)
                P.tt(t, t, var, ALU.mult)
                o = ob()[:, 0:n]
                P.act(o, t, AF.Silu, bias=V("conv_ln_b", l, j), scale=V("conv_ln_g", l, j))
                P.dma("sync", conv_s[j][:, g0:g0 + n], o)
        AR.release(m)

    def attn_stage(l, need_ctx):
        m = AR.mark()
        qT = AR.alloc(4 * NT, BF16).rearrange("p (j t) -> p j t", t=NT)
        kTp = [AR.alloc(NT, BF16) for _ in range(2)]
        Vs = AR.alloc(18 * 256, BF16).rearrange("p (t h d) -> p t h d", h=2, d=128)
        P.memset(kTp[0][64:128, :], 0.0)
        P.memset(kTp[1][0:64, :], 0.0)
        for kv_ in range(2):
            P.memset(Vs[:, :, kv_, 64:128], 1.0)
        m1 = AR.mark()
        wq = AR.alloc(KC * 512, BF16).rearrange("p (k n) -> p k n", n=512)
        wkv = AR.alloc(KC * 256, BF16).rearrange("p (k n) -> p k n", n=256)
        Ct = AR.alloc(NL)
        St = AR.alloc(NL)
        wv = wview(w_in_d, l)
        for j in range(4):
            for half in range(2):
                hd = half * 4 + j
                wload(wq[:, :, j * 128 + half * 64:j * 128 + half * 64 + 64], wv[:, :, hd * 64:(hd + 1) * 64])
        wload(wkv, wv[:, :, 512:768])
        P.dma("sync", Ct, rope_d[0])
        P.dma("sync", St, rope_d[1])
        sqr = AR.rot(2, 512, BF16)
        stdb = AR.rot(2, 512)
        qnb = AR.rot(2, 512)
        t1b = AR.rot(2, 512)
        t2b = AR.rot(2, 512)
        qtiles = ALL_TILES if need_ctx else LAT_TILES
        jobs = [(j, t, "q") for j in range(4) for t in qtiles] + [(0, t, "k") for t in ALL_TILES]
        for ji, (j, (g0, n, s), kind) in enumerate(jobs):
            ps = banks[ji % 2][:, 0:n]
            if kind == "q":
                proj(ps, wq, j * 128, g0, n)
                gain = V("q_norm", l, 0)
                dst = qT[:, j, g0:g0 + n]
            else:
                proj(ps, wkv, 0, g0, n)
                gain = V("k_norm", l, 0)
                dst = None
            sq = sqr()[:, 0:n]
            P.act(sq, ps, AF.Square)
            ps2 = banks[2 + ji % 2][:, 0:n]
            P.mm(ps2, bo64[:], sq)
            std = stdb()[:, 0:n]
            P.act(std, ps2, AF.Sqrt, bias=EPS, scale=1.0 / 64)
            P.recip(std, std)
            qn = qnb()[:, 0:n]
            P.stt(qn, ps, gain, std, ALU.mult, ALU.mult)
            if s == 0:
                ps3 = banks[4 + ji % 2][:, 0:n]
                P.mm(ps3, rotm[:], qn)
                t1 = t1b()[:, 0:n]
                t2 = t2b()[:, 0:n]
                P.tt(t1, qn, Ct[:, g0:g0 + n], ALU.mult)
                P.tt(t2, ps3, St[:, g0:g0 + n], ALU.mult)
                if dst is not None:
                    P.tt(dst, t1, t2, ALU.add)
                else:
                    P.tt(kTp[0][0:64, g0:g0 + n], t1[0:64, :], t2[0:64, :], ALU.add)
                    P.tt(kTp[1][64:128, g0:g0 + n], t1[64:128, :], t2[64:128, :], ALU.add)
            else:
                if dst is not None:
                    P.act(dst, qn, AF.Identity)
                else:
                    P.act(kTp[0][0:64, g0:g0 + n], qn[0:64, :], AF.Identity)
                    P.act(kTp[1][64:128, g0:g0 + n], qn[64:128, :], AF.Identity)
        for tt_ in range(18):
            ps = banks[6 + tt_ % 2][:, 0:128]
            for k in range(KC):
                P.mm(ps, hT[:, k, tt_ * 128:(tt_ + 1) * 128], wkv[:, k, 128:256], start=(k == 0), stop=(k == KC - 1))
            P.act(Vs[:, tt_, :, 0:64], ps.rearrange("p (a b) -> p a b", b=64), AF.Identity)
        AR.release(m1)
        aT = AR.alloc(4 * NT, BF16).rearrange("p (j t) -> p j t", t=NT)
        Er = AR.rot(3, 1024, BF16)
        rdb = AR.rot(2, 512)
        units = []
        for h in range(8):
            qsets = [(g0, n, list(range(18))) for (g0, n, s) in LAT_TILES]
            if need_ctx:
                qsets.append((NL, NCX, [16, 17]))
            for qi, (g0, n, kts) in enumerate(qsets):
                prs = [kts[i:i + 2] for i in range(0, len(kts), 2)]
                for pi, pr in enumerate(prs):
                    units.append((h, g0, n, pr, pi == 0, pi == len(prs) - 1))
        state = {"pso_i": -1}

        def emit_S(ui):
            h, g0, n, pr, first, last = units[ui]
            j, half = h % 4, h // 4
            pss = pbig[ui % 2].rearrange("p (a b) -> p a b", b=512)
            for a_, kt in enumerate(pr):
                P.mm(pss[:, a_, 0:n], kTp[half][:, kt * 128:(kt + 1) * 128], qT[:, j, g0:g0 + n])
            E = Er().rearrange("p (a b) -> p a b", b=512)
            P.act(E[:, 0:len(pr), 0:n], pss[:, 0:len(pr), 0:n], AF.Exp, scale=0.125)
            return E

        def emit_PV(ui, E):
            h, g0, n, pr, first, last = units[ui]
            j, half = h % 4, h // 4
            po = half * 64
            if first:
                state["pso_i"] += 1
            pso = banks[4 + state["pso_i"] % 3][:, 0:n]
            for a_, kt in enumerate(pr):
                P.mm(pso, Vs[:, kt, half, :], E[:, a_, 0:n], start=(first and a_ == 0), stop=(last and a_ == len(pr) - 1))
            if last:
                rd = rdb()
                P.recip(rd[0:64, 0:n], pso[64:128, :])
                P.tt(aT[po:po + 64, j, g0:g0 + n], pso[0:64, :], rd[0:64, 0:n], ALU.mult)
        Eq = [emit_S(0)]
        for ui in range(len(units)):
            if ui + 1 < len(units):
                Eq.append(emit_S(ui + 1))
            emit_PV(ui, Eq.pop(0))
        nn = NT if need_ctx else NL
        for j in range(4):
            P.dma("sync", attn_s[j][:, 0:nn], aT[:, j, 0:nn])
        AR.release(m)

    def merge_stage(l, need_ctx):
        m = AR.mark()
        tiles = ALL_TILES if need_ctx else LAT_TILES
        mT = AR.alloc(KC * NT, BF16).rearrange("p (k t) -> p k t", t=NT)
        m1 = AR.mark()
        wbr = [[AR.alloc(4 * 128, BF16).rearrange("p (k n) -> p k n", n=128) for _ in range(3)] for _ in range(2)]
        wgt = [[AR.alloc(KC * 128, BF16).rearrange("p (k n) -> p k n", n=128) for _ in range(3)] for _ in range(2)]
        brr = [AR.rot(2, 4 * 512, BF16) for _ in range(3)]
        gsb = [AR.alloc(512) for _ in range(3)]
        mt = AR.rot(2, 512)
        wv = wview(w_in_d, l)
        wao = w_ao_d[l].rearrange("(h j d) n -> h d j n", h=2, j=4)
        wco = wview(w_co_d, l)
        wro = wview(w_ro_d, l)

        def load_c(c):
            wb, wg = wbr[c % 2], wgt[c % 2]
            for half in range(2):
                wload(wb[0][half * 64:(half + 1) * 64, :, :], wao[half][:, :, c * 128:(c + 1) * 128])
            wload(wb[1], wco[:, :, c * 128:(c + 1) * 128])
            wload(wb[2], wro[:, :, c * 128:(c + 1) * 128])
            for br in range(3):
                c0 = 2816 + br * 1024 + c * 128
                wload(wg[br], wv[:, :, c0:c0 + 128])
        if not dbg.get("skip_merge"):
            load_c(0)
            load_c(1)
        scr = (attn_s, conv_s, rnn_s)
        for c in (range(KC) if not dbg.get("skip_merge") else []):
            wb, wg = wbr[c % 2], wgt[c % 2]
            for ti, (g0, n, s) in enumerate(tiles):
                brt = []
                for br in range(3):
                    bt = brr[br]().rearrange("p (j t) -> p j t", t=512)
                    P.dma("sync", bt[:, :, 0:n], scr[br].rearrange("j p t -> p j t")[:, :, g0:g0 + n])
                    brt.append(bt)
                pb = [banks[br][:, 0:n] for br in range(3)]
                pg = [banks[3 + br][:, 0:n] for br in range(3)]
                for br in range(3):
                    for k in range(KC):
                        P.mm(pg[br], wg[br][:, k, :], hT[:, k, g0:g0 + n], start=(k == 0), stop=(k == KC - 1))
                    for k in range(4):
                        P.mm(pb[br], wb[br][:, k, :], brt[br][:, k, 0:n], start=(k == 0), stop=(k == 3))
                for br in range(3):
                    P.act(gsb[br][:, 0:n], pg[br], AF.Sigmoid)
                ma = mt()[:, 0:n]
                mb = mt()[:, 0:n]
                P.tt(ma, pb[0], gsb[0][:, 0:n], ALU.mult)
                P.tt(mb, pb[1], gsb[1][:, 0:n], ALU.mult)
                P.tt(ma, ma, mb, ALU.add)
                P.tt(mb, pb[2], gsb[2][:, 0:n], ALU.mult)
                P.tt(mT[:, c, g0:g0 + n], ma, mb, ALU.add)
            if c + 2 < KC:
                load_c(c + 2)
        AR.release(m1)
        wo = AR.alloc(KC * D, BF16).rearrange("p (k n) -> p k n", n=D)
        wov = wview(w_out_d, l)
        wload(wo[:, :, 0:512], wov[:, :, 0:512])
        wload(wo[:, :, 512:1024], wov[:, :, 512:1024])
        sqr = AR.rot(3, 512, BF16)
        stdb = AR.rot(2, 512)
        tb = AR.rot(2, 512)
        ob = AR.alloc(KC * 512).rearrange("p (k t) -> p k t", t=512)
        t512 = LAT_TILES + ([CTX_TILE] if need_ctx else [])
        for ti, (g0, n, s) in enumerate(t512 if not (dbg.get("skip_out") or dbg.get("o_nomm")) else []):
            pss = banks[6 + ti % 2][:, 0:n]
            for c in range(KC):
                pso = banks[c % 6][:, 0:n]
                for k in range(KC):
                    P.mm(pso, wo[:, k, c * 128:(c + 1) * 128], mT[:, k, g0:g0 + n], start=(k == 0), stop=(k == KC - 1))
                if dbg.get("o_mmonly"):
                    continue
                sq = sqr()[:, 0:n]
                P.act(sq, pso, AF.Square)
                P.act(ob[:, c, 0:n], pso, AF.Identity)
                if not dbg.get("o_noss"):
                    P.mm(pss, ones_bf[:], sq, start=(c == 0), stop=(c == KC - 1))
            if dbg.get("o_noss"):
                continue
            std = stdb()[:, 0:n]
            P.act(std, pss, AF.Sqrt, bias=EPS, scale=1.0 / D)
            P.recip(std, std)
            if dbg.get("o_noupd"):
                continue
            for c in range(KC):
                t = tb()[:, 0:n]
                P.tt(t, ob[:, c, 0:n], std, ALU.mult)
                P.stt(xT[:, c, g0:g0 + n], t, G1[l][:, c, s:s + 1], xT[:, c, g0:g0 + n], ALU.mult, ALU.add)
        AR.release(m)

    def ffn_stage(l, need_ctx):
        m = AR.mark()
        ftiles = [[(0, 768, 0)], [(768, 1536, 0)],
                  [(1536, 2048, 0)] + ([(NL, NT, 1)] if need_ctx else [])]
        aT = AR.alloc(24 * 768, BF16).rearrange("p (k t) -> p k t", t=768)
        wgr = AR.rot(3, KC * 128, BF16)
        wvr = AR.rot(3, KC * 128, BF16)
        wdr = AR.rot(2, 24 * 128, BF16)
        UW = 776
        m2 = AR.mark()
        ug = AR.rot(2, UW)
        uv = AR.rot(2, UW)
        ag = AR.rot(2, 768)
        av = AR.rot(2, 768)
        AR.release(m2)
        obf = AR.alloc(KC * 768, BF16).rearrange("p (k t) -> p k t", t=768)
        sqr = AR.rot(3, 512, BF16)
        stdb = AR.alloc(768)
        tb = AR.rot(2, 768)
        AR.top = max(AR.top, m2 + 2 * (2 * UW + 2 * 768))
        upv = wview(ffn_up_d, l)
        dnv = wview(ffn_down_d, l)
        for segs in ftiles:
            lay = []
            ucol = 0
            bcol = 0
            for (g0, g1, s) in segs:
                slo, shi = (0, NL) if s == 0 else (NL, NT)
                hl = 1 if g0 > slo else 0
                hr = 1 if g1 < shi else 0
                lay.append((g0, g1, s, hl, hr, ucol, bcol))
                ucol += (g1 - g0) + 2
                bcol += g1 - g0
            ntok = bcol

            def load_pair(p):
                a = wgr().rearrange("p (k n) -> p k n", n=128)
                b = wvr().rearrange("p (k n) -> p k n", n=128)
                wload(a, upv[:, :, p * 128:(p + 1) * 128])
                wload(b, upv[:, :, 3072 + p * 128:3072 + (p + 1) * 128])
                return a, b
            pend = [load_pair(0), load_pair(1), load_pair(2)]
            bk = [0]
            tail = None
            for p in range(24):
                wg_, wv_ = pend.pop(0)
                ugb, uvb, agb, avb = ug(), uv(), ag(), av()
                for (wsb, ub, ab, fch) in ((wg_, ugb, agb, p), (wv_, uvb, avb, 24 + p)):
                    for (g0, g1, s, hl, hr, uc, bc) in lay:
                        n = g1 - g0
                        if not hl:
                            P.memset(ub[:, uc:uc + 1], 0.0)
                        if not hr:
                            P.memset(ub[:, uc + n + 1:uc + n + 2], 0.0)
                        r0, r1 = g0 - hl, g1 + hr
                        dc = uc + 1 - hl
                        while r0 < r1:
                            pn = min(512, r1 - r0)
                            ps = banks[bk[0] % 6][:, 0:pn]
                            bk[0] += 1
                            for k in range(KC):
                                P.mm(ps, wsb[:, k, :], hT[:, k, r0:r0 + pn], start=(k == 0), stop=(k == KC - 1))
                            P.act(ub[:, dc:dc + pn], ps, AF.Copy)
                            r0 += pn
                            dc += pn
                        for tp in range(3):
                            src = ub[:, uc + tp:uc + tp + n]
                            wcol = V("ffn_dw", l, tp, fch)
                            if tp == 0:
                                P.act(ab[:, bc:bc + n], src, AF.Copy, scale=wcol)
                            else:
                                P.stt(ab[:, bc:bc + n], src, wcol, ab[:, bc:bc + n], ALU.mult, ALU.add)
                P.act(agb[:, 0:ntok], agb[:, 0:ntok], AF.Gelu_apprx_tanh, bias=V("ffn_dw_b", l, p))
                P.stt(aT[:, p, 0:ntok], avb[:, 0:ntok], V("ffn_dw_b", l, 24 + p), agb[:, 0:ntok], ALU.add, ALU.mult)
                if p + 3 < 24:
                    pend.append(load_pair(p + 3))
            pieces = []
            r0 = 0
            while r0 < ntok:
                pn = min(512, ntok - r0)
                pieces.append((r0, pn))
                r0 += pn

            def load_d(c):
                wd = wdr().rearrange("p (k n) -> p k n", n=128)
                wload(wd[:, 0:12, :], dnv[:, 0:12, c * 128:(c + 1) * 128])
                wload(wd[:, 12:24, :], dnv[:, 12:24, c * 128:(c + 1) * 128])
                return wd
            pendd = [load_d(0), load_d(1)]
            ssq = []
            for c in range(KC):
                wd = pendd.pop(0)
                for pi, (r0, pn) in enumerate(pieces):
                    ps = banks[(c * 2 + pi) % 6][:, 0:pn]
                    for k in range(24):
                        P.mm(ps, wd[:, k, :], aT[:, k, r0:r0 + pn], start=(k == 0), stop=(k == 23))
                    P.act(obf[:, c, r0:r0 + pn], ps, AF.Copy)
                    sq = sqr()[:, 0:pn]
                    P.act(sq, ps, AF.Square)
                    if ssq:
                        ssq.pop(0)()
                    ssq.append(lambda sq=sq, pi=pi, pn=pn, c=c: P.mm(banks[6 + pi][:, 0:pn], ones_bf[:], sq,
                                                                   start=(c == 0), stop=(c == KC - 1)))
                if c + 2 < KC:
                    pendd.append(load_d(c + 2))
            while ssq:
                ssq.pop(0)()
            for pi, (r0, pn) in enumerate(pieces):
                P.act(stdb[:, r0:r0 + pn], banks[6 + pi][:, 0:pn], AF.Sqrt, bias=EPS, scale=1.0 / D)
            P.recip(stdb[:, 0:ntok], stdb[:, 0:ntok])
            for c in range(KC):
                t = tb()
                P.tt(t[:, 0:ntok], obf[:, c, 0:ntok], stdb[:, 0:ntok], ALU.mult)
                for (g0, g1, s, hl, hr, uc, bc) in lay:
                    n = g1 - g0
                    P.stt(xT[:, c, g0:g1], t[:, bc:bc + n], G2[l][:, c, s:s + 1], xT[:, c, g0:g1], ALU.mult, ALU.add)
        AR.release(m)

    finals = []

    def on(name):
        return stages is None or name in stages
    for l in range(nlayers):
        need_ctx = l < L - 1
        if on("N1"):
            norm_stage(l, A1[l], 0, ALL_TILES)
        if l == 0 and dbg.get("dump_h1"):
            hd = nc.dram_tensor("hT_dump", [128, KC, NT], BF16, kind="ExternalOutput").ap()
            for c in range(KC):
                finals.append(P.dma("sync", hd[:, c, :], hT[:, c, :]))
            md = nc.dram_tensor("modv_dump", [128, 96], F32, kind="ExternalOutput").ap()
            finals.append(P.dma("sync", md, modv[0][:].rearrange("p a b -> p (a b)")))
        if on("R"):
            rnn_stage(l, need_ctx)
        if on("C"):
            conv_stage(l, need_ctx)
        if on("Q"):
            attn_stage(l, need_ctx)
        if on("M"):
            merge_stage(l, need_ctx)
        if l == 0 and dbg.get("dump_x1"):
            xd = nc.dram_tensor("x1_dump", [128, KC, NT], F32, kind="ExternalOutput").ap()
            for c in range(KC):
                finals.append(P.dma("sync", xd[:, c, :], xT[:, c, :]))
        if on("N2"):
            norm_stage(l, A2[l], 24, ALL_TILES if need_ctx else LAT_TILES)
        if on("F"):
            ffn_stage(l, need_ctx)

    for c in range(KC):
        finals.append(P.dma("sync", yT_d[:, c, :], xT[:, c, 0:NL]))
    P.emit(final_waits=finals)
    st.close()
    return nc, P


def rope_tables():
    rows = NL // 64
    row = np.repeat(np.arange(rows), 64).astype(np.float32)
    colv = np.tile(np.arange(64), rows).astype(np.float32)
    n_freq = 16
    freq = (np.float32(10000.0) ** (-np.arange(n_freq, dtype=np.float32) / np.float32(n_freq))).astype(np.float32)
    ang = np.concatenate([row[:, None] * freq, colv[:, None] * freq], axis=-1).astype(np.float32)
    cos = np.cos(ang).astype(np.float32).T
    sin = np.sin(ang).astype(np.float32).T
    C = np.concatenate([cos, cos, cos, cos], axis=0)
    S = np.concatenate([sin, sin, sin, sin], axis=0)
    rot = np.zeros((128, 128), np.float32)
    for hb in (0, 64):
        for mI in range(32):
            rot[hb + mI + 32, hb + mI] = -1.0
            rot[hb + mI, hb + mI + 32] = 1.0
    return np.ascontiguousarray(np.stack([C, S], 0)), np.ascontiguousarray(np.stack([rot, np.eye(128, dtype=np.float32)], 0))


_CACHE = {}


def kernel(**inputs):
    if "nc" not in _CACHE:
        _CACHE["nc"] = build_program()
    nc, _ = _CACHE["nc"]
    f = lambda k: np.ascontiguousarray(np.asarray(inputs[k], np.float32))
    x, ctx, c, c_ctx = f("x"), f("ctx"), f("c"), f("c_ctx")
    vecs = pack_vecs(inputs)
    rope, rot = rope_tables()
    shared = {"vecs": vecs, "rope": rope, "rotm": rot}
    for k in ("w_mod", "w_in", "w_attn_out", "w_conv_out", "w_rnn_out", "w_out", "ffn_up", "ffn_down", "rnn_wa", "rnn_wx"):
        shared[k] = f(k)
    in_maps = []
    B = x.shape[0]
    for b in range(B):
        xa = np.concatenate([x[b], ctx[b]], axis=0)
        xTb = np.ascontiguousarray(xa.T.reshape(KC, 128, NT).transpose(1, 0, 2))
        cc = np.stack([c[b], c_ctx], axis=-1)
        cTb = np.ascontiguousarray(cc.reshape(KC, 128, 2).transpose(1, 0, 2).reshape(128, KC * 2))
        d = dict(shared)
        d["xT"] = xTb
        d["cT"] = cTb
        in_maps.append(d)
    res = run_bass_kernel_spmd(nc, in_maps, core_ids=list(range(B)))
    out = np.empty((B, NL, D), np.float32)
    for b in range(B):
        yT = np.asarray(res.results[b]["yT"])
        out[b] = yT.transpose(2, 1, 0).reshape(NL, D)
    return out
```

```python
import contextlib
import numpy as np
import concourse.bass as bass
import concourse.mybir as mybir
from concourse.bass_utils import run_bass_kernel_spmd

F32 = mybir.dt.float32
BF16 = mybir.dt.bfloat16
AF = mybir.ActivationFunctionType
ALU = mybir.AluOpType
ESZ = {F32: 4, BF16: 2}

ENGS = ("sync", "scalar", "vector", "gpsimd", "tensor")

L = 2
D = 1024
NL = 2048
NCX = 256
NT = NL + NCX
EPS = 1e-6
KC = 8


def _esz(ap):
    try:
        return ESZ[ap.dtype]
    except Exception:
        return 4


def _box(ap):
    name = ap.tensor.name
    dims = [(int(s), int(c)) for s, c in ap.ap]
    off = int(ap.offset)
    es = _esz(ap)
    space = str(ap.space).upper()
    if "SB" not in space and "PSUM" not in space:
        lo = off + sum(min(0, s * (c - 1)) for s, c in dims)
        hi = off + sum(max(0, s * (c - 1)) for s, c in dims) + 1
        return (name, 0, 1, lo * es, hi * es)
    pstep, pcnt = dims[0]
    if pstep <= 0:
        p0 = 0
        foff = off
    else:
        p0 = off // pstep
        foff = off - p0 * pstep
    fd = dims[1:]
    lo = foff + sum(min(0, s * (c - 1)) for s, c in fd)
    hi = foff + sum(max(0, s * (c - 1)) for s, c in fd) + 1
    return (name, p0, p0 + pcnt, lo * es, hi * es)


class Op:
    __slots__ = ("eng", "fn", "deps", "tok", "needs_inc", "is_dma", "slot", "val")

    def __init__(self, eng, fn):
        self.eng = eng
        self.fn = fn
        self.deps = set()
        self.tok = None
        self.needs_inc = False
        self.is_dma = False
        self.slot = None
        self.val = 0


class Prog:
    def __init__(self, nc, n_dma_slots=16):
        self.nc = nc
        self.ops = []
        self.recs = {}
        self.n_dma_slots = n_dma_slots
        self.dma_count = {}
        self.slot_last = {}

    def _track(self, op, reads, writes):
        for ap in reads:
            b = _box(ap)
            lst = self.recs.get(b[0], [])
            keep = []
            for r in lst:
                ov = r[0] < b[2] and b[1] < r[1] and r[2] < b[4] and b[3] < r[3]
                if ov and r[5] and r[4] is not op:
                    op.deps.add(r[4])
                if (not r[5]) and (not r[4].is_dma) and (not op.is_dma) and r[4].eng == op.eng \
                        and b[1] <= r[0] and r[1] <= b[2] and b[3] <= r[2] and r[3] <= b[4]:
                    continue
                keep.append(r)
            keep.append([b[1], b[2], b[3], b[4], op, False])
            self.recs[b[0]] = keep
        for ap in writes:
            b = _box(ap)
            lst = self.recs.get(b[0], [])
            keep = []
            for r in lst:
                ov = r[0] < b[2] and b[1] < r[1] and r[2] < b[4] and b[3] < r[3]
                if ov and r[4] is not op:
                    op.deps.add(r[4])
                cov = b[1] <= r[0] and r[1] <= b[2] and b[3] <= r[2] and r[3] <= b[4]
                if cov and r[4] is not op:
                    continue
                keep.append(r)
            keep.append([b[1], b[2], b[3], b[4], op, True])
            self.recs[b[0]] = keep

    def add(self, eng, fn, reads=(), writes=()):
        op = Op(eng, fn)
        self._track(op, reads, writes)
        self.ops.append(op)
        return op

    def dma(self, eng, out, in_, **kw):
        op = Op(eng, None)
        op.is_dma = True
        i = self.dma_count.get(eng, 0)
        self.dma_count[eng] = i + 1
        op.slot = "%s%d" % (eng[0], i % (self.n_dma_slots if eng == "sync" else 4))
        prev = self.slot_last.get(op.slot)
        op.val = (prev.val if prev is not None else 0) + 16
        if prev is not None:
            op.deps.add(prev)
        self.slot_last[op.slot] = op
        op.fn = lambda e: e.dma_start(out=out, in_=in_, **kw)
        self._track(op, [in_], [out])
        self.ops.append(op)
        return op

    def mm(self, out, lhsT, rhs, start=True, stop=True):
        return self.add("tensor", lambda e: e.matmul(out, lhsT, rhs, start=start, stop=stop),
                        [lhsT, rhs] + ([] if start else [out]), [out])

    def act(self, out, in_, func, bias=None, scale=None):
        kw = {}
        rd = [in_]
        if bias is not None:
            kw["bias"] = bias
            if not isinstance(bias, (int, float)):
                rd.append(bias)
        if scale is not None:
            kw["scale"] = scale
            if not isinstance(scale, (int, float)):
                rd.append(scale)
        return self.add("scalar", lambda e: e.activation(out=out, in_=in_, func=func, **kw), rd, [out])

    def tt(self, out, in0, in1, op, eng="vector"):
        return self.add(eng, lambda e: e.tensor_tensor(out=out, in0=in0, in1=in1, op=op), [in0, in1], [out])

    def ts(self, out, in0, s1, s2, op0, op1=None, eng="vector"):
        rd = [in0] + [s for s in (s1, s2) if s is not None and not isinstance(s, (int, float))]
        if op1 is None:
            return self.add(eng, lambda e: e.tensor_scalar(out=out, in0=in0, scalar1=s1, scalar2=None, op0=op0), rd, [out])
        return self.add(eng, lambda e: e.tensor_scalar(out=out, in0=in0, scalar1=s1, scalar2=s2, op0=op0, op1=op1), rd, [out])

    def stt(self, out, in0, scalar, in1, op0, op1):
        rd = [in0, in1] + ([] if isinstance(scalar, (int, float)) else [scalar])
        return self.add("vector", lambda e: e.scalar_tensor_tensor(out=out, in0=in0, scalar=scalar, in1=in1, op0=op0, op1=op1), rd, [out])

    def copy(self, out, in_, eng="vector"):
        return self.add(eng, lambda e: e.tensor_copy(out=out, in_=in_), [in_], [out])

    def memset(self, ap, val, eng="vector"):
        return self.add(eng, lambda e: e.memset(ap, val), [], [ap])

    def recip(self, out, in_):
        return self.add("vector", lambda e: e.reciprocal(out=out, in_=in_), [in_], [out])

    def scan(self, out, d0, d1, init):
        rd = [d0, d1] + ([] if isinstance(init, (int, float)) else [init])
        return self.add("vector", lambda e: e.tensor_tensor_scan(out=out, data0=d0, data1=d1, initial=init,
                                                                 op0=ALU.mult, op1=ALU.add), rd, [out])

    def emit(self, final_waits=()):
        nc = self.nc
        seq = {e: 0 for e in ENGS}
        for op in self.ops:
            for d in op.deps:
                d.needs_inc = True
        for op in final_waits:
            op.needs_inc = True
        for op in self.ops:
            if op.is_dma:
                op.tok = (op.slot, op.val)
            elif op.needs_inc:
                seq[op.eng] += 1
                op.tok = (op.eng, seq[op.eng])
        clock = {e: {} for e in ENGS}
        opclock = {}
        per_eng = {e: [] for e in ENGS}
        slots = set()
        for op in self.ops:
            ck = clock[op.eng]
            wm = {}
            for d in sorted(op.deps, key=lambda o: (o.tok[0], o.tok[1])):
                src, v = d.tok
                if src == "tensor" and op.eng == "tensor":
                    continue
                if ck.get(src, 0) >= v:
                    continue
                wm[src] = max(wm.get(src, 0), v)
                oc = opclock.get(id(d))
                if oc:
                    for k, vv in oc.items():
                        if ck.get(k, 0) < vv:
                            ck[k] = vv
                ck[src] = max(ck.get(src, 0), v)
            if op.tok is not None:
                oc = dict(ck)
                oc[op.tok[0]] = max(oc.get(op.tok[0], 0), op.tok[1])
                opclock[id(op)] = oc
                if op.is_dma:
                    slots.add(op.slot)
            per_eng[op.eng].append((op, list(wm.items())))
        fin = [op.tok for op in final_waits] + [o.tok for o in self.slot_last.values()]
        with contextlib.ExitStack() as st:
            sems = {}
            for e in ENGS:
                sems[e] = st.enter_context(nc.semaphore("s_" + e))
            for s in sorted(slots):
                sems[s] = st.enter_context(nc.semaphore("s_" + s))
            block = st.enter_context(nc.Block())

            def run(engname):
                def body(e):
                    for op, waits in per_eng[engname]:
                        for s, v in waits:
                            e.wait_ge(sems[s], v)
                        ins = op.fn(e)
                        if op.is_dma:
                            ins.then_inc(sems[op.tok[0]], 16)
                        elif op.needs_inc:
                            ins.then_inc(sems[op.eng], 1)
                    if engname == "sync":
                        for s, v in fin:
                            e.wait_ge(sems[s], v)
                return body

            block.sync(run("sync"))
            block.scalar(run("scalar"))
            block.vector(run("vector"))
            block.gpsimd(run("gpsimd"))
            block.tensor(run("tensor"))
        self.stats = {e: len(per_eng[e]) for e in ENGS}


VEC_SPEC = [
    ("b_mod", (L,), 6144), ("norm_pre_mix", (L,), 1024), ("norm_post_mix", (L,), 1024),
    ("norm_pre_ffn", (L,), 1024), ("norm_post_ffn", (L,), 1024),
    ("q_norm", (L,), 128), ("k_norm", (L,), 128),
    ("conv_dw", (L, 31), 512), ("conv_dw_b", (L,), 512), ("conv_ln_g", (L,), 512), ("conv_ln_b", (L,), 512),
    ("rnn_conv_w", (L, 2, 4), 512), ("rnn_conv_b", (L, 2), 512), ("rnn_ba", (L, 2), 512),
    ("rnn_bx", (L, 2), 512), ("rnn_lambda", (L, 2), 512),
    ("ffn_dw", (L, 3), 6144), ("ffn_dw_b", (L,), 6144),
]
VEC_OFF = {}
_o = 0
for _n, _lead, _f in VEC_SPEC:
    VEC_OFF[_n] = (_o, _lead, _f // 128)
    _o += int(np.prod(_lead)) * (_f // 128)
NV = _o


def pack_vecs(inputs):
    out = np.zeros((128, NV), np.float32)
    for n, lead, f in VEC_SPEC:
        a = np.asarray(inputs[n], np.float32)
        if n in ("q_norm", "k_norm"):
            a = np.concatenate([a, a], axis=-1)
        a = a.reshape(int(np.prod(lead)), f // 128, 128)
        base = VEC_OFF[n][0]
        out[:, base:base + a.shape[0] * a.shape[1]] = a.reshape(-1, 128).T
    return out


def build_program(dbg=None):
    nc = bass.Bass("TRN2", target_bir_lowering=False)
    dbg = dbg or {}
    stages = dbg.get("stages")
    nlayers = dbg.get("layers", L)
    st = contextlib.ExitStack()

    def din(name, shape, dt=F32):
        return nc.dram_tensor(name, list(shape), dt, kind="ExternalInput").ap()

    xT_d = din("xT", [128, KC, NT])
    cT_d = din("cT", [128, KC * 2])
    vecs_d = din("vecs", [128, NV])
    rope_d = din("rope", [2, 128, NL])
    rotm_d = din("rotm", [2, 128, 128])
    w_mod_d = din("w_mod", [L, D, 6144])
    w_in_d = din("w_in", [L, D, 5888])
    w_ao_d = din("w_attn_out", [L, 512, D])
    w_co_d = din("w_conv_out", [L, 512, D])
    w_ro_d = din("w_rnn_out", [L, 512, D])
    w_out_d = din("w_out", [L, D, D])
    ffn_up_d = din("ffn_up", [L, D, 6144])
    ffn_down_d = din("ffn_down", [L, 3072, D])
    rnn_wa_d = din("rnn_wa", [L, 2, 8, 64, 64])
    rnn_wx_d = din("rnn_wx", [L, 2, 8, 64, 64])
    yT_d = nc.dram_tensor("yT", [128, KC, NL], F32, kind="ExternalOutput").ap()
    skind = "ExternalOutput" if dbg else "Internal"
    attn_s = nc.dram_tensor("attn_s", [4, 128, NT], BF16, kind=skind).ap()
    conv_s = nc.dram_tensor("conv_s", [4, 128, NT], BF16, kind=skind).ap()
    rnn_s = nc.dram_tensor("rnn_s", [4, 128, NT], BF16, kind=skind).ap()

    def sb(name, shape, dt=F32):
        return st.enter_context(nc.sbuf_tensor(name, list(shape), dt))

    P = Prog(nc)

    xT = sb("xTs", [128, KC, NT])
    hT = sb("hTs", [128, KC, NT], BF16)
    vecs = sb("vecs_s", [128, NV])
    modv = [sb("modv%d" % l, [128, 48, 2]) for l in range(L)]
    A1 = [sb("A1_%d" % l, [128, KC, 2]) for l in range(L)]
    G1 = [sb("G1_%d" % l, [128, KC, 2]) for l in range(L)]
    A2 = [sb("A2_%d" % l, [128, KC, 2]) for l in range(L)]
    G2 = [sb("G2_%d" % l, [128, KC, 2]) for l in range(L)]
    clv = sb("clv", [128, L * 2 * 4])
    ones_bf = sb("ones_bf", [128, 128], BF16)
    ones32 = sb("ones32", [128, 128])
    bo64 = sb("bo64", [128, 128], BF16)
    rotm = sb("rotm_s", [128, 128])
    ident = sb("ident_s", [128, 128])
    ct = sb("ct_s", [128, KC * 2])
    scb = sb("scb", [128, KC * 2], BF16)
    ARENA_W = 23296
    arena = sb("arena", [128, ARENA_W])
    pbig = [st.enter_context(nc.psum_tensor("pbig%d" % i, [128, 1024], F32)) for i in range(4)]
    banks = [pbig[i // 2][:, (i % 2) * 512:(i % 2) * 512 + 512] for i in range(8)]

    class Arena:
        def __init__(self):
            self.top = 0

        def mark(self):
            return self.top

        def release(self, m):
            self.top = m

        def alloc(self, n, dt=F32):
            w = n if dt == F32 else (n + 1) // 2
            w = (w + 7) // 8 * 8
            assert self.top + w <= ARENA_W, ("arena overflow", self.top, w)
            v = arena[:, self.top:self.top + w]
            self.top += w
            if dt != F32:
                v = v.bitcast(dt)[:, 0:n]
            return v

        def rot(self, k, n, dt=F32):
            bufs = [self.alloc(n, dt) for _ in range(k)]
            state = [0]

            def nxt():
                b = bufs[state[0] % k]
                state[0] += 1
                return b
            return nxt

    AR = Arena()

    def V(name, *idx):
        base, lead, nch = VEC_OFF[name]
        *li, c = idx
        flat = 0
        for i, d in zip(li, lead):
            flat = flat * d + i
        col = base + flat * nch + c
        return vecs[:, col:col + 1]

    def wview(wd, l):
        return wd[l].rearrange("(k p) n -> p k n", p=128)

    def wload(dst, src):
        P.dma("gpsimd", dst, src)

    P.dma("sync", vecs[:], vecs_d)
    P.dma("sync", ct[:], cT_d)
    P.dma("sync", rotm[:], rotm_d[0])
    P.dma("sync", ident[:], rotm_d[1])
    P.memset(ones_bf[:], 1.0)
    P.memset(ones32[:], 1.0)
    P.memset(bo64[:], 0.0)
    P.memset(bo64[0:64, 0:64], 1.0)
    P.memset(bo64[64:128, 64:128], 1.0)
    for c in range(KC):
        P.dma("sync", xT[:, c, :], xT_d[:, c, :])
    P.act(scb[:], ct[:], AF.Silu)
    lb = VEC_OFF["rnn_lambda"][0]
    P.act(clv[:], vecs[:, lb:lb + L * 8], AF.Exp, scale=-1.0)
    P.act(clv[:], clv[:], AF.Ln, bias=1.0)
    P.ts(clv[:], clv[:], -8.0, None, ALU.mult)

    m0 = AR.mark()
    wrot = AR.rot(3, KC * 512, BF16)
    scb3 = scb[:].rearrange("p (k s) -> p k s", s=2)
    bi = [0]

    def modblock_load(l, blk):
        wb = wrot().rearrange("p (k n) -> p k n", n=512)
        wload(wb, wview(w_mod_d, l)[:, :, blk * 512:(blk + 1) * 512])
        return wb

    jobs = [(l, blk) for l in range(L) for blk in range(12)]
    pend = [modblock_load(*jobs[0]), modblock_load(*jobs[1])]
    for ji, (l, blk) in enumerate(jobs):
        wb = pend.pop(0)
        for j in range(4):
            ch = blk * 4 + j
            ps = banks[bi[0] % 8][:, 0:2]
            bi[0] += 1
            for k in range(KC):
                P.mm(ps, wb[:, k, j * 128:(j + 1) * 128], scb3[:, k, :], start=(k == 0), stop=(k == KC - 1))
            P.ts(modv[l][:, ch, :], ps, V("b_mod", l, ch), None, ALU.add)
        if ji + 2 < len(jobs):
            pend.append(modblock_load(*jobs[ji + 2]))
    for l in range(L):
        for c in range(KC):
            P.ts(A1[l][:, c, :], modv[l][:, 8 + c, :], 1.0, V("norm_pre_mix", l, c), ALU.add, ALU.mult)
            P.ts(G1[l][:, c, :], modv[l][:, 16 + c, :], V("norm_post_mix", l, c), None, ALU.mult)
            P.ts(A2[l][:, c, :], modv[l][:, 32 + c, :], 1.0, V("norm_pre_ffn", l, c), ALU.add, ALU.mult)
            P.ts(G2[l][:, c, :], modv[l][:, 40 + c, :], V("norm_post_ffn", l, c), None, ALU.mult)
    AR.release(m0)

    LAT_TILES = [(i * 512, 512, 0) for i in range(4)]
    CTX_TILE = (NL, NCX, 1)
    ALL_TILES = LAT_TILES + [CTX_TILE]

    def norm_stage(l, A, shbase, tiles):
        m = AR.mark()
        sqr = AR.rot(3, 512, BF16)
        stdb = AR.rot(2, 512)
        rstb = AR.rot(2, 512)
        tb = AR.rot(3, 512)
        for ti, (g0, n, s) in enumerate(tiles):
            ss = banks[ti % 2][:, 0:n]
            for c in range(KC):
                sq = sqr()[:, 0:n]
                P.act(sq, xT[:, c, g0:g0 + n], AF.Square)
                P.mm(ss, ones_bf[:], sq, start=(c == 0), stop=(c == KC - 1))
            std = stdb()[:, 0:n]
            rstd = rstb()[:, 0:n]
            P.act(std, ss, AF.Sqrt, bias=EPS, scale=1.0 / D)
            P.recip(rstd, std)
            for c in range(KC):
                t = tb()[:, 0:n]
                P.tt(t, xT[:, c, g0:g0 + n], rstd, ALU.mult)
                P.act(hT[:, c, g0:g0 + n], t, AF.Identity, bias=modv[l][:, shbase + c, s:s + 1], scale=A[:, c, s:s + 1])
        AR.release(m)

    def proj(ps, wsb, col0, g0, n):
        for k in range(KC):
            P.mm(ps, wsb[:, k, col0:col0 + 128], hT[:, k, g0:g0 + n], start=(k == 0), stop=(k == KC - 1))

    def rnn_stage(l, need_ctx):
        m = AR.mark()
        LAT0, CTX0, W = 4, 4 + NL + 8, 4 + NL + 8 + NCX + 4

        def col(g0, s):
            return LAT0 + g0 if s == 0 else CTX0 + (g0 - NL)
        wxr = AR.rot(2, KC * 128, BF16)
        wrr = AR.rot(2, KC * 128, BF16)
        wv = wview(w_in_d, l)

        def load_w(j):
            a_ = wxr().rearrange("p (k n) -> p k n", n=128)
            b_ = wrr().rearrange("p (k n) -> p k n", n=128)
            wload(a_, wv[:, :, 1792 + j * 128:1792 + (j + 1) * 128])
            wload(b_, wv[:, :, 2304 + j * 128:2304 + (j + 1) * 128])
            return a_, b_
        xbuf = AR.alloc(W)
        xc = AR.alloc(W)
        rbs = [AR.alloc(W), AR.alloc(W)]
        ibs = [AR.alloc(W), AR.alloc(W)]
        hf = AR.alloc(W)
        gz = AR.alloc(W)
        xcb = AR.alloc(W, BF16)
        yb = AR.alloc(W, BF16)
        wbd = [AR.alloc(128, BF16) for _ in range(4)]
        for bufz in [xbuf, xc, hf, gz] + rbs + ibs:
            P.memset(bufz, 0.0)
        P.memset(xcb, 0.0)
        P.memset(yb, 0.0)
        rtiles = ALL_TILES if need_ctx else LAT_TILES
        o0, o1 = LAT0, CTX0 + NCX
        pendw = [load_w(0), load_w(1)]
        for j in range(4):
            wx, wr = pendw.pop(0)
            for ti, (g0, n, s) in enumerate(ALL_TILES):
                ps = banks[ti % 2][:, 0:n]
                proj(ps, wx, 0, g0, n)
                P.act(xbuf[:, col(g0, s):col(g0, s) + n], ps, AF.Identity)
            for ti, (g0, n, s) in enumerate(rtiles):
                ps = banks[2 + ti % 2][:, 0:n]
                proj(ps, wr, 0, g0, n)
                P.act(gz[:, col(g0, s):col(g0, s) + n], ps, AF.Gelu_apprx_tanh)
            if j + 2 < 4:
                pendw.append(load_w(j + 2))
            for d in range(2):
                rb, ib = rbs[d], ibs[d]
                wa_t = wbd[d * 2]
                wx_t = wbd[d * 2 + 1]
                for wt, src in ((wa_t, rnn_wa_d), (wx_t, rnn_wx_d)):
                    P.memset(wt, 0.0)
                    P.dma("gpsimd", wt[0:64, 0:64], src[l, d, 2 * j])
                    P.dma("gpsimd", wt[64:128, 64:128], src[l, d, 2 * j + 1])
                for tp in range(4):
                    sh = (tp - 3) if d == 0 else (3 - tp)
                    src = xbuf[:, o0 + sh:o1 + sh]
                    wcol = V("rnn_conv_w", l, d, tp, j)
                    if tp == 0:
                        P.act(xc[:, o0:o1], src, AF.Identity, bias=V("rnn_conv_b", l, d, j), scale=wcol)
                    else:
                        P.stt(xc[:, o0:o1], src, wcol, xc[:, o0:o1], ALU.mult, ALU.add)
                P.act(xcb[:, o0:o1], xc[:, o0:o1], AF.Identity)
                for ti, (g0, n, s) in enumerate(ALL_TILES):
                    c0 = col(g0, s)
                    psr = banks[4 + ti % 2][:, 0:n]
                    psi = banks[6 + ti % 2][:, 0:n]
                    P.mm(psr, wa_t, xcb[:, c0:c0 + n])
                    P.mm(psi, wx_t, xcb[:, c0:c0 + n])
                    P.act(rb[:, c0:c0 + n], psr, AF.Sigmoid, bias=V("rnn_ba", l, d, j))
                    P.act(ib[:, c0:c0 + n], psi, AF.Sigmoid, bias=V("rnn_bx", l, d, j))
                ci = (l * 2 + d) * 4 + j
                P.act(rb[:, o0:o1], rb[:, o0:o1], AF.Exp, scale=clv[:, ci:ci + 1])
                P.tt(ib[:, o0:o1], ib[:, o0:o1], xc[:, o0:o1], ALU.mult)
                P.tt(xc[:, o0:o1], rb[:, o0:o1], rb[:, o0:o1], ALU.mult)
                P.act(xc[:, o0:o1], xc[:, o0:o1], AF.Sqrt, bias=1.0, scale=-1.0)
                P.tt(ib[:, o0:o1], ib[:, o0:o1], xc[:, o0:o1], ALU.mult)
            rb, ib = rbs[0], ibs[0]
            P.scan(hf[:, CTX0:CTX0 + NCX], rb[:, CTX0:CTX0 + NCX], ib[:, CTX0:CTX0 + NCX], 0.0)
            P.scan(hf[:, LAT0:LAT0 + NL], rb[:, LAT0:LAT0 + NL], ib[:, LAT0:LAT0 + NL],
                   hf[:, CTX0 + NCX - 1:CTX0 + NCX])
            rb, ib = rbs[1], ibs[1]
            P.scan(xc[:, CTX0:CTX0 + NCX][:, ::-1], rb[:, CTX0:CTX0 + NCX][:, ::-1],
                   ib[:, CTX0:CTX0 + NCX][:, ::-1], 0.0)
            P.scan(xc[:, LAT0:LAT0 + NL][:, ::-1], rb[:, LAT0:LAT0 + NL][:, ::-1],
                   ib[:, LAT0:LAT0 + NL][:, ::-1], xc[:, CTX0:CTX0 + 1])
            segs = [(LAT0, NL, 0)] + ([(CTX0, NCX, NL)] if need_ctx else [])
            for (c0, n, g0) in segs:
                P.tt(hf[:, c0:c0 + n], hf[:, c0:c0 + n], xc[:, c0:c0 + n], ALU.add)
                P.tt(yb[:, c0:c0 + n], hf[:, c0:c0 + n], gz[:, c0:c0 + n], ALU.mult)
                P.dma("sync", rnn_s[j][:, g0:g0 + n], yb[:, c0:c0 + n])
        AR.release(m)

    def conv_stage(l, need_ctx):
        m = AR.mark()
        LAT0, CTX0, W = 15, 15 + NL + 30, 15 + NL + 30 + NCX + 15

        def col(g0, s):
            return LAT0 + g0 if s == 0 else CTX0 + (g0 - NL)
        tiles = ALL_TILES if need_ctx else LAT_TILES
        wval = AR.alloc(KC * 512, BF16).rearrange("p (k n) -> p k n", n=512)
        wgat = AR.alloc(KC * 512, BF16).rearrange("p (k n) -> p k n", n=512)
        wv = wview(w_in_d, l)
        wload(wval, wv[:, :, 768:1280])
        wload(wgat, wv[:, :, 1280:1792])
        ubuf = AR.alloc(W, BF16)
        acc = [AR.alloc(W) for _ in range(4)]
        sigr = AR.rot(2, 512)
        dg = [AR.alloc(31 * 128, BF16).rearrange("p (k n) -> p k n", n=128) for _ in range(1)]
        P.memset(ubuf, 0.0)
        for j in range(4):
            dgj = dg[0]
            for tp in range(31):
                P.ts(dgj[:, tp, :], ident[:], V("conv_dw", l, tp, j), None, ALU.mult)
            for ti, (g0, n, s) in enumerate(tiles):
                pv = banks[ti % 2][:, 0:n]
                pg = banks[2 + ti % 2][:, 0:n]
                proj(pg, wgat, j * 128, g0, n)
                proj(pv, wval, j * 128, g0, n)
                sg = sigr()[:, 0:n]
                P.act(sg, pg, AF.Sigmoid)
                P.tt(ubuf[:, col(g0, s):col(g0, s) + n], pv, sg, ALU.mult)
            for ti, (g0, n, s) in enumerate(tiles):
                c0 = col(g0, s)
                pc = banks[4 + ti % 2][:, 0:n]
                for tp in range(31):
                    P.mm(pc, dgj[:, tp, :], ubuf[:, c0 + tp - 15:c0 + tp - 15 + n], start=(tp == 0), stop=(tp == 30))
                P.act(acc[j][:, c0:c0 + n], pc, AF.Identity, bias=V("conv_dw_b", l, j))
        sqr = AR.rot(2, 512)
        meanb = AR.rot(2, 512)
        varb = AR.rot(2, 512)
        tb = AR.rot(2, 512)
        ob = AR.rot(3, 512, BF16)
        for ti, (g0, n, s) in enumerate(tiles):
            c0 = col(g0, s)
            psum_ = banks[(ti % 2) * 2][:, 0:n]
            psq = banks[(ti % 2) * 2 + 1][:, 0:n]
            for j in range(4):
                P.mm(psum_, ones32[:], acc[j][:, c0:c0 + n], start=(j == 0), stop=(j == 3))
            for j in range(4):
                sq = sqr()[:, 0:n]
                P.act(sq, acc[j][:, c0:c0 + n], AF.Square)
                P.mm(psq, ones32[:], sq, start=(j == 0), stop=(j == 3))
            mean = meanb()[:, 0:n]
            var = varb()[:, 0:n]
            P.act(mean, psum_, AF.Identity, scale=1.0 / 512)
            P.tt(var, mean, mean, ALU.mult)
            P.stt(var, psq, 1.0 / 512, var, ALU.mult, ALU.subtract)
            P.act(var, var, AF.Sqrt, bias=EPS, scale=1.0)
            P.recip(var, var)
            for j in range(4):
                t = tb()[:, 0:n]
                P.tt(t, acc[j][:, c0:c0 + n], mean, ALU.subtract)
                P.tt(t, t, var, ALU.mult)
                o = ob()[:, 0:n]
                P.act(o, t, AF.Silu, bias=V("conv_ln_b", l, j), scale=V("conv_ln_g", l, j))
                P.dma("sync", conv_s[j][:, g0:g0 + n], o)
        AR.release(m)

    def attn_stage(l, need_ctx):
        m = AR.mark()
        qT = AR.alloc(4 * NT, BF16).rearrange("p (j t) -> p j t", t=NT)
        kTp = [AR.alloc(NT, BF16) for _ in range(2)]
        Vs = AR.alloc(18 * 256, BF16).rearrange("p (t h d) -> p t h d", h=2, d=128)
        P.memset(kTp[0][64:128, :], 0.0)
        P.memset(kTp[1][0:64, :], 0.0)
        for kv_ in range(2):
            P.memset(Vs[:, :, kv_, 64:128], 1.0)
        m1 = AR.mark()
        wq = AR.alloc(KC * 512, BF16).rearrange("p (k n) -> p k n", n=512)
        wkv = AR.alloc(KC * 256, BF16).rearrange("p (k n) -> p k n", n=256)
        Ct = AR.alloc(NL)
        St = AR.alloc(NL)
        wv = wview(w_in_d, l)
        for j in range(4):
            for half in range(2):
                hd = half * 4 + j
                wload(wq[:, :, j * 128 + half * 64:j * 128 + half * 64 + 64], wv[:, :, hd * 64:(hd + 1) * 64])
        wload(wkv, wv[:, :, 512:768])
        P.dma("sync", Ct, rope_d[0])
        P.dma("sync", St, rope_d[1])
        sqr = AR.rot(2, 512, BF16)
        stdb = AR.rot(2, 512)
        qnb = AR.rot(2, 512)
        t1b = AR.rot(2, 512)
        t2b = AR.rot(2, 512)
        qtiles = ALL_TILES if need_ctx else LAT_TILES
        jobs = [(j, t, "q") for j in range(4) for t in qtiles] + [(0, t, "k") for t in ALL_TILES]
        for ji, (j, (g0, n, s), kind) in enumerate(jobs):
            ps = banks[ji % 2][:, 0:n]
            if kind == "q":
                proj(ps, wq, j * 128, g0, n)
                gain = V("q_norm", l, 0)
                dst = qT[:, j, g0:g0 + n]
            else:
                proj(ps, wkv, 0, g0, n)
                gain = V("k_norm", l, 0)
                dst = None
            sq = sqr()[:, 0:n]
            P.act(sq, ps, AF.Square)
            ps2 = banks[2 + ji % 2][:, 0:n]
            P.mm(ps2, bo64[:], sq)
            std = stdb()[:, 0:n]
            P.act(std, ps2, AF.Sqrt, bias=EPS, scale=1.0 / 64)
            P.recip(std, std)
            qn = qnb()[:, 0:n]
            P.stt(qn, ps, gain, std, ALU.mult, ALU.mult)
            if s == 0:
                ps3 = banks[4 + ji % 2][:, 0:n]
                P.mm(ps3, rotm[:], qn)
                t1 = t1b()[:, 0:n]
                t2 = t2b()[:, 0:n]
                P.tt(t1, qn, Ct[:, g0:g0 + n], ALU.mult)
                P.tt(t2, ps3, St[:, g0:g0 + n], ALU.mult)
                if dst is not None:
                    P.tt(dst, t1, t2, ALU.add)
                else:
                    P.tt(kTp[0][0:64, g0:g0 + n], t1[0:64, :], t2[0:64, :], ALU.add)
                    P.tt(kTp[1][64:128, g0:g0 + n], t1[64:128, :], t2[64:128, :], ALU.add)
            else:
                if dst is not None:
                    P.act(dst, qn, AF.Identity)
                else:
                    P.act(kTp[0][0:64, g0:g0 + n], qn[0:64, :], AF.Identity)
                    P.act(kTp[1][64:128, g0:g0 + n], qn[64:128, :], AF.Identity)
        for tt_ in range(18):
            ps = banks[6 + tt_ % 2][:, 0:128]
            for k in range(KC):
                P.mm(ps, hT[:, k, tt_ * 128:(tt_ + 1) * 128], wkv[:, k, 128:256], start=(k == 0), stop=(k == KC - 1))
            P.act(Vs[:, tt_, :, 0:64], ps.rearrange("p (a b) -> p a b", b=64), AF.Identity)
        AR.release(m1)
        aT = AR.alloc(4 * NT, BF16).rearrange("p (j t) -> p j t", t=NT)
        Er = AR.rot(3, 1024, BF16)
        rdb = AR.rot(2, 512)
        units = []
        for h in range(8):
            qsets = [(g0, n, list(range(18))) for (g0, n, s) in LAT_TILES]
            if need_ctx:
                qsets.append((NL, NCX, [16, 17]))
            for qi, (g0, n, kts) in enumerate(qsets):
                prs = [kts[i:i + 2] for i in range(0, len(kts), 2)]
                for pi, pr in enumerate(prs):
                    units.append((h, g0, n, pr, pi == 0, pi == len(prs) - 1))
        state = {"pso_i": -1}

        def emit_S(ui):
            h, g0, n, pr, first, last = units[ui]
            j, half = h % 4, h // 4
            pss = pbig[ui % 2].rearrange("p (a b) -> p a b", b=512)
            for a_, kt in enumerate(pr):
                P.mm(pss[:, a_, 0:n], kTp[half][:, kt * 128:(kt + 1) * 128], qT[:, j, g0:g0 + n])
            E = Er().rearrange("p (a b) -> p a b", b=512)
            P.act(E[:, 0:len(pr), 0:n], pss[:, 0:len(pr), 0:n], AF.Exp, scale=0.125)
            return E

        def emit_PV(ui, E):
            h, g0, n, pr, first, last = units[ui]
            j, half = h % 4, h // 4
            po = half * 64
            if first:
                state["pso_i"] += 1
            pso = banks[4 + state["pso_i"] % 3][:, 0:n]
            for a_, kt in enumerate(pr):
                P.mm(pso, Vs[:, kt, half, :], E[:, a_, 0:n], start=(first and a_ == 0), stop=(last and a_ == len(pr) - 1))
            if last:
                rd = rdb()
                P.recip(rd[0:64, 0:n], pso[64:128, :])
                P.tt(aT[po:po + 64, j, g0:g0 + n], pso[0:64, :], rd[0:64, 0:n], ALU.mult)
        Eq = [emit_S(0)]
        for ui in range(len(units)):
            if ui + 1 < len(units):
                Eq.append(emit_S(ui + 1))
            emit_PV(ui, Eq.pop(0))
        nn = NT if need_ctx else NL
        for j in range(4):
            P.dma("sync", attn_s[j][:, 0:nn], aT[:, j, 0:nn])
        AR.release(m)

    def merge_stage(l, need_ctx):
        m = AR.mark()
        tiles = ALL_TILES if need_ctx else LAT_TILES
        mT = AR.alloc(KC * NT, BF16).rearrange("p (k t) -> p k t", t=NT)
        m1 = AR.mark()
        wbr = [[AR.alloc(4 * 128, BF16).rearrange("p (k n) -> p k n", n=128) for _ in range(3)] for _ in range(2)]
        wgt = [[AR.alloc(KC * 128, BF16).rearrange("p (k n) -> p k n", n=128) for _ in range(3)] for _ in range(2)]
        brr = [AR.rot(2, 4 * 512, BF16) for _ in range(3)]
        gsb = [AR.alloc(512) for _ in range(3)]
        mt = AR.rot(2, 512)
        wv = wview(w_in_d, l)
        wao = w_ao_d[l].rearrange("(h j d) n -> h d j n", h=2, j=4)
        wco = wview(w_co_d, l)
        wro = wview(w_ro_d, l)

        def load_c(c):
            wb, wg = wbr[c % 2], wgt[c % 2]
            for half in range(2):
                wload(wb[0][half * 64:(half + 1) * 64, :, :], wao[half][:, :, c * 128:(c + 1) * 128])
            wload(wb[1], wco[:, :, c * 128:(c + 1) * 128])
            wload(wb[2], wro[:, :, c * 128:(c + 1) * 128])
            for br in range(3):
                c0 = 2816 + br * 1024 + c * 128
                wload(wg[br], wv[:, :, c0:c0 + 128])
        if not dbg.get("skip_merge"):
            load_c(0)
            load_c(1)
        scr = (attn_s, conv_s, rnn_s)
        for c in (range(KC) if not dbg.get("skip_merge") else []):
            wb, wg = wbr[c % 2], wgt[c % 2]
            for ti, (g0, n, s) in enumerate(tiles):
                brt = []
                for br in range(3):
                    bt = brr[br]().rearrange("p (j t) -> p j t", t=512)
                    P.dma("sync", bt[:, :, 0:n], scr[br].rearrange("j p t -> p j t")[:, :, g0:g0 + n])
                    brt.append(bt)
                pb = [banks[br][:, 0:n] for br in range(3)]
                pg = [banks[3 + br][:, 0:n] for br in range(3)]
                for br in range(3):
                    for k in range(KC):
                        P.mm(pg[br], wg[br][:, k, :], hT[:, k, g0:g0 + n], start=(k == 0), stop=(k == KC - 1))
                    for k in range(4):
                        P.mm(pb[br], wb[br][:, k, :], brt[br][:, k, 0:n], start=(k == 0), stop=(k == 3))
                for br in range(3):
                    P.act(gsb[br][:, 0:n], pg[br], AF.Sigmoid)
                ma = mt()[:, 0:n]
                mb = mt()[:, 0:n]
                P.tt(ma, pb[0], gsb[0][:, 0:n], ALU.mult)
                P.tt(mb, pb[1], gsb[1][:, 0:n], ALU.mult)
                P.tt(ma, ma, mb, ALU.add)
                P.tt(mb, pb[2], gsb[2][:, 0:n], ALU.mult)
                P.tt(mT[:, c, g0:g0 + n], ma, mb, ALU.add)
            if c + 2 < KC:
                load_c(c + 2)
        AR.release(m1)
        wo = AR.alloc(KC * D, BF16).rearrange("p (k n) -> p k n", n=D)
        wov = wview(w_out_d, l)
        wload(wo[:, :, 0:512], wov[:, :, 0:512])
        wload(wo[:, :, 512:1024], wov[:, :, 512:1024])
        sqr = AR.rot(3, 512, BF16)
        stdb = AR.rot(2, 512)
        tb = AR.rot(2, 512)
        ob = AR.alloc(KC * 512).rearrange("p (k t) -> p k t", t=512)
        t512 = LAT_TILES + ([CTX_TILE] if need_ctx else [])
        for ti, (g0, n, s) in enumerate(t512 if not (dbg.get("skip_out") or dbg.get("o_nomm")) else []):
            pss = banks[6 + ti % 2][:, 0:n]
            ssq = []
            for c in range(KC):
                pso = banks[c % 6][:, 0:n]
                for k in range(KC):
                    P.mm(pso, wo[:, k, c * 128:(c + 1) * 128], mT[:, k, g0:g0 + n], start=(k == 0), stop=(k == KC - 1))
                if dbg.get("o_mmonly"):
                    continue
                sq = sqr()[:, 0:n]
                P.act(sq, pso, AF.Square)
                P.act(ob[:, c, 0:n], pso, AF.Identity)
                if ssq:
                    ssq.pop(0)()
                ssq.append(lambda sq=sq, c=c, pss=pss: P.mm(pss, ones_bf[:], sq, start=(c == 0), stop=(c == KC - 1)))
            while ssq:
                ssq.pop(0)()
            if dbg.get("o_noss"):
                continue
            std = stdb()[:, 0:n]
            P.act(std, pss, AF.Sqrt, bias=EPS, scale=1.0 / D)
            P.recip(std, std)
            if dbg.get("o_noupd"):
                continue
            for c in range(KC):
                t = tb()[:, 0:n]
                P.tt(t, ob[:, c, 0:n], std, ALU.mult)
                P.stt(xT[:, c, g0:g0 + n], t, G1[l][:, c, s:s + 1], xT[:, c, g0:g0 + n], ALU.mult, ALU.add)
        AR.release(m)

    def ffn_stage(l, need_ctx):
        m = AR.mark()
        ftiles = [[(0, 768, 0)], [(768, 1536, 0)],
                  [(1536, 2048, 0)] + ([(NL, NT, 1)] if need_ctx else [])]
        aT = AR.alloc(24 * 768, BF16).rearrange("p (k t) -> p k t", t=768)
        wgr = AR.rot(3, KC * 128, BF16)
        wvr = AR.rot(3, KC * 128, BF16)
        wdr = AR.rot(2, 24 * 128, BF16)
        UW = 776
        m2 = AR.mark()
        ug = AR.rot(2, UW)
        uv = AR.rot(2, UW)
        ag = AR.rot(2, 768)
        av = AR.rot(2, 768)
        AR.release(m2)
        obf = AR.alloc(KC * 768, BF16).rearrange("p (k t) -> p k t", t=768)
        sqr = AR.rot(3, 512, BF16)
        stdb = AR.alloc(768)
        tb = AR.rot(2, 768)
        AR.top = max(AR.top, m2 + 2 * (2 * UW + 2 * 768))
        upv = wview(ffn_up_d, l)
        dnv = wview(ffn_down_d, l)
        for segs in ftiles:
            lay = []
            ucol = 0
            bcol = 0
            for (g0, g1, s) in segs:
                slo, shi = (0, NL) if s == 0 else (NL, NT)
                hl = 1 if g0 > slo else 0
                hr = 1 if g1 < shi else 0
                lay.append((g0, g1, s, hl, hr, ucol, bcol))
                ucol += (g1 - g0) + 2
                bcol += g1 - g0
            ntok = bcol

            def load_pair(p):
                a = wgr().rearrange("p (k n) -> p k n", n=128)
                b = wvr().rearrange("p (k n) -> p k n", n=128)
                wload(a, upv[:, :, p * 128:(p + 1) * 128])
                wload(b, upv[:, :, 3072 + p * 128:3072 + (p + 1) * 128])
                return a, b
            pend = [load_pair(0), load_pair(1), load_pair(2)]
            bk = [0]
            tail = None
            for p in range(24):
                wg_, wv_ = pend.pop(0)
                ugb, uvb, agb, avb = ug(), uv(), ag(), av()
                for (wsb, ub, ab, fch) in ((wg_, ugb, agb, p), (wv_, uvb, avb, 24 + p)):
                    for (g0, g1, s, hl, hr, uc, bc) in lay:
                        n = g1 - g0
                        if not hl:
                            P.memset(ub[:, uc:uc + 1], 0.0)
                        if not hr:
                            P.memset(ub[:, uc + n + 1:uc + n + 2], 0.0)
                        r0, r1 = g0 - hl, g1 + hr
                        dc = uc + 1 - hl
                        while r0 < r1:
                            pn = min(512, r1 - r0)
                            ps = banks[bk[0] % 6][:, 0:pn]
                            bk[0] += 1
                            for k in range(KC):
                                P.mm(ps, wsb[:, k, :], hT[:, k, r0:r0 + pn], start=(k == 0), stop=(k == KC - 1))
                            P.act(ub[:, dc:dc + pn], ps, AF.Copy)
                            r0 += pn
                            dc += pn
                        for tp in range(3):
                            src = ub[:, uc + tp:uc + tp + n]
                            wcol = V("ffn_dw", l, tp, fch)
                            if tp == 0:
                                P.act(ab[:, bc:bc + n], src, AF.Copy, scale=wcol)
                            else:
                                P.stt(ab[:, bc:bc + n], src, wcol, ab[:, bc:bc + n], ALU.mult, ALU.add)
                P.act(agb[:, 0:ntok], agb[:, 0:ntok], AF.Gelu_apprx_tanh, bias=V("ffn_dw_b", l, p))
                P.stt(aT[:, p, 0:ntok], avb[:, 0:ntok], V("ffn_dw_b", l, 24 + p), agb[:, 0:ntok], ALU.add, ALU.mult)
                if p + 3 < 24:
                    pend.append(load_pair(p + 3))
            pieces = []
            r0 = 0
            while r0 < ntok:
                pn = min(512, ntok - r0)
                pieces.append((r0, pn))
                r0 += pn

            def load_d(c):
                wd = wdr().rearrange("p (k n) -> p k n", n=128)
                wload(wd[:, 0:12, :], dnv[:, 0:12, c * 128:(c + 1) * 128])
                wload(wd[:, 12:24, :], dnv[:, 12:24, c * 128:(c + 1) * 128])
                return wd
            pendd = [load_d(0), load_d(1)]
            ssq = []
            for c in range(KC):
                wd = pendd.pop(0)
                for pi, (r0, pn) in enumerate(pieces):
                    ps = banks[(c * 2 + pi) % 6][:, 0:pn]
                    for k in range(24):
                        P.mm(ps, wd[:, k, :], aT[:, k, r0:r0 + pn], start=(k == 0), stop=(k == 23))
                    P.act(obf[:, c, r0:r0 + pn], ps, AF.Copy)
                    sq = sqr()[:, 0:pn]
                    P.act(sq, ps, AF.Square)
                    if ssq:
                        ssq.pop(0)()
                    ssq.append(lambda sq=sq, pi=pi, pn=pn, c=c: P.mm(banks[6 + pi][:, 0:pn], ones_bf[:], sq,
                                                                   start=(c == 0), stop=(c == KC - 1)))
                if c + 2 < KC:
                    pendd.append(load_d(c + 2))
            while ssq:
                ssq.pop(0)()
            for pi, (r0, pn) in enumerate(pieces):
                P.act(stdb[:, r0:r0 + pn], banks[6 + pi][:, 0:pn], AF.Sqrt, bias=EPS, scale=1.0 / D)
            P.recip(stdb[:, 0:ntok], stdb[:, 0:ntok])
            for c in range(KC):
                t = tb()
                P.tt(t[:, 0:ntok], obf[:, c, 0:ntok], stdb[:, 0:ntok], ALU.mult)
                for (g0, g1, s, hl, hr, uc, bc) in lay:
                    n = g1 - g0
                    P.stt(xT[:, c, g0:g1], t[:, bc:bc + n], G2[l][:, c, s:s + 1], xT[:, c, g0:g1], ALU.mult, ALU.add)
        AR.release(m)

    finals = []

    def on(name):
        return stages is None or name in stages
    for l in range(nlayers):
        need_ctx = l < L - 1
        if on("N1"):
            norm_stage(l, A1[l], 0, ALL_TILES)
        if l == 0 and dbg.get("dump_h1"):
            hd = nc.dram_tensor("hT_dump", [128, KC, NT], BF16, kind="ExternalOutput").ap()
            for c in range(KC):
                finals.append(P.dma("sync", hd[:, c, :], hT[:, c, :]))
            md = nc.dram_tensor("modv_dump", [128, 96], F32, kind="ExternalOutput").ap()
            finals.append(P.dma("sync", md, modv[0][:].rearrange("p a b -> p (a b)")))
        if on("R"):
            rnn_stage(l, need_ctx)
        if on("C"):
            conv_stage(l, need_ctx)
        if on("Q"):
            attn_stage(l, need_ctx)
        if on("M"):
            merge_stage(l, need_ctx)
        if l == 0 and dbg.get("dump_x1"):
            xd = nc.dram_tensor("x1_dump", [128, KC, NT], F32, kind="ExternalOutput").ap()
            for c in range(KC):
                finals.append(P.dma("sync", xd[:, c, :], xT[:, c, :]))
        if on("N2"):
            norm_stage(l, A2[l], 24, ALL_TILES if need_ctx else LAT_TILES)
        if on("F"):
            ffn_stage(l, need_ctx)

    for c in range(KC):
        finals.append(P.dma("sync", yT_d[:, c, :], xT[:, c, 0:NL]))
    P.emit(final_waits=finals)
    st.close()
    return nc, P


def rope_tables():
    rows = NL // 64
    row = np.repeat(np.arange(rows), 64).astype(np.float32)
    colv = np.tile(np.arange(64), rows).astype(np.float32)
    n_freq = 16
    freq = (np.float32(10000.0) ** (-np.arange(n_freq, dtype=np.float32) / np.float32(n_freq))).astype(np.float32)
    ang = np.concatenate([row[:, None] * freq, colv[:, None] * freq], axis=-1).astype(np.float32)
    cos = np.cos(ang).astype(np.float32).T
    sin = np.sin(ang).astype(np.float32).T
    C = np.concatenate([cos, cos, cos, cos], axis=0)
    S = np.concatenate([sin, sin, sin, sin], axis=0)
    rot = np.zeros((128, 128), np.float32)
    for hb in (0, 64):
        for mI in range(32):
            rot[hb + mI + 32, hb + mI] = -1.0
            rot[hb + mI, hb + mI + 32] = 1.0
    return np.ascontiguousarray(np.stack([C, S], 0)), np.ascontiguousarray(np.stack([rot, np.eye(128, dtype=np.float32)], 0))


_CACHE = {}


def kernel(**inputs):
    if "nc" not in _CACHE:
        _CACHE["nc"] = build_program()
    nc, _ = _CACHE["nc"]
    f = lambda k: np.ascontiguousarray(np.asarray(inputs[k], np.float32))
    x, ctx, c, c_ctx = f("x"), f("ctx"), f("c"), f("c_ctx")
    vecs = pack_vecs(inputs)
    rope, rot = rope_tables()
    shared = {"vecs": vecs, "rope": rope, "rotm": rot}
    for k in ("w_mod", "w_in", "w_attn_out", "w_conv_out", "w_rnn_out", "w_out", "ffn_up", "ffn_down", "rnn_wa", "rnn_wx"):
        shared[k] = f(k)
    in_maps = []
    B = x.shape[0]
    for b in range(B):
        xa = np.concatenate([x[b], ctx[b]], axis=0)
        xTb = np.ascontiguousarray(xa.T.reshape(KC, 128, NT).transpose(1, 0, 2))
        cc = np.stack([c[b], c_ctx], axis=-1)
        cTb = np.ascontiguousarray(cc.reshape(KC, 128, 2).transpose(1, 0, 2).reshape(128, KC * 2))
        d = dict(shared)
        d["xT"] = xTb
        d["cT"] = cTb
        in_maps.append(d)
    res = run_bass_kernel_spmd(nc, in_maps, core_ids=list(range(B)))
    out = np.empty((B, NL, D), np.float32)
    for b in range(B):
        yT = np.asarray(res.results[b]["yT"])
        out[b] = yT.transpose(2, 1, 0).reshape(NL, D)
    return out
```
